# Optimizing a Trainium2 kernel written in Bass

```python
import math
import jax, jax.numpy as jnp
from jax import lax
import numpy as np

D_MODEL = 1024
BATCH = 4
SEQ = 8192
DEPTH = 2

HEAD_DIM = 64
GRID_W = 64
BLOCK = 128
NA_HEADS = 8
NA_ROW_K = 8
NA_COL_K = 16
SW_Q_HEADS = 8
SW_KV_HEADS = 2
SW_WINDOW = 128
DIFF_HEADS = 4
N_BRANCH = 3
BRANCH_W = 512
D_FF = 2816
ROPE_THETA = 10000.0
EPS = 1e-6
NEG = -1e30

A_W = NA_HEADS * HEAD_DIM
B_QW = SW_Q_HEADS * HEAD_DIM
B_KVW = SW_KV_HEADS * HEAD_DIM
C_QKW = 2 * DIFF_HEADS * HEAD_DIM
C_VW = DIFF_HEADS * 2 * HEAD_DIM
SPLIT_SIZES = (A_W, A_W, A_W, B_QW, B_KVW, B_KVW, C_QKW, C_QKW, C_VW, N_BRANCH * D_MODEL)
IN_COLS = sum(SPLIT_SIZES)

kernel_name = "hybrid_natten_swa_diffattn_macaron_encoder"


def rmsnorm(x, g):
    xf = x.astype(jnp.float32)
    y = xf * lax.rsqrt(jnp.mean(xf * xf, axis=-1, keepdims=True) + EPS)
    return (y * g.astype(jnp.float32)).astype(x.dtype)


def swiglu(x, w_gu, w_down):
    gate, up = jnp.split(x @ w_gu, 2, axis=-1)
    return (jax.nn.silu(gate) * up) @ w_down


def rope_tables(s):
    pos = jnp.arange(s, dtype=jnp.float32)
    inv = ROPE_THETA ** (-jnp.arange(0, HEAD_DIM, 2, dtype=jnp.float32) / HEAD_DIM)
    ang = pos[:, None] * inv[None, :]
    ang = jnp.concatenate([ang, ang], axis=-1)
    return jnp.cos(ang), jnp.sin(ang)


def apply_rope(t, cos, sin):
    t1, t2 = jnp.split(t, 2, axis=-1)
    rot = jnp.concatenate([-t2, t1], axis=-1)
    return t * cos.astype(t.dtype) + rot * sin.astype(t.dtype)


def split_heads(t, n):
    b, s, _ = t.shape
    return t.reshape(b, s, n, HEAD_DIM).transpose(0, 2, 1, 3)


def neighbourhood_attention(q, k, v, rpb):
    b, h, s, d = q.shape
    rows = s // GRID_W
    kr = min(NA_ROW_K, rows)
    kc = NA_COL_K
    r = jnp.arange(rows)
    c = jnp.arange(GRID_W)
    row_start = jnp.clip(r - kr // 2, 0, rows - kr)
    col_start = jnp.clip(c - kc // 2, 0, GRID_W - kc)
    key_rows = row_start[:, None] + jnp.arange(kr)[None, :]
    qg = q.reshape(b, h, rows, GRID_W, d)
    kg = k.reshape(b, h, rows, GRID_W, d)[:, :, key_rows]
    vg = v.reshape(b, h, rows, GRID_W, d)[:, :, key_rows]
    scores = jnp.einsum('bhrcd,bhrijd->bhrcij', qg, kg).astype(jnp.float32) * (d ** -0.5)
    dr = key_rows - r[:, None] + (NA_ROW_K - 1)
    dc = jnp.clip(c[None, :] - c[:, None] + (NA_COL_K - 1), 0, 2 * NA_COL_K - 2)
    bias = rpb[:, dr[:, None, :, None], dc[None, :, None, :]]
    col_ok = (c[None, :] >= col_start[:, None]) & (c[None, :] < col_start[:, None] + kc)
    scores = scores + bias[None].astype(jnp.float32)
    scores = jnp.where(col_ok[None, None, None, :, None, :], scores, NEG)
    p = jax.nn.softmax(scores.reshape(b, h, rows, GRID_W, kr * GRID_W), axis=-1)
    p = p.reshape(b, h, rows, GRID_W, kr, GRID_W).astype(v.dtype)
    out = jnp.einsum('bhrcij,bhrijd->bhrcd', p, vg)
    return out.reshape(b, h, s, d)


def sliding_window_attention(q, k, v, sink):
    b, hq, s, d = q.shape
    hkv = k.shape[1]
    g = hq // hkv
    nb = s // BLOCK
    qb = q.reshape(b, hkv, g, nb, BLOCK, d)

    def band(t):
        tp = jnp.pad(t, ((0, 0), (0, 0), (BLOCK, BLOCK), (0, 0))).reshape(b, hkv, nb + 2, BLOCK, d)
        return jnp.concatenate([tp[:, :, :-2], tp[:, :, 1:-1], tp[:, :, 2:]], axis=3)

    kb, vb = band(k), band(v)
    scores = jnp.einsum('bkgnqd,bknjd->bkgnqj', qb, kb).astype(jnp.float32) * (d ** -0.5)
    qpos = jnp.arange(nb)[:, None] * BLOCK + jnp.arange(BLOCK)[None, :]
    kpos = (jnp.arange(nb)[:, None] - 1) * BLOCK + jnp.arange(3 * BLOCK)[None, :]
    ok = ((jnp.abs(qpos[:, :, None] - kpos[:, None, :]) <= SW_WINDOW)
          & (kpos[:, None, :] >= 0) & (kpos[:, None, :] < s))
    scores = jnp.where(ok, scores, NEG)
    sink_logit = jnp.broadcast_to(sink.astype(jnp.float32).reshape(1, hkv, g, 1, 1, 1),
                                  scores.shape[:-1] + (1,))
    p = jax.nn.softmax(jnp.concatenate([scores, sink_logit], axis=-1), axis=-1)[..., :-1]
    out = jnp.einsum('bkgnqj,bknjd->bkgnqd', p.astype(v.dtype), vb)
    return out.reshape(b, hq, s, d)


def differential_attention(q, k, v, lam, lam_init, subln_g):
    b, h, _, s, d = q.shape
    nb = s // BLOCK
    qb = jnp.moveaxis(q.reshape(b, h, 2, nb, BLOCK, d), 3, 0)

    def one_block(qblk):
        sc = jnp.einsum('bhtqd,bhtkd->bhtqk', qblk, k).astype(jnp.float32) * (d ** -0.5)
        p = jax.nn.softmax(sc, axis=-1)
        a = p[:, :, 0] - lam * p[:, :, 1]
        return jnp.einsum('bhqk,bhke->bhqe', a.astype(v.dtype), v)

    o = lax.map(one_block, qb)
    o = jnp.moveaxis(o, 0, 2).reshape(b, h, s, 2 * d)
    return rmsnorm(o, subln_g) * (1.0 - lam_init)


def setup_inputs(seed: int = 0) -> dict:
    key = jax.random.key(seed)
    ks = jax.random.split(key, 24)
    L, D, F = DEPTH, D_MODEL, D_FF

    def nrm(k, shape, scale):
        return jax.random.normal(k, shape, jnp.float32) * scale

    def gain(k, n):
        return 1.0 + nrm(k, (L, n), 0.05)

    return {
        "x": nrm(ks[0], (BATCH, SEQ, D), 1.0),
        "ffn1_pre_g": gain(ks[1], D),
        "ffn1_w_gu": nrm(ks[2], (L, D, 2 * F), D ** -0.5),
        "ffn1_w_down": nrm(ks[3], (L, F, D), F ** -0.5),
        "ffn1_post_g": gain(ks[4], D),
        "mix_pre_g": gain(ks[5], D),
        "w_in": nrm(ks[6], (L, D, IN_COLS), D ** -0.5),
        "na_rpb": nrm(ks[7], (L, NA_HEADS, 2 * NA_ROW_K - 1, 2 * NA_COL_K - 1), 0.5),
        "sw_sink": nrm(ks[8], (L, SW_Q_HEADS), 1.0),
        "diff_lambda_q1": nrm(ks[9], (L, HEAD_DIM), 0.1),
        "diff_lambda_k1": nrm(ks[10], (L, HEAD_DIM), 0.1),
        "diff_lambda_q2": nrm(ks[11], (L, HEAD_DIM), 0.1),
        "diff_lambda_k2": nrm(ks[12], (L, HEAD_DIM), 0.1),
        "diff_subln_g": gain(ks[13], 2 * HEAD_DIM),
        "w_branch": nrm(ks[14], (L, N_BRANCH, BRANCH_W, D), BRANCH_W ** -0.5),
        "b_gate": nrm(ks[15], (L, N_BRANCH, D), 0.1),
        "w_out": nrm(ks[16], (L, D, D), D ** -0.5),
        "mix_post_g": gain(ks[17], D),
        "ffn2_pre_g": gain(ks[18], D),
        "ffn2_w_gu": nrm(ks[19], (L, D, 2 * F), D ** -0.5),
        "ffn2_w_down": nrm(ks[20], (L, F, D), F ** -0.5),
        "ffn2_post_g": gain(ks[21], D),
    }


def reference(x, ffn1_pre_g, ffn1_w_gu, ffn1_w_down, ffn1_post_g, mix_pre_g, w_in, na_rpb,
              sw_sink, diff_lambda_q1, diff_lambda_k1, diff_lambda_q2, diff_lambda_k2,
              diff_subln_g, w_branch, b_gate, w_out, mix_post_g, ffn2_pre_g, ffn2_w_gu,
              ffn2_w_down, ffn2_post_g):
    b, s, _ = x.shape
    cos, sin = rope_tables(s)
    split_at = np.cumsum(SPLIT_SIZES)[:-1].tolist()
    h = x
    for l in range(DEPTH):
        y = swiglu(rmsnorm(h, ffn1_pre_g[l]), ffn1_w_gu[l], ffn1_w_down[l])
        h = h + 0.5 * rmsnorm(y, ffn1_post_g[l])

        u = rmsnorm(h, mix_pre_g[l])
        proj = u @ w_in[l]
        qa, ka, va, qb, kb, vb, qc, kc, vc, gl = jnp.split(proj, split_at, axis=-1)

        oa = neighbourhood_attention(split_heads(qa, NA_HEADS), split_heads(ka, NA_HEADS),
                                     split_heads(va, NA_HEADS), na_rpb[l])
        oa = oa.transpose(0, 2, 1, 3).reshape(b, s, A_W)

        ob = sliding_window_attention(apply_rope(split_heads(qb, SW_Q_HEADS), cos, sin),
                                      apply_rope(split_heads(kb, SW_KV_HEADS), cos, sin),
                                      split_heads(vb, SW_KV_HEADS), sw_sink[l])
        ob = ob.transpose(0, 2, 1, 3).reshape(b, s, B_QW)

        qc = apply_rope(qc.reshape(b, s, DIFF_HEADS, 2, HEAD_DIM).transpose(0, 2, 3, 1, 4), cos, sin)
        kc = apply_rope(kc.reshape(b, s, DIFF_HEADS, 2, HEAD_DIM).transpose(0, 2, 3, 1, 4), cos, sin)
        vc = vc.reshape(b, s, DIFF_HEADS, 2 * HEAD_DIM).transpose(0, 2, 1, 3)
        lam_init = 0.8 - 0.6 * math.exp(-0.3 * l)
        lam = (jnp.exp(jnp.sum(diff_lambda_q1[l].astype(jnp.float32) * diff_lambda_k1[l].astype(jnp.float32)))
               - jnp.exp(jnp.sum(diff_lambda_q2[l].astype(jnp.float32) * diff_lambda_k2[l].astype(jnp.float32)))
               + lam_init)
        oc = differential_attention(qc, kc, vc, lam, lam_init, diff_subln_g[l])
        oc = oc.transpose(0, 2, 1, 3).reshape(b, s, C_VW)

        branches = jnp.stack([oa, ob, oc], axis=2)
        yb = jnp.einsum('bsnc,ncd->bsnd', branches, w_branch[l])
        gates = jax.nn.sigmoid(gl.reshape(b, s, N_BRANCH, D_MODEL) + b_gate[l])
        merged = jnp.sum(gates * yb, axis=2)
        h = h + rmsnorm(merged @ w_out[l], mix_post_g[l])

        y = swiglu(rmsnorm(h, ffn2_pre_g[l]), ffn2_w_gu[l], ffn2_w_down[l])
        h = h + 0.5 * rmsnorm(y, ffn2_post_g[l])
    return h
```

```python
import contextlib
import numpy as np
import concourse.bass as bass
import concourse.mybir as mybir
from concourse.bass_utils import run_bass_kernel_spmd

F32 = mybir.dt.float32
BF16 = mybir.dt.bfloat16
AF = mybir.ActivationFunctionType
ALU = mybir.AluOpType

D = 1024
FF = 2816
S = 8192
NT = 4096
TB = 512
NBLK = NT // TB
NRANK = 2


def set_mode(nrank):
    global NT, NBLK, NRANK
    NRANK = nrank
    NT = S // nrank
    NBLK = NT // TB
L = 2
EPS = 1e-6
KV_ROWS = 2304
KA0, KB0, KC0, VA0, VB0, VC0 = 0, 512, 640, 1152, 1664, 1792

N_DMA_SEMS = 16
SAME_ENGINE_SYNC = True


class Op:
    __slots__ = ("eng", "emit", "deps", "is_dma", "marked", "semval", "dsem", "dval")

    def __init__(self, eng, emit, is_dma):
        self.eng = eng
        self.emit = emit
        self.deps = []
        self.is_dma = is_dma
        self.marked = False
        self.semval = 0
        self.dsem = None
        self.dval = 0


class Prog:
    ENGS = ("pe", "act", "dve", "pool", "sp")

    def __init__(self, nc):
        self.nc = nc
        self.ops = []
        self.last_writer = {}
        self.readers = {}

    def op(self, eng, emit, reads=(), writes=(), dma=False):
        o = Op(eng, emit, dma)
        deps = []
        reads = _expand(reads)
        writes = _expand(writes)
        for b in reads:
            w = self.last_writer.get(b)
            if w is not None:
                deps.append(w)
        for b in writes:
            w = self.last_writer.get(b)
            if w is not None:
                deps.append(w)
            rs = self.readers.get(b)
            if rs:
                deps.extend(rs)
        for b in reads:
            self.readers.setdefault(b, []).append(o)
        for b in writes:
            self.last_writer[b] = o
            self.readers[b] = []
        seen = set()
        for d in deps:
            if id(d) not in seen and d is not o:
                seen.add(id(d))
                o.deps.append(d)
        self.ops.append(o)
        return o

    def finalize(self, stack):
        nc = self.nc
        per_eng = {e: [] for e in self.ENGS}
        for o in self.ops:
            per_eng[o.eng].append(o)
        for o in self.ops:
            for d in o.deps:
                if d.is_dma:
                    continue
                if d.eng == o.eng and (o.eng == "pe" or not SAME_ENGINE_SYNC) and not o.is_dma:
                    continue
                d.marked = True
        self.csem = {e: stack.enter_context(nc.semaphore("c_" + e)) for e in ("pe", "act", "dve", "pool")}
        dsems = {}
        for e in self.ENGS:
            ndma = sum(1 for o in per_eng[e] if o.is_dma)
            if ndma:
                dsems[e] = [stack.enter_context(nc.semaphore("d_%s_%d" % (e, i))) for i in range(min(ndma, N_DMA_SEMS))]
        for e in self.ENGS:
            cnt = 0
            j = 0
            for o in per_eng[e]:
                if o.is_dma:
                    o.dsem = dsems[e][j % N_DMA_SEMS]
                    o.dval = 16 * (j // N_DMA_SEMS + 1)
                    j += 1
                elif o.marked:
                    cnt += 1
                    o.semval = cnt
        self.per_eng = per_eng

    def emit_engine(self, ename, eng):
        waited = {}

        def wait(sem, val):
            key = id(sem)
            if waited.get(key, 0) >= val:
                return
            waited[key] = val
            eng.wait_ge(sem, val)

        for o in self.per_eng[ename]:
            for d in o.deps:
                if d.is_dma:
                    wait(d.dsem, d.dval)
                else:
                    if not d.marked:
                        continue
                    if d.eng == ename and not o.is_dma and (ename == "pe" or not SAME_ENGINE_SYNC):
                        continue
                    wait(self.csem[d.eng], d.semval)
            if o.is_dma and o.dval > 16:
                wait(o.dsem, o.dval - 16)
            if o.emit is None:
                continue
            ins = o.emit(eng)
            if o.is_dma:
                ins.then_inc(o.dsem, 16)
            elif o.marked:
                ins.then_inc(self.csem[ename], 1)

    def run_block(self):
        nc = self.nc
        with nc.Block() as block:
            @block.tensor
            def _(e):
                self.emit_engine("pe", e)

            @block.scalar
            def _(e):
                self.emit_engine("act", e)

            @block.vector
            def _(e):
                self.emit_engine("dve", e)

            @block.gpsimd
            def _(e):
                self.emit_engine("pool", e)

            @block.sync
            def _(e):
                self.emit_engine("sp", e)


class KS(list):
    pass


def _expand(keys):
    out = []
    for k in keys:
        if isinstance(k, KS):
            out.extend(k)
        else:
            out.append(k)
    return out


class AliasRot:
    def __init__(self, views, keysets):
        self.bufs, self.keys, self.i = views, keysets, 0

    def next(self):
        b, k = self.bufs[self.i], self.keys[self.i]
        self.i = (self.i + 1) % len(self.bufs)
        return b, k


class Rot:
    def __init__(self, K, name, n, shape, dt):
        self.bufs = [K.T("%s%d" % (name, i), shape, dt) for i in range(n)]
        self.keys = ["%s%d" % (name, i) for i in range(n)]
        self.i = 0

    def next(self):
        b, k = self.bufs[self.i], self.keys[self.i]
        self.i = (self.i + 1) % len(self.bufs)
        return b, k


class WSpec:
    def __init__(self, name, src2d, K, N, w, gk, swap=False):
        self.name, self.src, self.K, self.N, self.w, self.gk, self.swap = name, src2d, K, N, w, gk, swap
        self.ncb = N // w
        self.nkg = K // (128 * gk)
        assert self.ncb * w == N and self.nkg * 128 * gk == K


class KB:
    def __init__(self, cfg):
        self.cfg = cfg
        self.nc = bass.Bass("TRN2", target_bir_lowering=False)
        self.st = contextlib.ExitStack()
        self.P = Prog(self.nc)
        self.uid = 0
        self.bank_i = {}
        self.outs = []

    def T(self, name, shape, dt=F32):
        return self.st.enter_context(self.nc.sbuf_tensor(name, shape, dt))

    def dram(self, name, shape, dt, kind="Internal"):
        return self.nc.dram_tensor(name, shape, dt, kind=kind).ap()

    def key(self, p):
        self.uid += 1
        return "%s#%d" % (p, self.uid)

    def dma(self, out, in_, reads, writes, eng="sp"):
        return self.P.op(eng, lambda e: e.dma_start(out=out, in_=in_), reads, writes, dma=True)

    def mm(self, out, lhsT, rhs, start, stop, reads, writes, **kw):
        return self.P.op("pe", lambda e: e.matmul(out, lhsT=lhsT, rhs=rhs, start=start, stop=stop, **kw), reads, writes)

    def act(self, out, in_, func, reads, writes, **kw):
        return self.P.op("act", lambda e: e.activation(out=out, in_=in_, func=func, **kw), reads, writes)

    def tt(self, eng, out, in0, in1, op, reads, writes):
        return self.P.op(eng, lambda e: e.tensor_tensor(out=out, in0=in0, in1=in1, op=op), reads, writes)

    def ts(self, eng, out, in0, s1, s2, op0, op1, reads, writes):
        if op1 is None:
            return self.P.op(eng, lambda e: e.tensor_scalar(out=out, in0=in0, scalar1=s1, scalar2=None, op0=op0), reads, writes)
        return self.P.op(eng, lambda e: e.tensor_scalar(out=out, in0=in0, scalar1=s1, scalar2=s2, op0=op0, op1=op1), reads, writes)

    def stt(self, eng, out, in0, scalar, in1, op0, op1, reads, writes):
        return self.P.op(eng, lambda e: e.scalar_tensor_tensor(out=out, in0=in0, scalar=scalar, in1=in1, op0=op0, op1=op1), reads, writes)

    def copy(self, eng, out, in_, reads, writes):
        if eng == "act":
            return self.act(out, in_, AF.Copy, reads, writes)
        return self.P.op(eng, lambda e: e.tensor_copy(out=out, in_=in_), reads, writes)

    def recip(self, out, in_, reads, writes):
        return self.P.op("dve", lambda e: e.reciprocal(out=out, in_=in_), reads, writes)

    def bank(self, role, choices):
        i = self.bank_i.get(role, 0)
        self.bank_i[role] = i + 1
        b = choices[i % len(choices)]
        return self.ps[b], "ps%d" % b

    def build(self):
        nc, cfg = self.nc, self.cfg
        IN = lambda n, s, dt=F32: self.dram(n, s, dt, "ExternalInput")
        if cfg.get("launch", 0) in (0, 1):
            self.xT = IN("xT", [D, NT])
        self.w_gu = [IN("ffn1_w_gu", [L, D, 2 * FF]), IN("ffn2_w_gu", [L, D, 2 * FF])]
        self.w_dn = [IN("ffn1_w_down", [L, FF, D]), IN("ffn2_w_down", [L, FF, D])]
        self.w_in = IN("w_in", [L, D, 6912])
        self.w_br = IN("w_branch", [L, 3, 512, D])
        self.w_o = IN("w_out", [L, D, D])
        self.gcols_d = IN("gcols", [128, 9 * L * 8])
        self.bgate_d = IN("bgate", [128, L * 3 * 8])
        self.gsub_d = IN("gsub", [128, L])
        self.sink_d = IN("sink", [L * 8])
        self.lamv_d = IN("lamv", [4 * L * 64])
        self.rpbT_d = IN("rpbT", [L, 8, 64, 15 * 64])
        self.cmask_d = IN("cmask", [64, 64])
        self.cosT_d = IN("cosT", [128, NT])
        self.sinT_d = IN("sinT", [128, NT])
        self.strip_d = IN("strip", [128, 1152])
        self.sel_d = IN("sel", [128, 4])
        self.selb_d = IN("selb", [2, 256])
        self.flags_d = IN("flags", [128, 2])
        launch = cfg.get("launch", 0)
        self.launch = launch
        OUT = lambda n, s, dt=F32: self.dram(n, s, dt, "ExternalOutput")
        self.kvloc = [None] * L
        self.kvall = [None] * L
        if launch == 0:
            self.outT = OUT("outT", [D, NT])
            self.hbuf = self.dram("hbuf", [D, NT], F32)
            self.kvloc = [self.dram("kvloc%d" % l, [KV_ROWS, NT], BF16) for l in range(L)]
            if NRANK == 2:
                self.kvall = [self.dram("kvall%d" % l, [2 * KV_ROWS, NT], BF16) for l in range(L)]
        else:
            self.lB = {2: 0, 3: 1}.get(launch)
            self.lA = {1: 0, 2: 1}.get(launch)
            if cfg.get("skipA"):
                self.lA = None
            if launch > 1:
                self.h_in = IN("h_in", [D, NT])
            self.h_out = OUT("outT" if launch == 3 else "h_out", [D, NT])
            if self.lA is not None:
                self.kvloc[self.lA] = OUT("kvloc_out", [KV_ROWS, NT], BF16)
            if self.lB is not None:
                self.kvall[self.lB] = IN("kvall_in", [2 * KV_ROWS, NT], BF16)
                self.kvloc[self.lB] = IN("kvloc_in", [KV_ROWS, NT], BF16)
        self.ebt = [self.dram("ebt%d" % l, [8, 64, 15 * 64], BF16) for l in range(L)]
        psA = self.st.enter_context(nc.psum_tensor("psA", [128, 2048], F32))
        self.ps = [psA[:, i * 512:(i + 1) * 512] for i in range(4)]
        self.ps += [self.st.enter_context(nc.psum_tensor("ps%d" % i, [128, 512], F32))[:] for i in range(4, 8)]
        self.ps2 = [(psA[:, 0:1024], KS(["ps0", "ps1"])), (psA[:, 1024:2048], KS(["ps2", "ps3"]))]
        self.ps2_i = 0
        self.ones_f = self.T("ones_f", [128, 128], F32)
        self.ones_b = self.T("ones_b", [128, 128], BF16)
        self.gcols = self.T("gcols_s", [128, 9 * L * 8], F32)
        self.nbg = self.T("nbg", [128, L * 3 * 8], F32)
        self.gsub = self.T("gsub_s", [128, L], F32)
        self.esink = self.T("esink", [128, L * 8], F32)
        self.lamt = self.T("lamt", [128, 8 * L], F32)
        self.lamc = self.T("lamc", [128, L], F32)
        self.nlam = self.T("nlam", [128, L], F32)
        self.dacc = self.T("dacc", [128, 2 * TB], F32)
        self.e2 = Rot(self, "e2b", 3, [128, 2 * TB], BF16)
        self.strip = self.T("strip_s", [128, 1152], BF16)
        self.sel = self.T("sel_s", [128, 4], BF16)
        self.selb = self.T("selb_s", [2, 256], F32)
        self.cmask = self.T("cmask_s", [64, 64], F32)
        self.eps_t = self.T("eps_t", [128, 1], F32)
        self.flags = self.T("flags_s", [128, 2], F32)
        self.hb = self.T("hb", [128, 8, TB], F32)
        self.xnT = self.T("xnT", [128, 8, TB], BF16)
        self.hT = self.T("hT", [128, 22, TB], BF16)
        self.ysb = self.T("ysb", [128, 8, TB], F32)
        self.cosT = self.T("cosT_s", [128, TB], F32)
        self.sinT = self.T("sinT_s", [128, TB], F32)
        self.wslot = Rot(self, "wslot", 4, [128, 2048], BF16)
        self.f32s = Rot(self, "f32s", 3, [128, TB], F32)
        self.bfs = Rot(self, "bfs", 3, [128, TB], BF16)
        self.rstd = Rot(self, "rstd", 2, [128, TB], F32)
        self.dfb = Rot(self, "dfb", 4, [128, TB], F32)
        self.sqb = Rot(self, "sqb", 2, [128, TB], BF16)
        self.pm = Rot(self, "pm", 3, [128, TB], BF16)
        self.kst = Rot(self, "kst", 2, [128, TB], BF16)
        self.vst = Rot(self, "vst", 1, [128, 4, 512], BF16)
        self.cin = AliasRot([self.ysb[:, 0:4, :].rearrange("p a b -> p (a b)"), self.ysb[:, 4:8, :].rearrange("p a b -> p (a b)")],
                            [KS([("ysb", c) for c in range(0, 4)]), KS([("ysb", c) for c in range(4, 8)])])
        self.cout = AliasRot([self.hT[:, 0:4, :].rearrange("p a b -> p (a b)"), self.hT[:, 4:8, :].rearrange("p a b -> p (a b)")],
                             [KS([("hT", c) for c in range(0, 4)]), KS([("hT", c) for c in range(4, 8)])])
        self.qT = self.T("qT", [128, 4, TB], BF16)
        self.oT = [self.T("oT%d" % i, [128, 4, TB], BF16) for i in range(3)]
        self.mT = self.T("mT", [128, 8, TB], BF16)
        self.macc = self.T("macc", [128, 2, TB], F32)
        self.kTA = self.T("kTA", [128, 4, 1024], BF16)
        self.vA = self.T("vA", [64, 16, 512], BF16)
        self.ebh = Rot(self, "ebh", 2, [64, 15 * 64], BF16)
        self.kTB = self.T("kTB", [128, 2, 768], BF16)
        self.vB = self.T("vB", [128, 6, 128], BF16)
        self.kTC = Rot(self, "kTC", 2, [128, 1024], BF16)
        self.vC = Rot(self, "vC", 2, [128, 8, 128], BF16)

        self.kvkeys = {i: [] for i in range(L)}
        self.setup_consts()
        nl = cfg.get("layers", L)
        nblk = cfg.get("nblk", NBLK)
        stop_after = cfg.get("stop_after", None)
        if launch == 0:
            self.setup_weights(list(range(nl)), list(range(nl)))
            for blk in range(nblk):
                self.load_h(blk, self.xT)
                self.stageA(0, blk)
                self.store_h(blk, self.hbuf)
            for l in range(nl):
                self.allgather(l)
                if stop_after == ("A", l):
                    break
                for blk in range(nblk):
                    self.load_h(blk, self.hbuf)
                    self.stageB(l, blk)
                    last = (l == nl - 1)
                    if not last:
                        self.stageA(l + 1, blk)
                    self.store_h(blk, self.outT if (last and not cfg.get("debug")) else self.hbuf)
        else:
            self.setup_weights([] if self.lB is None else [self.lB], [] if self.lA is None else [self.lA])
            for blk in range(nblk):
                self.load_h(blk, self.xT if launch == 1 else self.h_in)
                if self.lB is not None:
                    self.stageB(self.lB, blk)
                if self.lA is not None:
                    self.stageA(self.lA, blk)
                self.store_h(blk, self.h_out)
        if cfg.get("debug") and launch == 0:
            dh = self.dram("dbg_h", [D, NT], F32, "ExternalOutput")
            self.dma(dh, self.hbuf, [("hbuf", b) for b in range(nblk)], ["dbg_h"], eng="pool")
            self.outs.append("dbg_h")
            dk = self.dram("dbg_kv", [2 * KV_ROWS, NT], BF16, "ExternalOutput")
            self.dma(dk, self.kvall[0], [("kvall", 0)], ["dbg_kv"], eng="pool")
            self.outs.append("dbg_kv")
        self.finish()
        return nc

    def setup_consts(self):
        P = self.P
        P.op("pool", lambda e: e.memset(self.ones_f[:], 1.0), writes=["ones_f"])
        P.op("pool", lambda e: e.memset(self.ones_b[:], 1.0), writes=["ones_b"])
        P.op("pool", lambda e: e.memset(self.eps_t[:], EPS), writes=["eps_t"])
        self.dma(self.flags[:], self.flags_d, [], ["flags"])
        self.dma(self.gcols[:], self.gcols_d, [], ["gcols"])
        for l in range(L):
            for which in (1, 8):
                c0 = (l * 9 + which) * 8
                self.ts("dve", self.gcols[:, c0:c0 + 8], self.gcols[:, c0:c0 + 8], 0.5, None, ALU.mult, None, ["gcols"], ["gcols"])
        self.dma(self.nbg[:], self.bgate_d, [], ["nbg"])
        self.ts("dve", self.nbg[:], self.nbg[:], -1.0, None, ALU.mult, None, ["nbg"], ["nbg"])
        self.dma(self.gsub[:], self.gsub_d, [], ["gsub"])
        for l in range(L):
            lam_init = 0.8 - 0.6 * float(np.exp(-0.3 * l))
            self.ts("dve", self.gsub[:, l:l + 1], self.gsub[:, l:l + 1], 1.0 - lam_init, None, ALU.mult, None, ["gsub"], ["gsub"])
        self.dma(self.esink[:], self.sink_d.partition_broadcast(128), [], ["esink"])
        self.act(self.esink[:], self.esink[:], AF.Exp, ["esink"], ["esink"])
        lrt, lrk = self.cin.next()
        self.dma(lrt[:, 0:4 * L * 64], self.lamv_d.partition_broadcast(128), [], [lrk])
        lr = lrt
        n = L * 64
        for l in range(L):
            for pair in range(2):
                q = lr[:, (2 * pair) * n + l * 64:(2 * pair) * n + (l + 1) * 64]
                k = lr[:, (2 * pair + 1) * n + l * 64:(2 * pair + 1) * n + (l + 1) * 64]
                col = l * 4 + pair
                tmp, tk = self.f32s.next()
                self.tt("dve", tmp[:, 0:64], q, k, ALU.mult, [lrk], [tk])
                self.P.op("act", (lambda e, tmp=tmp, col=col: e.activation(out=tmp[:, 64:128], in_=tmp[:, 0:64], func=AF.Copy, accum_out=self.lamt[:, col:col + 1])), [tk], [tk, "lamt"])
            self.act(self.lamt[:, l * 4:l * 4 + 2], self.lamt[:, l * 4:l * 4 + 2], AF.Exp, ["lamt"], ["lamt"])
            lam_init = 0.8 - 0.6 * float(np.exp(-0.3 * l))
            self.tt("dve", self.lamt[:, l * 4 + 2:l * 4 + 3], self.lamt[:, l * 4:l * 4 + 1], self.lamt[:, l * 4 + 1:l * 4 + 2], ALU.subtract, ["lamt"], ["lamt"])
            self.ts("dve", self.lamc[:, l:l + 1], self.lamt[:, l * 4 + 2:l * 4 + 3], -1.0, -lam_init, ALU.mult, ALU.add, ["lamt"], ["lamc"])
            self.ts("dve", self.nlam[:, l:l + 1], self.lamt[:, l * 4 + 2:l * 4 + 3], -1.0, -lam_init, ALU.mult, ALU.add, ["lamt"], ["nlam"])
            self.P.op("dve", (lambda e, l=l: e.memset(self.lamc[0:1, l:l + 1], 1.0)), ["lamc"], ["lamc"])
        t, tk = self.cin.next()
        self.dma(t[:, 0:1152], self.strip_d, [], [tk])
        self.copy("dve", self.strip[:], t[:, 0:1152], [tk], ["strip"])
        t2, tk2 = self.cin.next()
        self.dma(t2[:, 0:4], self.sel_d, [], [tk2])
        self.copy("dve", self.sel[:], t2[:, 0:4], [tk2], ["sel"])
        self.dma(self.selb[:], self.selb_d, [], ["selb"])
        self.dma(self.cmask[:], self.cmask_d, [], ["cmask"])
        for l in range(L):
            for h in range(8):
                t, tk = self.cin.next()
                self.dma(t[0:64, 0:960], self.rpbT_d[l, h], [], [tk])
                self.act(t[0:64, 0:960], t[0:64, 0:960], AF.Exp, [tk], [tk])
                o, ok = self.cout.next()
                self.tt("dve", o[0:64, 0:960].rearrange("p (i c) -> p i c", c=64), t[0:64, 0:960].rearrange("p (i c) -> p i c", c=64),
                        self.cmask[:].unsqueeze(1).to_broadcast([64, 15, 64]), ALU.mult, [tk, "cmask"], [ok])
                self.dma(self.ebt[l][h], o[0:64, 0:960], [ok], [("ebt", l, h)], eng="pool")

    def conv_weight(self, spec):
        scr = self.dram("ws_" + spec.name, [spec.ncb, spec.nkg, 128, spec.gk * spec.w], BF16)
        spec.scr = scr
        n = spec.gk * spec.w
        for cb in range(spec.ncb):
            for kg in range(spec.nkg):
                src = spec.src[kg * spec.gk * 128:(kg + 1) * spec.gk * 128, cb * spec.w:(cb + 1) * spec.w].rearrange("(k p) n -> p k n", p=128)
                t, tk = self.cin.next()
                self.dma(t[:, 0:n].rearrange("p (k n) -> p k n", k=spec.gk), src, [], [tk])
                o, ok = self.cout.next()
                self.cv_i = getattr(self, "cv_i", 0) + 1
                eng = ("dve", "act", "pool")[self.cv_i % 3] if not spec.swap else ("dve", "act")[self.cv_i % 2]
                if not spec.swap:
                    self.copy(eng, o[:, 0:n], t[:, 0:n], [tk], [ok])
                else:
                    sv = t[:, 0:n].rearrange("p (a t d) -> p a t d", t=2, d=32)
                    dv = o[:, 0:n].rearrange("p (a t d) -> p a t d", t=2, d=32)
                    self.copy("dve", dv[:, :, 0, :], sv[:, :, 1, :], [tk], [ok])
                    self.copy("act", dv[:, :, 1, :], sv[:, :, 0, :], [tk], [ok])
                self.dma(scr[cb, kg], o[:, 0:n], [ok], [("ws", spec.name, cb, kg)], eng="pool")

    def setup_weights(self, layersB, layersA):
        self.W = {}
        for l in range(L):
            for f in range(2):
                s = WSpec("gu%d_%d" % (f, l), self.w_gu[f][l], D, 2 * FF, 256, 8)
                self.W[("gu", f, l)] = s
                s2 = WSpec("dn%d_%d" % (f, l), self.w_dn[f][l], FF, D, 128, 11)
                self.W[("dn", f, l)] = s2
            self.W[("in", l)] = WSpec("in_%d" % l, self.w_in[l], D, 6912, 256, 8)
            self.W[("insw", l)] = WSpec("insw_%d" % l, self.w_in[l][:, 1536:3328], D, 1792, 256, 8, swap=True)
            for i in range(3):
                self.W[("br", l, i)] = WSpec("br%d_%d" % (i, l), self.w_br[l, i], 512, D, 256, 4)
            self.W[("out", l)] = WSpec("out_%d" % l, self.w_o[l], D, D, 256, 8)
        order = []
        for l in sorted(set(layersA) | set(layersB)):
            if l in layersA:
                order += [("gu", 0, l), ("dn", 0, l)]
            order += [("in", l), ("insw", l)]
            if l in layersB:
                order += [("br", l, 0), ("br", l, 1), ("br", l, 2), ("out", l), ("gu", 1, l), ("dn", 1, l)]
        for k in order:
            self.conv_weight(self.W[k])

    def load_slot(self, spec, cb, kg):
        t, tk = self.wslot.next()
        n = spec.gk * spec.w
        self.dma(t[:, 0:n], spec.scr[cb, kg], [("ws", spec.name, cb, kg)], [tk])
        return t[:, 0:n].rearrange("p (k n) -> p k n", k=spec.gk), tk

    def load_h(self, blk, src_t):
        src = src_t.rearrange("(c p) t -> p c t", p=128)[:, :, blk * TB:(blk + 1) * TB]
        self.dma(self.hb[:], src, [("hbuf", blk)], ["hb"])
        self.dma(self.cosT[:], self.cosT_d[:, blk * TB:(blk + 1) * TB], [], ["cosT"])
        self.dma(self.sinT[:], self.sinT_d[:, blk * TB:(blk + 1) * TB], [], ["sinT"])

    def store_h(self, blk, dst_t):
        dst = dst_t.rearrange("(c p) t -> p c t", p=128)[:, :, blk * TB:(blk + 1) * TB]
        final = dst_t is not getattr(self, "hbuf", None)
        self.dma(dst, self.hb[:], ["hb"], [("out", blk) if final else ("hbuf", blk)], eng="pool")
        if final:
            self.outs.append(("out", blk))

    def finish(self):
        if self.launch != 0 and self.lA is not None:
            self.outs.extend(self.kvkeys[self.lA])
        self.P.op("sp", None, reads=self.outs)
        self.P.finalize(self.st)
        self.P.run_block()

    def stats_rstd(self, srcs, src_keys, inv_n, from_psum=False):
        psS, pk = self.bank("stat", [6, 7])
        n = len(srcs)
        for c, (s, sk) in enumerate(zip(srcs, src_keys)):
            sq, qk = self.sqb.next()
            self.act(sq[:], s, AF.Square, [sk], [qk])
            self.mm(psS[:], self.ones_b[:], sq[:], c == 0, c == n - 1, [qk, "ones_b"], [pk])
        ln, lk = self.rstd.next()
        self.act(ln[:], psS[:], AF.Ln, [pk, "eps_t"], [lk], scale=inv_n, bias=self.eps_t[:])
        self.act(ln[:], ln[:], AF.Exp, [lk], [lk], scale=-0.5)
        return ln, lk

    def norm_to_xnT(self, l, which):
        rstd, rk = self.stats_rstd([self.hb[:, c, :] for c in range(8)], ["hb"] * 8, 1.0 / D)
        g0 = (l * 9 + which) * 8
        for c in range(8):
            self.stt("dve", self.xnT[:, c, :], self.hb[:, c, :], self.gcols[:, g0 + c:g0 + c + 1], rstd[:], ALU.mult, ALU.mult,
                     ["hb", "gcols", rk], [("xnT", c)])

    def post_norm_add(self, l, which):
        rstd, rk = self.stats_rstd([self.ysb[:, c, :] for c in range(8)], [("ysb", c) for c in range(8)], 1.0 / D)
        g0 = (l * 9 + which) * 8
        for c in range(8):
            t, tk = self.f32s.next()
            self.stt("dve", t[:], self.ysb[:, c, :], self.gcols[:, g0 + c:g0 + c + 1], rstd[:], ALU.mult, ALU.mult, [("ysb", c), "gcols", rk], [tk])
            self.tt("pool", self.hb[:, c, :], self.hb[:, c, :], t[:], ALU.add, ["hb", tk], ["hb"])

    def ffn(self, l, f):
        gu, dn = self.W[("gu", f, l)], self.W[("dn", f, l)]
        self.norm_to_xnT(l, 0 if f == 0 else 7)
        xk = [("xnT", c) for c in range(8)]
        for jj in range(11):
            gs, gk_ = self.load_slot(gu, jj, 0)
            us, uk_ = self.load_slot(gu, 11 + jj, 0)
            for sub in range(2):
                j = 2 * jj + sub
                pg, pgk = self.bank("ffn_g", [0, 1])
                pu, puk = self.bank("ffn_u", [2, 3])
                for kc in range(8):
                    self.mm(pg[:], gs[:, kc, sub * 128:(sub + 1) * 128], self.xnT[:, kc, :], kc == 0, kc == 7, [gk_, xk[kc]], [pgk])
                for kc in range(8):
                    self.mm(pu[:], us[:, kc, sub * 128:(sub + 1) * 128], self.xnT[:, kc, :], kc == 0, kc == 7, [uk_, xk[kc]], [puk])
                s, sk = self.f32s.next()
                self.act(s[:], pg[:], AF.Silu, [pgk], [sk])
                self.tt("dve", self.hT[:, j, :], s[:], pu[:], ALU.mult, [sk, puk], [("hT", j)])
        for c in range(8):
            py, pyk = self.bank("ffn_y", [4, 5])
            for kg in range(2):
                ws, wk = self.load_slot(dn, c, kg)
                for ki in range(11):
                    kc = kg * 11 + ki
                    self.mm(py[:], ws[:, ki, :], self.hT[:, kc, :], kc == 0, kc == 21, [wk, ("hT", kc)], [pyk])
            self.copy("act" if c % 2 == 0 else "dve", self.ysb[:, c, :], py[:], [pyk], [("ysb", c)])
        self.post_norm_add(l, 1 if f == 0 else 8)

    def proj_fm(self, l, cb_list, dst_fn, rope, xk):
        win, wsw = self.W[("in", l)], self.W[("insw", l)]
        i = 0
        for cb in cb_list:
            ws, wk = self.load_slot(win, cb, 0)
            if rope:
                ss, sk_ = self.load_slot(wsw, cb - 6, 0)
            for half in range(2):
                pk_, pkk = self.bank("pj_a", [0, 1])
                for kc in range(8):
                    self.mm(pk_[:], ws[:, kc, half * 128:(half + 1) * 128], self.xnT[:, kc, :], kc == 0, kc == 7, [wk, xk[kc]], [pkk])
                dst, dk, post = dst_fn(i)
                if dst is None:
                    i += 1
                    continue
                if rope:
                    ps2, ps2k = self.bank("pj_b", [2, 3])
                    for kc in range(8):
                        self.mm(ps2[:], ss[:, kc, half * 128:(half + 1) * 128], self.xnT[:, kc, :], kc == 0, kc == 7, [sk_, xk[kc]], [ps2k])
                    t1, t1k = self.f32s.next()
                    t2, t2k = self.f32s.next()
                    self.tt("dve", t1[:], pk_[:], self.cosT[:], ALU.mult, [pkk, "cosT"], [t1k])
                    self.tt("dve", t2[:], ps2[:], self.sinT[:], ALU.mult, [ps2k, "sinT"], [t2k])
                    self.tt("pool", dst, t1[:], t2[:], ALU.add, [t1k, t2k], [dk])
                else:
                    self.copy("act" if i % 2 == 0 else "dve", dst, pk_[:], [pkk], [dk])
                if post is not None:
                    post()
                i += 1

    def stageA(self, l, blk):
        self.ffn(l, 0)
        if self.cfg.get("skip_kv"):
            return
        self.norm_to_xnT(l, 2)
        xk = [("xnT", c) for c in range(8)]
        kv = self.kvloc[l]
        t0 = blk * TB
        if not hasattr(self, "kvkeys"):
            self.kvkeys = {i: [] for i in range(L)}

        def kdst_factory(row0, only0=False):
            def fn(i):
                if only0 and i > 0:
                    return None, None, None
                t, tk = self.kst.next()
                r0 = row0 + i * 128

                def post():
                    self.dma(kv[r0:r0 + 128, t0:t0 + TB], t[:], [tk], [self.key("kvw%d" % l)], eng="pool")
                    self.kvkeys[l].append("kvw%d#%d" % (l, self.uid))
                return t[:], tk, post
            return fn

        self.proj_fm(l, [2, 3], kdst_factory(KA0), False, xk)
        self.proj_fm(l, [8], kdst_factory(KB0, True), True, xk)
        self.proj_fm(l, [11, 12], kdst_factory(KC0), True, xk)
        win = self.W[("in", l)]
        for (cbs, vrow0, ncols) in (([4, 5], VA0, 512), ([8], VB0, 128), ([13, 14], VC0, 512)):
            vt, vk = self.vst.next()
            for ci, cb in enumerate(cbs):
                ws, wk = self.load_slot(win, cb, 0)
                for tt_ in range(4):
                    pv, pvk = self.bank("pj_v", [4, 5])
                    if ncols == 128:
                        for kc in range(8):
                            self.mm(pv[:, 0:128], self.xnT[:, kc, tt_ * 128:(tt_ + 1) * 128], ws[:, kc, 128:256], kc == 0, kc == 7, [wk, xk[kc]], [pvk])
                        self.copy("act" if tt_ % 2 == 0 else "dve", vt[:, tt_, 0:128], pv[:, 0:128], [pvk], [vk])
                    else:
                        for kc in range(8):
                            self.mm(pv[:, 0:256], self.xnT[:, kc, tt_ * 128:(tt_ + 1) * 128], ws[:, kc, :], kc == 0, kc == 7, [wk, xk[kc]], [pvk])
                        self.copy("act" if tt_ % 2 == 0 else "dve", vt[:, tt_, ci * 256:(ci + 1) * 256], pv[:, 0:256], [pvk], [vk])
            nrows = ncols * NT // NT
            view = kv[vrow0:vrow0 + ncols, :].rearrange("r (q c) -> (r q) c", c=ncols)
            dstv = view[t0:t0 + TB, :].rearrange("(t p) c -> p t c", p=128)
            self.dma(dstv, vt[:, :, 0:ncols], [vk], [self.key("kvw%d" % l)], eng="pool")
            self.kvkeys[l].append("kvw%d#%d" % (l, self.uid))

    def allgather(self, l):
        if NRANK == 1:
            return
        if self.cfg.get("no_ag"):
            self.dma(self.kvall[l][0:KV_ROWS, :], self.kvloc[l], list(self.kvkeys[l]), [("kvall", l)], eng="pool")
            return
        self.P.op("pool", lambda e: e.collective_compute("AllGather", ALU.bypass, replica_groups=[[0, 1], [2, 3], [4, 5], [6, 7]],
                                                         ins=[self.kvloc[l]], outs=[self.kvall[l]]),
                  reads=list(self.kvkeys[l]), writes=[("kvall", l)], dma=True)

    def setup_layer_tables(self, l):
        pass

    def kv_rows_tok(self, l, row0, nrows, tok_lo, tok_hi):
        out = []
        for r in range(2):
            lo, hi = max(tok_lo, r * NT), min(tok_hi, (r + 1) * NT)
            if lo < hi:
                out.append((self.kvall[l][r * KV_ROWS + row0:r * KV_ROWS + row0 + nrows, lo - r * NT:hi - r * NT], lo - tok_lo, hi - lo))
        return out

    def v_view(self, l, r, vrow0, ncols):
        return self.kvall[l][r * KV_ROWS + vrow0:r * KV_ROWS + vrow0 + ncols, :].rearrange("r (q c) -> (r q) c", c=ncols)

    def na_valid(self, half, R0, krl, j):
        if NRANK == 1:
            half = 0
        qr = R0 + j + (NT // 64) * half
        kr = krl + (NT // 64) * half
        ws = min(max(qr - 4, 0), 120)
        return (0 <= kr <= 127) and (ws <= kr < ws + 8)

    def attn_na(self, l, blk):
        R0 = 8 * blk
        NR_ = NT // 64
        own_lo, own_hi = max(R0 - 4, 0), min(R0 + 12, NR_)
        kvl = self.kvloc[l]
        pieces = []
        if R0 - 4 < 0 and NRANK == 2:
            pieces.append(("prev", R0 - 4, 0))
        pieces.append(("own", own_lo, own_hi))
        if R0 + 12 > NR_ and NRANK == 2:
            pieces.append(("next", NR_, R0 + 12))
        for kind, lo, hi in pieces:
            s0 = lo - (R0 - 4)
            n = (hi - lo) * 64
            if kind == "own":
                ksrc = kvl[KA0:KA0 + 512, lo * 64:hi * 64]
                vsrc = kvl[VA0:VA0 + 512, :].rearrange("r (q c) -> (r q) c", c=512)[lo * 64:hi * 64, :]
                rk = list(self.kvkeys[l])
            elif kind == "prev":
                ksrc = self.kvall[l][KA0:KA0 + 512, NT + lo * 64:NT + hi * 64]
                vsrc = self.v_view(l, 0, VA0, 512)[NT + lo * 64:NT + hi * 64, :]
                rk = [("kvall", l)]
            else:
                ksrc = self.kvall[l][KV_ROWS + KA0:KV_ROWS + KA0 + 512, (lo - NR_) * 64:(hi - NR_) * 64]
                vsrc = self.v_view(l, 1, VA0, 512)[(lo - NR_) * 64:(hi - NR_) * 64, :]
                rk = [("kvall", l)]
            self.dma(self.kTA[:, :, s0 * 64:s0 * 64 + n], ksrc.rearrange("(c p) t -> p c t", p=128), rk, ["kTA"])
            self.dma(self.vA[:, s0:s0 + (hi - lo), :], vsrc.rearrange("(r p) c -> p r c", p=64), rk, ["vA"])
        for h in range(8):
            ch, b = h // 2, (h % 2) * 64
            eb, ebk = self.ebh.next()
            self.dma(eb[:], self.ebt[l][h], [("ebt", l, h)], [ebk])
            ebv = eb[:].rearrange("p (i c) -> p i c", c=64)
            pO, pOk = self.bank("att_o", [2, 3])
            pR, pRk = self.bank("att_r", [4, 5])
            first = True
            segs = []
            for s_ in range(16):
                krl = R0 - 4 + s_
                cats = [(self.na_valid(0, R0, krl, j), self.na_valid(1, R0, krl, j)) for j in range(8)]
                j = 0
                while j < 8:
                    if cats[j] == (False, False):
                        j += 1
                        continue
                    j1 = j
                    while j1 + 1 < 8 and cats[j1 + 1] == cats[j]:
                        j1 += 1
                    segs.append((s_, krl, j, j1, cats[j]))
                    j = j1 + 1
            def na_S(si, ch=ch, b=b):
                s_, krl, j0, j1, cat = segs[si]
                nq = (j1 - j0 + 1) * 64
                pS, pSk = self.bank("att_s", [0, 1])
                self.mm(pS[0:64, 0:nq], self.kTA[b:b + 64, ch, s_ * 64:(s_ + 1) * 64], self.qT[b:b + 64, ch, j0 * 64:(j1 + 1) * 64], True, True, ["kTA", ("qT", ch)], [pSk])
                return pS, pSk

            cur = na_S(0)
            for si, (s_, krl, j0, j1, cat) in enumerate(segs):
                nxt = na_S(si + 1) if si + 1 < len(segs) else None
                pS, pSk = cur
                nq = (j1 - j0 + 1) * 64
                c0, c1 = j0 * 64, (j1 + 1) * 64
                e, ek = self.bfs.next()
                self.act(e[0:64, 0:nq], pS[0:64, 0:nq], AF.Exp, [pSk], [ek], scale=0.125)
                pm, pmk = self.pm.next()
                idx0 = 7 - krl + R0 + j0
                ev = e[0:64, 0:nq].rearrange("p (i c) -> p i c", c=64)
                pv = pm[0:64, 0:nq].rearrange("p (i c) -> p i c", c=64)
                tb = ebv[:, idx0:idx0 + (j1 - j0 + 1), :]
                if cat == (True, True):
                    self.tt("dve", pv, ev, tb, ALU.mult, [ek, ebk], [pmk])
                else:
                    fl = self.flags[0:64, 1:2] if cat == (True, False) else self.flags[0:64, 0:1]
                    self.stt("dve", pv, ev, fl, tb, ALU.mult, ALU.mult, [ek, ebk, "flags"], [pmk])
                last = si == len(segs) - 1
                self.mm(pO[0:64, c0:c1], self.vA[0:64, s_, h * 64:(h + 1) * 64], pm[0:64, 0:nq], si == 0, last, ["vA", pmk], [pOk], skip_group_check=True)
                self.mm(pR[0:64, c0:c1], self.ones_b[0:64, 0:64], pm[0:64, 0:nq], si == 0, last, ["ones_b", pmk], [pRk], skip_group_check=True)
                cur = nxt
            r, rk_ = self.f32s.next()
            self.recip(r[0:64, :], pR[0:64, :], [pRk], [rk_])
            self.tt("dve", self.oT[0][b:b + 64, ch, :], pO[0:64, :], r[0:64, :], ALU.mult, [pOk, rk_], [("oT0", ch, h % 2)])

    def attn_sw(self, l, blk):
        kvl = self.kvloc[l]
        T0 = blk * 4 - 1
        tiles = []
        for s_ in range(6):
            T = T0 + s_
            kind = "prev" if T < 0 else ("next" if T >= NT // 128 else "own")
            if NRANK == 1 and kind != "own":
                kind = "none"
            tiles.append(kind)
        own_s = [s_ for s_ in range(6) if tiles[s_] == "own"]
        lo, hi = T0 + own_s[0], T0 + own_s[-1] + 1
        vview = kvl[VB0:VB0 + 128, :].rearrange("r (q c) -> (r q) c", c=128)
        rk = list(self.kvkeys[l])
        for kvh in range(2):
            for ph in range(2):
                self.dma(self.kTB[ph * 64:(ph + 1) * 64, kvh, own_s[0] * 128:(own_s[-1] + 1) * 128], kvl[KB0 + kvh * 64:KB0 + (kvh + 1) * 64, lo * 128:hi * 128], rk, ["kTB"])
        self.dma(self.vB[:, own_s[0]:own_s[-1] + 1, :], vview[lo * 128:hi * 128, :].rearrange("(t p) c -> p t c", p=128), rk, ["vB"])
        for s_ in range(6):
            if tiles[s_] in ("own", "none"):
                continue
            r = 0 if tiles[s_] == "prev" else 1
            t_lo = NT - 128 if r == 0 else 0
            for kvh in range(2):
                for ph in range(2):
                    self.dma(self.kTB[ph * 64:(ph + 1) * 64, kvh, s_ * 128:(s_ + 1) * 128],
                             self.kvall[l][r * KV_ROWS + KB0 + kvh * 64:r * KV_ROWS + KB0 + (kvh + 1) * 64, t_lo:t_lo + 128], [("kvall", l)], ["kTB"])
            self.dma(self.vB[:, s_, :], self.v_view(l, r, VB0, 128)[t_lo:t_lo + 128, :], [("kvall", l)], ["vB"])
        for h in range(8):
            ch, b, kvh = h // 2, (h % 2) * 64, h // 4
            pO, pOk = self.bank("att_o", [2, 3])
            pR, pRk = self.bank("att_r", [4, 5])
            proc = [s_ for s_ in range(6) if tiles[s_] != "none"]

            def sw_S(pi, ch=ch, b=b, kvh=kvh):
                s_ = proc[pi]
                pS, pSk = self.bank("att_s", [0, 1])
                self.mm(pS, self.kTB[b:b + 64, kvh, s_ * 128:(s_ + 1) * 128], self.qT[b:b + 64, ch, :], True, True, ["kTB", ("qT", ch)], [pSk])
                return pS, pSk

            cur = sw_S(0)
            for pi, s_ in enumerate(proc):
                nxt = sw_S(pi + 1) if pi + 1 < len(proc) else None
                pS, pSk = cur
                e, ek = self.bfs.next()
                self.act(e[:], pS, AF.Exp, [pSk], [ek], scale=0.125)
                pm, pmk = self.pm.next()
                off = 512 - (s_ - 1) * 128
                if tiles[s_] == "own":
                    self.tt("dve", pm[:], e[:], self.strip[:, off:off + 512], ALU.mult, [ek, "strip"], [pmk])
                else:
                    fl = self.flags[:, 0:1] if tiles[s_] == "prev" else self.flags[:, 1:2]
                    self.stt("dve", pm[:], e[:], fl, self.strip[:, off:off + 512], ALU.mult, ALU.mult, [ek, "strip", "flags"], [pmk])
                self.mm(pO[0:64, :], self.vB[:, s_, kvh * 64:(kvh + 1) * 64], pm[:], s_ == proc[0], s_ == proc[-1], ["vB", pmk], [pOk])
                self.mm(pR[0:64, :], self.ones_b[:, 0:64], pm[:], s_ == proc[0], s_ == proc[-1], ["ones_b", pmk], [pRk])
                cur = nxt
            r, rk_ = self.f32s.next()
            self.ts("dve", r[0:64, :], pR[0:64, :], self.esink[0:64, l * 8 + h:l * 8 + h + 1], None, ALU.add, None, [pRk, "esink"], [rk_])
            self.recip(r[0:64, :], r[0:64, :], [rk_], [rk_])
            self.tt("dve", self.oT[1][b:b + 64, ch, :], pO[0:64, :], r[0:64, :], ALU.mult, [pOk, rk_], [("oT1", ch, h % 2)])

    def attn_diff(self, l, blk):
        npr = NT // 1024
        nchunks = S // 1024
        for h in range(4):
            pO = [self.ps[4], self.ps[5]]
            pOk = ["ps4", "ps5"]
            acc = self.dacc
            self.P.op("pool", lambda e: e.memset(self.dacc[:], 0.0), [], ["dacc"])
            chunks = {}

            def load_chunk(kc8, h=h):
                r, tl = kc8 // npr, (kc8 % npr) * 1024
                kt_, ktk = self.kTC.next()
                vt_, vtk = self.vC.next()
                if NRANK == 1:
                    kvl_ = self.kvloc[l]
                    rk_ = list(self.kvkeys[l])
                    self.dma(kt_[:], kvl_[KC0 + h * 128:KC0 + (h + 1) * 128, tl:tl + 1024], rk_, [ktk])
                    vv_ = kvl_[VC0:VC0 + 512, :].rearrange("r (q c) -> (r q) c", c=512)
                    self.dma(vt_[:], vv_[tl:tl + 1024, h * 128:(h + 1) * 128].rearrange("(t p) c -> p t c", p=128), rk_, [vtk])
                else:
                    self.dma(kt_[:], self.kvall[l][r * KV_ROWS + KC0 + h * 128:r * KV_ROWS + KC0 + (h + 1) * 128, tl:tl + 1024], [("kvall", l)], [ktk])
                    self.dma(vt_[:], self.v_view(l, r, VC0, 512)[tl:tl + 1024, h * 128:(h + 1) * 128].rearrange("(t p) c -> p t c", p=128), [("kvall", l)], [vtk])
                chunks[kc8] = (kt_, ktk, vt_, vtk)

            items = [(kc8, kt) for kc8 in range(nchunks) for kt in range(8)]

            def emit_S(idx, h=h):
                kc8, kt = items[idx]
                if kt == 0:
                    load_chunk(kc8)
                kt_, ktk, vt_, vtk = chunks[kc8]
                pS2, pS2k = self.ps2[self.ps2_i % 2]
                self.ps2_i += 1
                for t in range(2):
                    self.mm(pS2[:, t * 512:(t + 1) * 512], kt_[t * 64:(t + 1) * 64, kt * 128:(kt + 1) * 128], self.qT[t * 64:(t + 1) * 64, h, :],
                            True, True, [ktk, ("qT", h)], [pS2k])
                return pS2, pS2k

            cur = emit_S(0)
            n_it = len(items)
            for idx in range(n_it):
                nxt = emit_S(idx + 1) if idx + 1 < n_it else None
                kc8, kt = items[idx]
                kt_, ktk, vt_, vtk = chunks[kc8]
                pS2, pS2k = cur
                e, ek = self.e2.next()
                self.act(e[:], pS2, AF.Exp, [pS2k], [ek], scale=0.125)
                self.tt("dve", acc[:], acc[:], e[:], ALU.add, ["dacc", ek], ["dacc"])
                for t in range(2):
                    self.mm(pO[t], vt_[:, kt, :], e[:, t * 512:(t + 1) * 512], idx == 0, idx == n_it - 1, [vtk, ek], [pOk[t]])
                cur = nxt
            rs = []
            for t in range(2):
                pR, pRk = self.bank("stat", [6, 7])
                self.mm(pR, self.ones_f[:], acc[:, t * 512:(t + 1) * 512], True, True, ["ones_f", "dacc"], [pRk])
                r_, rk_ = self.dfb.next()
                self.recip(r_[:], pR, [pRk], [rk_])
                rs.append((r_, rk_))
            o0, o0k = self.dfb.next()
            self.tt("dve", o0[:], pO[0], rs[0][0][:], ALU.mult, [pOk[0], rs[0][1]], [o0k])
            o1, o1k = self.dfb.next()
            self.tt("dve", o1[:], pO[1], rs[1][0][:], ALU.mult, [pOk[1], rs[1][1]], [o1k])
            self.stt("dve", o0[:], o1[:], self.nlam[:, l:l + 1], o0[:], ALU.mult, ALU.add, [o1k, o0k, "nlam"], [o0k])
            rstd, rk2 = self.stats_rstd([o0[:]], [o0k], 1.0 / 128)
            self.stt("dve", self.oT[2][:, h, :], o0[:], self.gsub[:, l:l + 1], rstd[:], ALU.mult, ALU.mult, [o0k, "gsub", rk2], [("oT2", h, 0), ("oT2", h, 1)])

    def merge_out(self, l):
        win, wo = self.W[("in", l)], self.W[("out", l)]
        xk = [("xnT", c) for c in range(8)]
        for cp in range(4):
            for i in range(3):
                gs, gk_ = self.load_slot(win, 15 + i * 4 + cp, 0)
                bs, bk_ = self.load_slot(self.W[("br", l, i)], cp, 0)
                for sub in range(2):
                    c = 2 * cp + sub
                    pY, pYk = self.bank("mg_y", [0, 1])
                    pG, pGk = self.bank("mg_g", [2, 3])
                    for kc in range(4):
                        self.mm(pY[:], bs[:, kc, sub * 128:(sub + 1) * 128], self.oT[i][:, kc, :], kc == 0, kc == 3, [bk_, ("oT%d" % i, kc, 0), ("oT%d" % i, kc, 1)], [pYk])
                    for kc in range(8):
                        self.mm(pG[:], gs[:, kc, sub * 128:(sub + 1) * 128], self.xnT[:, kc, :], kc == 0, kc == 7, [gk_, xk[kc]], [pGk])
                    g, gk2 = self.f32s.next()
                    col = (l * 3 + i) * 8 + c
                    self.act(g[:], pG[:], AF.Exp, [pGk, "nbg"], [gk2], scale=-1.0, bias=self.nbg[:, col:col + 1])
                    self.ts("pool", g[:], g[:], 1.0, None, ALU.add, None, [gk2], [gk2])
                    self.recip(g[:], g[:], [gk2], [gk2])
                    if i == 0:
                        self.tt("dve", self.macc[:, sub, :], pY[:], g[:], ALU.mult, [pYk, gk2], [("macc", sub)])
                    else:
                        self.tt("dve", g[:], pY[:], g[:], ALU.mult, [pYk, gk2], [gk2])
                        self.tt("pool", self.macc[:, sub, :], self.macc[:, sub, :], g[:], ALU.add, [("macc", sub), gk2], [("macc", sub)])
                    if i == 2:
                        self.copy("act", self.mT[:, c, :], self.macc[:, sub, :], [("macc", sub)], [("mT", c)])
        for cb in range(4):
            ws, wk = self.load_slot(wo, cb, 0)
            for sub in range(2):
                c = 2 * cb + sub
                pM, pMk = self.bank("ffn_y", [4, 5])
                for kc in range(8):
                    self.mm(pM[:], ws[:, kc, sub * 128:(sub + 1) * 128], self.mT[:, kc, :], kc == 0, kc == 7, [wk, ("mT", kc)], [pMk])
                self.copy("act" if c % 2 == 0 else "dve", self.ysb[:, c, :], pM[:], [pMk], [("ysb", c)])
        self.post_norm_add(l, 6)

    def stageB(self, l, blk):
        cfg = self.cfg
        self.norm_to_xnT(l, 2)
        xk = [("xnT", c) for c in range(8)]
        parts = cfg.get("parts", "ABC")
        qdst = lambda i: (self.qT[:, i, :], ("qT", i), None)
        self.proj_fm(l, [0, 1], qdst, False, xk)
        self.attn_na(l, blk)
        self.proj_fm(l, [6, 7], qdst, True, xk)
        self.attn_sw(l, blk)
        self.proj_fm(l, [9, 10], qdst, True, xk)
        self.attn_diff(l, blk)
        if cfg.get("dbg_o"):
            if not hasattr(self, "dbg_o"):
                self.dbg_o = [self.dram("dbg_o%d" % i, [512, NT], BF16, "ExternalOutput") for i in range(3)]
            for i in range(3):
                k_ = self.key("dbgo")
                self.dma(self.dbg_o[i].rearrange("(c p) t -> p c t", p=128)[:, :, blk * TB:(blk + 1) * TB], self.oT[i][:],
                         [("oT%d" % i, c, hh) for c in range(4) for hh in range(2)], [k_], eng="pool")
                self.outs.append(k_)
        self.merge_out(l)
        self.ffn(l, 1)


def build_program(cfg):
    kb = KB(cfg)
    with kb.st:
        nc = kb.build()
    return nc


def _consts():
    pos = np.arange(S, dtype=np.float32)
    inv = (np.float32(10000.0) ** (-(np.arange(0, 64, 2, dtype=np.float32) / np.float32(64)))).astype(np.float32)
    ang = (pos[:, None] * inv[None, :]).astype(np.float32)
    ang = np.concatenate([ang, ang], axis=-1)
    cos = np.cos(ang).astype(np.float32)
    sin = np.sin(ang).astype(np.float32)
    sgn = np.concatenate([-np.ones(32, np.float32), np.ones(32, np.float32)])
    cosT = np.ascontiguousarray(np.concatenate([cos.T, cos.T], axis=0))
    sinT = np.ascontiguousarray(np.concatenate([(sin * sgn).T, (sin * sgn).T], axis=0))
    c = np.arange(64)
    col_start = np.clip(c - 8, 0, 48)
    col_ok = (c[None, :] >= col_start[:, None]) & (c[None, :] < col_start[:, None] + 16)
    cmask = np.ascontiguousarray(col_ok.T.astype(np.float32))
    dc = np.clip(c[:, None] - c[None, :] + 15, 0, 30)
    kl = np.arange(128)[:, None]
    xx = np.arange(1152)[None, :]
    strip = (np.abs(xx - 512 - kl) <= 128).astype(np.float32)
    sel = np.zeros((128, 4), np.float32)
    sel[:, 0] = 1.0
    sel[:, 3] = 1.0
    selb = np.zeros((2, 256), np.float32)
    selb[0, 0:128] = 1.0
    selb[1, 128:256] = 1.0
    return cosT, sinT, cmask, dc, strip, sel, selb


def _host_inputs(inputs):
    cosT, sinT, cmask, dc, strip, sel, selb = _consts()
    f = lambda a: np.ascontiguousarray(np.asarray(a, dtype=np.float32))
    gnames = {0: "ffn1_pre_g", 1: "ffn1_post_g", 2: "mix_pre_g", 6: "mix_post_g", 7: "ffn2_pre_g", 8: "ffn2_post_g"}
    gcols = np.zeros((128, 9 * L * 8), np.float32)
    for l in range(L):
        for which, nm in gnames.items():
            g = f(inputs[nm])[l]
            gcols[:, (l * 9 + which) * 8:(l * 9 + which) * 8 + 8] = g.reshape(8, 128).T
    bg = f(inputs["b_gate"])
    bgate = np.zeros((128, L * 3 * 8), np.float32)
    for l in range(L):
        for i in range(3):
            bgate[:, (l * 3 + i) * 8:(l * 3 + i) * 8 + 8] = bg[l, i].reshape(8, 128).T
    gsub = np.ascontiguousarray(f(inputs["diff_subln_g"]).T)
    sink = f(inputs["sw_sink"]).reshape(-1)
    lamv = np.concatenate([f(inputs[k]).reshape(-1) for k in ("diff_lambda_q1", "diff_lambda_k1", "diff_lambda_q2", "diff_lambda_k2")])
    rpb = f(inputs["na_rpb"])
    idx = np.arange(15)
    rpbT = rpb[:, :, (14 - idx)[None, :, None], dc[:, None, :]]
    rpbT = np.ascontiguousarray(rpbT.reshape(L, 8, 64, 15 * 64))
    shared = {
        "ffn1_w_gu": f(inputs["ffn1_w_gu"]), "ffn2_w_gu": f(inputs["ffn2_w_gu"]),
        "ffn1_w_down": f(inputs["ffn1_w_down"]), "ffn2_w_down": f(inputs["ffn2_w_down"]),
        "w_in": f(inputs["w_in"]), "w_branch": f(inputs["w_branch"]), "w_out": f(inputs["w_out"]),
        "gcols": gcols, "bgate": bgate, "gsub": gsub, "sink": sink, "lamv": lamv, "rpbT": rpbT,
        "cmask": cmask, "strip": strip, "sel": sel, "selb": selb,
    }
    x = f(inputs["x"])
    in_maps = []
    xts = {}
    for c in range(8):
        b, half = c // 2, (c % 2 if NRANK == 2 else 0)
        m = dict(shared)
        if (b, half) not in xts:
            xts[(b, half)] = np.ascontiguousarray(x[b, half * NT:(half + 1) * NT, :].T)
        m["xT"] = xts[(b, half)]
        m["cosT"] = np.ascontiguousarray(cosT[:, half * NT:(half + 1) * NT])
        m["sinT"] = np.ascontiguousarray(sinT[:, half * NT:(half + 1) * NT])
        fl = np.zeros((128, 2), np.float32)
        if NRANK == 2:
            fl[:, 0] = half
            fl[:, 1] = 1 - half
        m["flags"] = fl
        in_maps.append(m)
    return in_maps


_NC_CACHE = {}
FUSED = True


def _get_nc(cfg_key, cfg):
    if cfg_key not in _NC_CACHE:
        _NC_CACHE[cfg_key] = build_program(cfg)
    return _NC_CACHE[cfg_key]


def _run(nc, in_maps, names):
    maps = [{k: m[k] for k in names if k in m} for m in in_maps]
    res = run_bass_kernel_spmd(nc, maps, core_ids=list(range(len(maps))))
    return res.results


WNAMES = ["ffn1_w_gu", "ffn2_w_gu", "ffn1_w_down", "ffn2_w_down", "w_in", "w_branch", "w_out", "gcols", "bgate", "gsub", "sink",
          "lamv", "rpbT", "cmask", "strip", "sel", "selb", "cosT", "sinT", "flags"]


def _pair_gather(results, name):
    out = []
    for c in range(len(results)):
        p = (c // 2) * 2
        out.append(np.concatenate([np.asarray(results[p][name]), np.asarray(results[p + 1][name])], axis=0))
    return out


def kernel_split(in_maps, cfg_extra=None):
    ce = cfg_extra or {}
    n = len(in_maps)
    r1 = _run(_get_nc("l1", dict(ce, launch=1)), in_maps, WNAMES + ["xT"])
    kvall = _pair_gather(r1, "kvloc_out")
    for c in range(n):
        in_maps[c]["h_in"] = np.asarray(r1[c]["h_out"])
        in_maps[c]["kvall_in"] = kvall[c]
        in_maps[c]["kvloc_in"] = np.asarray(r1[c]["kvloc_out"])
    del r1
    r2 = _run(_get_nc("l2", dict(ce, launch=2)), in_maps, WNAMES + ["h_in", "kvall_in", "kvloc_in"])
    kvall = _pair_gather(r2, "kvloc_out")
    for c in range(n):
        in_maps[c]["h_in"] = np.asarray(r2[c]["h_out"])
        in_maps[c]["kvall_in"] = kvall[c]
        in_maps[c]["kvloc_in"] = np.asarray(r2[c]["kvloc_out"])
    del r2
    r3 = _run(_get_nc("l3", dict(ce, launch=3)), in_maps, WNAMES + ["h_in", "kvall_in", "kvloc_in"])
    return [np.asarray(r3[c]["outT"]) for c in range(n)]


def kernel(**inputs):
    if FUSED:
        set_mode(1)
        in_maps = _host_inputs(inputs)
        res = _run(_get_nc("fused", {}), in_maps, WNAMES + ["xT"])
        out = np.empty((4, S, D), np.float32)
        for b in range(4):
            out[b] = np.asarray(res[2 * b]["outT"]).T
        return out
    set_mode(2)
    in_maps = _host_inputs(inputs)
    outs = kernel_split(in_maps)
    out = np.empty((4, S, D), np.float32)
    for c in range(8):
        b, half = c // 2, c % 2
        out[b, half * NT:(half + 1) * NT, :] = outs[c].T
    return out
```

```python
import contextlib
import numpy as np
import concourse.bass as bass
import concourse.mybir as mybir
from concourse.bass_utils import run_bass_kernel_spmd

F32 = mybir.dt.float32
BF16 = mybir.dt.bfloat16
AF = mybir.ActivationFunctionType
ALU = mybir.AluOpType

D = 1024
FF = 2816
S = 8192
NT = 4096
TB = 512
NBLK = NT // TB
NRANK = 2


def set_mode(nrank):
    global NT, NBLK, NRANK
    NRANK = nrank
    NT = S // nrank
    NBLK = NT // TB
L = 2
EPS = 1e-6
KV_ROWS = 2304
KA0, KB0, KC0, VA0, VB0, VC0 = 0, 512, 640, 1152, 1664, 1792

N_DMA_SEMS = 16
SAME_ENGINE_SYNC = True


class Op:
    __slots__ = ("eng", "emit", "deps", "is_dma", "marked", "semval", "dsem", "dval")

    def __init__(self, eng, emit, is_dma):
        self.eng = eng
        self.emit = emit
        self.deps = []
        self.is_dma = is_dma
        self.marked = False
        self.semval = 0
        self.dsem = None
        self.dval = 0


class Prog:
    ENGS = ("pe", "act", "dve", "pool", "sp")

    def __init__(self, nc):
        self.nc = nc
        self.ops = []
        self.last_writer = {}
        self.readers = {}

    def op(self, eng, emit, reads=(), writes=(), dma=False):
        o = Op(eng, emit, dma)
        deps = []
        reads = _expand(reads)
        writes = _expand(writes)
        for b in reads:
            w = self.last_writer.get(b)
            if w is not None:
                deps.append(w)
        for b in writes:
            w = self.last_writer.get(b)
            if w is not None:
                deps.append(w)
            rs = self.readers.get(b)
            if rs:
                deps.extend(rs)
        for b in reads:
            self.readers.setdefault(b, []).append(o)
        for b in writes:
            self.last_writer[b] = o
            self.readers[b] = []
        seen = set()
        for d in deps:
            if id(d) not in seen and d is not o:
                seen.add(id(d))
                o.deps.append(d)
        self.ops.append(o)
        return o

    def finalize(self, stack):
        nc = self.nc
        per_eng = {e: [] for e in self.ENGS}
        for o in self.ops:
            per_eng[o.eng].append(o)
        for o in self.ops:
            for d in o.deps:
                if d.is_dma:
                    continue
                if d.eng == o.eng and (o.eng == "pe" or not SAME_ENGINE_SYNC) and not o.is_dma:
                    continue
                d.marked = True
        self.csem = {e: stack.enter_context(nc.semaphore("c_" + e)) for e in ("pe", "act", "dve", "pool")}
        dsems = {}
        for e in self.ENGS:
            ndma = sum(1 for o in per_eng[e] if o.is_dma)
            if ndma:
                dsems[e] = [stack.enter_context(nc.semaphore("d_%s_%d" % (e, i))) for i in range(min(ndma, N_DMA_SEMS))]
        for e in self.ENGS:
            cnt = 0
            j = 0
            for o in per_eng[e]:
                if o.is_dma:
                    o.dsem = dsems[e][j % N_DMA_SEMS]
                    o.dval = 16 * (j // N_DMA_SEMS + 1)
                    j += 1
                elif o.marked:
                    cnt += 1
                    o.semval = cnt
        self.per_eng = per_eng

    def emit_engine(self, ename, eng):
        waited = {}

        def wait(sem, val):
            key = id(sem)
            if waited.get(key, 0) >= val:
                return
            waited[key] = val
            eng.wait_ge(sem, val)

        for o in self.per_eng[ename]:
            for d in o.deps:
                if d.is_dma:
                    wait(d.dsem, d.dval)
                else:
                    if not d.marked:
                        continue
                    if d.eng == ename and not o.is_dma and (ename == "pe" or not SAME_ENGINE_SYNC):
                        continue
                    wait(self.csem[d.eng], d.semval)
            if o.is_dma and o.dval > 16:
                wait(o.dsem, o.dval - 16)
            if o.emit is None:
                continue
            ins = o.emit(eng)
            if o.is_dma:
                ins.then_inc(o.dsem, 16)
            elif o.marked:
                ins.then_inc(self.csem[ename], 1)

    def run_block(self):
        nc = self.nc
        with nc.Block() as block:
            @block.tensor
            def _(e):
                self.emit_engine("pe", e)

            @block.scalar
            def _(e):
                self.emit_engine("act", e)

            @block.vector
            def _(e):
                self.emit_engine("dve", e)

            @block.gpsimd
            def _(e):
                self.emit_engine("pool", e)

            @block.sync
            def _(e):
                self.emit_engine("sp", e)


class KS(list):
    pass


def _expand(keys):
    out = []
    for k in keys:
        if isinstance(k, KS):
            out.extend(k)
        else:
            out.append(k)
    return out


class AliasRot:
    def __init__(self, views, keysets):
        self.bufs, self.keys, self.i = views, keysets, 0

    def next(self):
        b, k = self.bufs[self.i], self.keys[self.i]
        self.i = (self.i + 1) % len(self.bufs)
        return b, k


class Rot:
    def __init__(self, K, name, n, shape, dt):
        self.bufs = [K.T("%s%d" % (name, i), shape, dt) for i in range(n)]
        self.keys = ["%s%d" % (name, i) for i in range(n)]
        self.i = 0

    def next(self):
        b, k = self.bufs[self.i], self.keys[self.i]
        self.i = (self.i + 1) % len(self.bufs)
        return b, k


class WSpec:
    def __init__(self, name, src2d, K, N, w, gk, swap=False):
        self.name, self.src, self.K, self.N, self.w, self.gk, self.swap = name, src2d, K, N, w, gk, swap
        self.ncb = N // w
        self.nkg = K // (128 * gk)
        assert self.ncb * w == N and self.nkg * 128 * gk == K


class KB:
    def __init__(self, cfg):
        self.cfg = cfg
        self.nc = bass.Bass("TRN2", target_bir_lowering=False)
        self.st = contextlib.ExitStack()
        self.P = Prog(self.nc)
        self.uid = 0
        self.bank_i = {}
        self.outs = []

    def T(self, name, shape, dt=F32):
        return self.st.enter_context(self.nc.sbuf_tensor(name, shape, dt))

    def dram(self, name, shape, dt, kind="Internal"):
        return self.nc.dram_tensor(name, shape, dt, kind=kind).ap()

    def key(self, p):
        self.uid += 1
        return "%s#%d" % (p, self.uid)

    def dma(self, out, in_, reads, writes, eng="sp"):
        return self.P.op(eng, lambda e: e.dma_start(out=out, in_=in_), reads, writes, dma=True)

    def mm(self, out, lhsT, rhs, start, stop, reads, writes, **kw):
        return self.P.op("pe", lambda e: e.matmul(out, lhsT=lhsT, rhs=rhs, start=start, stop=stop, **kw), reads, writes)

    def act(self, out, in_, func, reads, writes, **kw):
        return self.P.op("act", lambda e: e.activation(out=out, in_=in_, func=func, **kw), reads, writes)

    def tt(self, eng, out, in0, in1, op, reads, writes):
        return self.P.op(eng, lambda e: e.tensor_tensor(out=out, in0=in0, in1=in1, op=op), reads, writes)

    def ts(self, eng, out, in0, s1, s2, op0, op1, reads, writes):
        if op1 is None:
            return self.P.op(eng, lambda e: e.tensor_scalar(out=out, in0=in0, scalar1=s1, scalar2=None, op0=op0), reads, writes)
        return self.P.op(eng, lambda e: e.tensor_scalar(out=out, in0=in0, scalar1=s1, scalar2=s2, op0=op0, op1=op1), reads, writes)

    def stt(self, eng, out, in0, scalar, in1, op0, op1, reads, writes):
        return self.P.op(eng, lambda e: e.scalar_tensor_tensor(out=out, in0=in0, scalar=scalar, in1=in1, op0=op0, op1=op1), reads, writes)

    def copy(self, eng, out, in_, reads, writes):
        if eng == "act":
            return self.act(out, in_, AF.Copy, reads, writes)
        return self.P.op(eng, lambda e: e.tensor_copy(out=out, in_=in_), reads, writes)

    def recip(self, out, in_, reads, writes):
        return self.P.op("dve", lambda e: e.reciprocal(out=out, in_=in_), reads, writes)

    def bank(self, role, choices):
        i = self.bank_i.get(role, 0)
        self.bank_i[role] = i + 1
        b = choices[i % len(choices)]
        return self.ps[b], "ps%d" % b

    def build(self):
        nc, cfg = self.nc, self.cfg
        IN = lambda n, s, dt=F32: self.dram(n, s, dt, "ExternalInput")
        if cfg.get("launch", 0) in (0, 1):
            self.xT = IN("xT", [D, NT])
        self.w_gu = [IN("ffn1_w_gu", [L, D, 2 * FF]), IN("ffn2_w_gu", [L, D, 2 * FF])]
        self.w_dn = [IN("ffn1_w_down", [L, FF, D]), IN("ffn2_w_down", [L, FF, D])]
        self.w_in = IN("w_in", [L, D, 6912])
        self.w_br = IN("w_branch", [L, 3, 512, D])
        self.w_o = IN("w_out", [L, D, D])
        self.gcols_d = IN("gcols", [128, 9 * L * 8])
        self.bgate_d = IN("bgate", [128, L * 3 * 8])
        self.gsub_d = IN("gsub", [128, L])
        self.sink_d = IN("sink", [L * 8])
        self.lamv_d = IN("lamv", [4 * L * 64])
        self.rpbT_d = IN("rpbT", [L, 8, 64, 15 * 64])
        self.cmask_d = IN("cmask", [64, 64])
        self.cosT_d = IN("cosT", [128, NT])
        self.sinT_d = IN("sinT", [128, NT])
        self.strip_d = IN("strip", [128, 1152])
        self.sel_d = IN("sel", [128, 4])
        self.selb_d = IN("selb", [2, 256])
        self.flags_d = IN("flags", [128, 2])
        launch = cfg.get("launch", 0)
        self.launch = launch
        OUT = lambda n, s, dt=F32: self.dram(n, s, dt, "ExternalOutput")
        self.kvloc = [None] * L
        self.kvall = [None] * L
        if launch == 0:
            self.outT = OUT("outT", [D, NT])
            self.hbuf = self.dram("hbuf", [D, NT], F32)
            self.kvloc = [self.dram("kvloc%d" % l, [KV_ROWS, NT], BF16) for l in range(L)]
            if NRANK == 2:
                self.kvall = [self.dram("kvall%d" % l, [2 * KV_ROWS, NT], BF16) for l in range(L)]
        else:
            self.lB = {2: 0, 3: 1}.get(launch)
            self.lA = {1: 0, 2: 1}.get(launch)
            if cfg.get("skipA"):
                self.lA = None
            if launch > 1:
                self.h_in = IN("h_in", [D, NT])
            self.h_out = OUT("outT" if launch == 3 else "h_out", [D, NT])
            if self.lA is not None:
                self.kvloc[self.lA] = OUT("kvloc_out", [KV_ROWS, NT], BF16)
            if self.lB is not None:
                self.kvall[self.lB] = IN("kvall_in", [2 * KV_ROWS, NT], BF16)
                self.kvloc[self.lB] = IN("kvloc_in", [KV_ROWS, NT], BF16)
        self.ebt = [self.dram("ebt%d" % l, [8, 64, 15 * 64], BF16) for l in range(L)]
        psA = self.st.enter_context(nc.psum_tensor("psA", [128, 2048], F32))
        self.ps = [psA[:, i * 512:(i + 1) * 512] for i in range(4)]
        self.ps += [self.st.enter_context(nc.psum_tensor("ps%d" % i, [128, 512], F32))[:] for i in range(4, 8)]
        self.ps2 = [(psA[:, 0:1024], KS(["ps0", "ps1"])), (psA[:, 1024:2048], KS(["ps2", "ps3"]))]
        self.ps2_i = 0
        self.ones_f = self.T("ones_f", [128, 128], F32)
        self.ones_b = self.T("ones_b", [128, 128], BF16)
        self.gcols = self.T("gcols_s", [128, 9 * L * 8], F32)
        self.nbg = self.T("nbg", [128, L * 3 * 8], F32)
        self.gsub = self.T("gsub_s", [128, L], F32)
        self.esink = self.T("esink", [128, L * 8], F32)
        self.lamt = self.T("lamt", [128, 8 * L], F32)
        self.lamc = self.T("lamc", [128, L], F32)
        self.nlam = self.T("nlam", [128, L], F32)
        self.dacc = self.T("dacc", [128, 2 * TB], F32)
        self.e2 = Rot(self, "e2b", 3, [128, 2 * TB], BF16)
        self.strip = self.T("strip_s", [128, 1152], BF16)
        self.sel = self.T("sel_s", [128, 4], BF16)
        self.selb = self.T("selb_s", [2, 256], F32)
        self.cmask = self.T("cmask_s", [64, 64], F32)
        self.eps_t = self.T("eps_t", [128, 1], F32)
        self.one_t = self.T("one_t", [128, 1], F32)
        self.flags = self.T("flags_s", [128, 2], F32)
        self.hb = self.T("hb", [128, 8, TB], F32)
        self.xnT = self.T("xnT", [128, 8, TB], BF16)
        self.hT = self.T("hT", [128, 22, TB], BF16)
        self.ysb = self.T("ysb", [128, 8, TB], F32)
        self.cosT = self.T("cosT_s", [128, TB], F32)
        self.sinT = self.T("sinT_s", [128, TB], F32)
        self.wslot = Rot(self, "wslot", 4, [128, 2048], BF16)
        self.f32s = Rot(self, "f32s", 3, [128, TB], F32)
        self.bfs = Rot(self, "bfs", 3, [128, TB], BF16)
        self.rstd = Rot(self, "rstd", 2, [128, TB], F32)
        self.dfb = Rot(self, "dfb", 4, [128, TB], F32)
        self.sqb = Rot(self, "sqb", 2, [128, TB], BF16)
        self.pm = Rot(self, "pm", 3, [128, TB], BF16)
        self.kst = Rot(self, "kst", 2, [128, TB], BF16)
        self.vst = Rot(self, "vst", 1, [128, 4, 512], BF16)
        self.cin = AliasRot([self.ysb[:, 0:4, :].rearrange("p a b -> p (a b)"), self.ysb[:, 4:8, :].rearrange("p a b -> p (a b)")],
                            [KS([("ysb", c) for c in range(0, 4)]), KS([("ysb", c) for c in range(4, 8)])])
        self.cout = AliasRot([self.hT[:, 0:4, :].rearrange("p a b -> p (a b)"), self.hT[:, 4:8, :].rearrange("p a b -> p (a b)")],
                             [KS([("hT", c) for c in range(0, 4)]), KS([("hT", c) for c in range(4, 8)])])
        self.qT = self.T("qT", [128, 4, TB], BF16)
        self.oT = [self.T("oT%d" % i, [128, 4, TB], BF16) for i in range(3)]
        self.mT = self.T("mT", [128, 8, TB], BF16)
        self.macc = self.T("macc", [128, 2, TB], F32)
        self.kTA = self.T("kTA", [128, 4, 1024], BF16)
        self.vA = self.T("vA", [128, 16, 512], BF16)
        self.ebh = Rot(self, "ebh", 2, [128, 15 * 64], BF16)
        self.kTB = self.T("kTB", [128, 2, 768], BF16)
        self.vB = self.T("vB", [128, 6, 128], BF16)
        self.kTC = Rot(self, "kTC", 2, [128, 1024], BF16)
        self.vC = Rot(self, "vC", 2, [128, 8, 128], BF16)

        self.kvkeys = {i: [] for i in range(L)}
        self.setup_consts()
        nl = cfg.get("layers", L)
        nblk = cfg.get("nblk", NBLK)
        stop_after = cfg.get("stop_after", None)
        if launch == 0:
            self.setup_weights(list(range(nl)), list(range(nl)))
            for blk in range(nblk):
                self.load_h(blk, self.xT)
                self.stageA(0, blk)
                self.store_h(blk, self.hbuf)
            for l in range(nl):
                self.allgather(l)
                if stop_after == ("A", l):
                    break
                for blk in range(nblk):
                    self.load_h(blk, self.hbuf)
                    self.stageB(l, blk)
                    last = (l == nl - 1)
                    if not last:
                        self.stageA(l + 1, blk)
                    self.store_h(blk, self.outT if (last and not cfg.get("debug")) else self.hbuf)
        else:
            self.setup_weights([] if self.lB is None else [self.lB], [] if self.lA is None else [self.lA])
            for blk in range(nblk):
                self.load_h(blk, self.xT if launch == 1 else self.h_in)
                if self.lB is not None:
                    self.stageB(self.lB, blk)
                if self.lA is not None:
                    self.stageA(self.lA, blk)
                self.store_h(blk, self.h_out)
        if cfg.get("debug") and launch == 0:
            dh = self.dram("dbg_h", [D, NT], F32, "ExternalOutput")
            self.dma(dh, self.hbuf, [("hbuf", b) for b in range(nblk)], ["dbg_h"], eng="pool")
            self.outs.append("dbg_h")
            dk = self.dram("dbg_kv", [2 * KV_ROWS, NT], BF16, "ExternalOutput")
            self.dma(dk, self.kvall[0], [("kvall", 0)], ["dbg_kv"], eng="pool")
            self.outs.append("dbg_kv")
        self.finish()
        return nc

    def setup_consts(self):
        P = self.P
        P.op("pool", lambda e: e.memset(self.ones_f[:], 1.0), writes=["ones_f"])
        P.op("pool", lambda e: e.memset(self.ones_b[:], 1.0), writes=["ones_b"])
        P.op("pool", lambda e: e.memset(self.eps_t[:], EPS), writes=["eps_t"])
        P.op("pool", lambda e: e.memset(self.one_t[:], 1.0), writes=["one_t"])
        self.dma(self.flags[:], self.flags_d, [], ["flags"])
        self.dma(self.gcols[:], self.gcols_d, [], ["gcols"])
        for l in range(L):
            for which in (1, 8):
                c0 = (l * 9 + which) * 8
                self.ts("dve", self.gcols[:, c0:c0 + 8], self.gcols[:, c0:c0 + 8], 0.5, None, ALU.mult, None, ["gcols"], ["gcols"])
        self.dma(self.nbg[:], self.bgate_d, [], ["nbg"])
        self.ts("dve", self.nbg[:], self.nbg[:], -1.0, None, ALU.mult, None, ["nbg"], ["nbg"])
        self.dma(self.gsub[:], self.gsub_d, [], ["gsub"])
        for l in range(L):
            lam_init = 0.8 - 0.6 * float(np.exp(-0.3 * l))
            self.ts("dve", self.gsub[:, l:l + 1], self.gsub[:, l:l + 1], 1.0 - lam_init, None, ALU.mult, None, ["gsub"], ["gsub"])
        self.dma(self.esink[:], self.sink_d.partition_broadcast(128), [], ["esink"])
        self.act(self.esink[:], self.esink[:], AF.Exp, ["esink"], ["esink"])
        lrt, lrk = self.cin.next()
        self.dma(lrt[:, 0:4 * L * 64], self.lamv_d.partition_broadcast(128), [], [lrk])
        lr = lrt
        n = L * 64
        for l in range(L):
            for pair in range(2):
                q = lr[:, (2 * pair) * n + l * 64:(2 * pair) * n + (l + 1) * 64]
                k = lr[:, (2 * pair + 1) * n + l * 64:(2 * pair + 1) * n + (l + 1) * 64]
                col = l * 4 + pair
                tmp, tk = self.f32s.next()
                self.tt("dve", tmp[:, 0:64], q, k, ALU.mult, [lrk], [tk])
                self.P.op("act", (lambda e, tmp=tmp, col=col: e.activation(out=tmp[:, 64:128], in_=tmp[:, 0:64], func=AF.Copy, accum_out=self.lamt[:, col:col + 1])), [tk], [tk, "lamt"])
            self.act(self.lamt[:, l * 4:l * 4 + 2], self.lamt[:, l * 4:l * 4 + 2], AF.Exp, ["lamt"], ["lamt"])
            lam_init = 0.8 - 0.6 * float(np.exp(-0.3 * l))
            self.tt("dve", self.lamt[:, l * 4 + 2:l * 4 + 3], self.lamt[:, l * 4:l * 4 + 1], self.lamt[:, l * 4 + 1:l * 4 + 2], ALU.subtract, ["lamt"], ["lamt"])
            self.ts("dve", self.lamc[:, l:l + 1], self.lamt[:, l * 4 + 2:l * 4 + 3], -1.0, -lam_init, ALU.mult, ALU.add, ["lamt"], ["lamc"])
            self.ts("dve", self.nlam[:, l:l + 1], self.lamt[:, l * 4 + 2:l * 4 + 3], -1.0, -lam_init, ALU.mult, ALU.add, ["lamt"], ["nlam"])
            self.P.op("dve", (lambda e, l=l: e.memset(self.lamc[0:1, l:l + 1], 1.0)), ["lamc"], ["lamc"])
        t, tk = self.cin.next()
        self.dma(t[:, 0:1152], self.strip_d, [], [tk])
        self.copy("dve", self.strip[:], t[:, 0:1152], [tk], ["strip"])
        t2, tk2 = self.cin.next()
        self.dma(t2[:, 0:4], self.sel_d, [], [tk2])
        self.copy("dve", self.sel[:], t2[:, 0:4], [tk2], ["sel"])
        self.dma(self.selb[:], self.selb_d, [], ["selb"])
        self.dma(self.cmask[:], self.cmask_d, [], ["cmask"])
        for l in range(L):
            for h in range(8):
                t, tk = self.cin.next()
                self.dma(t[0:64, 0:960], self.rpbT_d[l, h], [], [tk])
                self.act(t[0:64, 0:960], t[0:64, 0:960], AF.Exp, [tk], [tk])
                o, ok = self.cout.next()
                self.tt("dve", o[0:64, 0:960].rearrange("p (i c) -> p i c", c=64), t[0:64, 0:960].rearrange("p (i c) -> p i c", c=64),
                        self.cmask[:].unsqueeze(1).to_broadcast([64, 15, 64]), ALU.mult, [tk, "cmask"], [ok])
                self.dma(self.ebt[l][h], o[0:64, 0:960], [ok], [("ebt", l, h)], eng="pool")

    def conv_weight(self, spec):
        scr = self.dram("ws_" + spec.name, [spec.ncb, spec.nkg, 128, spec.gk * spec.w], BF16)
        spec.scr = scr
        n = spec.gk * spec.w
        for cb in range(spec.ncb):
            for kg in range(spec.nkg):
                src = spec.src[kg * spec.gk * 128:(kg + 1) * spec.gk * 128, cb * spec.w:(cb + 1) * spec.w].rearrange("(k p) n -> p k n", p=128)
                t, tk = self.cin.next()
                self.dma(t[:, 0:n].rearrange("p (k n) -> p k n", k=spec.gk), src, [], [tk])
                o, ok = self.cout.next()
                self.cv_i = getattr(self, "cv_i", 0) + 1
                eng = ("dve", "act", "pool")[self.cv_i % 3] if not spec.swap else ("dve", "act")[self.cv_i % 2]
                if not spec.swap:
                    self.copy(eng, o[:, 0:n], t[:, 0:n], [tk], [ok])
                else:
                    sv = t[:, 0:n].rearrange("p (a t d) -> p a t d", t=2, d=32)
                    dv = o[:, 0:n].rearrange("p (a t d) -> p a t d", t=2, d=32)
                    self.copy("dve", dv[:, :, 0, :], sv[:, :, 1, :], [tk], [ok])
                    self.copy("act", dv[:, :, 1, :], sv[:, :, 0, :], [tk], [ok])
                self.dma(scr[cb, kg], o[:, 0:n], [ok], [("ws", spec.name, cb, kg)], eng="pool")

    def setup_weights(self, layersB, layersA):
        self.W = {}
        for l in range(L):
            for f in range(2):
                s = WSpec("gu%d_%d" % (f, l), self.w_gu[f][l], D, 2 * FF, 256, 8)
                self.W[("gu", f, l)] = s
                s2 = WSpec("dn%d_%d" % (f, l), self.w_dn[f][l], FF, D, 128, 11)
                self.W[("dn", f, l)] = s2
            self.W[("in", l)] = WSpec("in_%d" % l, self.w_in[l], D, 6912, 256, 8)
            self.W[("insw", l)] = WSpec("insw_%d" % l, self.w_in[l][:, 1536:3328], D, 1792, 256, 8, swap=True)
            for i in range(3):
                self.W[("br", l, i)] = WSpec("br%d_%d" % (i, l), self.w_br[l, i], 512, D, 256, 4)
            self.W[("out", l)] = WSpec("out_%d" % l, self.w_o[l], D, D, 256, 8)
        order = []
        for l in sorted(set(layersA) | set(layersB)):
            if l in layersA:
                order += [("gu", 0, l), ("dn", 0, l)]
            order += [("in", l), ("insw", l)]
            if l in layersB:
                order += [("br", l, 0), ("br", l, 1), ("br", l, 2), ("out", l), ("gu", 1, l), ("dn", 1, l)]
        for k in order:
            self.conv_weight(self.W[k])

    def load_slot(self, spec, cb, kg):
        t, tk = self.wslot.next()
        n = spec.gk * spec.w
        self.dma(t[:, 0:n], spec.scr[cb, kg], [("ws", spec.name, cb, kg)], [tk])
        return t[:, 0:n].rearrange("p (k n) -> p k n", k=spec.gk), tk

    def load_h(self, blk, src_t):
        src = src_t.rearrange("(c p) t -> p c t", p=128)[:, :, blk * TB:(blk + 1) * TB]
        self.dma(self.hb[:], src, [("hbuf", blk)], ["hb"])
        self.dma(self.cosT[:], self.cosT_d[:, blk * TB:(blk + 1) * TB], [], ["cosT"])
        self.dma(self.sinT[:], self.sinT_d[:, blk * TB:(blk + 1) * TB], [], ["sinT"])

    def store_h(self, blk, dst_t):
        dst = dst_t.rearrange("(c p) t -> p c t", p=128)[:, :, blk * TB:(blk + 1) * TB]
        final = dst_t is not getattr(self, "hbuf", None)
        self.dma(dst, self.hb[:], ["hb"], [("out", blk) if final else ("hbuf", blk)], eng="pool")
        if final:
            self.outs.append(("out", blk))

    def finish(self):
        if self.launch != 0 and self.lA is not None:
            self.outs.extend(self.kvkeys[self.lA])
        self.P.op("sp", None, reads=self.outs)
        self.P.finalize(self.st)
        self.P.run_block()

    def stats_rstd(self, srcs, src_keys, inv_n, from_psum=False):
        psS, pk = self.bank("stat", [6, 7])
        n = len(srcs)
        for c, (s, sk) in enumerate(zip(srcs, src_keys)):
            sq, qk = self.sqb.next()
            self.act(sq[:], s, AF.Square, [sk], [qk])
            self.mm(psS[:], self.ones_b[:], sq[:], c == 0, c == n - 1, [qk, "ones_b"], [pk])
        ln, lk = self.rstd.next()
        self.act(ln[:], psS[:], AF.Ln, [pk, "eps_t"], [lk], scale=inv_n, bias=self.eps_t[:])
        self.act(ln[:], ln[:], AF.Exp, [lk], [lk], scale=-0.5)
        return ln, lk

    def norm_to_xnT(self, l, which):
        rstd, rk = self.stats_rstd([self.hb[:, c, :] for c in range(8)], ["hb"] * 8, 1.0 / D)
        g0 = (l * 9 + which) * 8
        for c in range(8):
            self.stt("dve", self.xnT[:, c, :], self.hb[:, c, :], self.gcols[:, g0 + c:g0 + c + 1], rstd[:], ALU.mult, ALU.mult,
                     ["hb", "gcols", rk], [("xnT", c)])

    def post_norm_add(self, l, which):
        rstd, rk = self.stats_rstd([self.ysb[:, c, :] for c in range(8)], [("ysb", c) for c in range(8)], 1.0 / D)
        g0 = (l * 9 + which) * 8
        for c in range(8):
            t, tk = self.f32s.next()
            self.stt("dve", t[:], self.ysb[:, c, :], self.gcols[:, g0 + c:g0 + c + 1], rstd[:], ALU.mult, ALU.mult, [("ysb", c), "gcols", rk], [tk])
            self.tt("pool", self.hb[:, c, :], self.hb[:, c, :], t[:], ALU.add, ["hb", tk], ["hb"])

    def ffn(self, l, f):
        gu, dn = self.W[("gu", f, l)], self.W[("dn", f, l)]
        self.norm_to_xnT(l, 0 if f == 0 else 7)
        xk = [("xnT", c) for c in range(8)]
        for jj in range(11):
            gs, gk_ = self.load_slot(gu, jj, 0)
            us, uk_ = self.load_slot(gu, 11 + jj, 0)
            for sub in range(2):
                j = 2 * jj + sub
                pg, pgk = self.bank("ffn_g", [0, 1])
                pu, puk = self.bank("ffn_u", [2, 3])
                for kc in range(8):
                    self.mm(pg[:], gs[:, kc, sub * 128:(sub + 1) * 128], self.xnT[:, kc, :], kc == 0, kc == 7, [gk_, xk[kc]], [pgk])
                for kc in range(8):
                    self.mm(pu[:], us[:, kc, sub * 128:(sub + 1) * 128], self.xnT[:, kc, :], kc == 0, kc == 7, [uk_, xk[kc]], [puk])
                s, sk = self.f32s.next()
                self.act(s[:], pg[:], AF.Silu, [pgk], [sk])
                self.tt("dve", self.hT[:, j, :], s[:], pu[:], ALU.mult, [sk, puk], [("hT", j)])
        for c in range(8):
            py, pyk = self.bank("ffn_y", [4, 5])
            for kg in range(2):
                ws, wk = self.load_slot(dn, c, kg)
                for ki in range(11):
                    kc = kg * 11 + ki
                    self.mm(py[:], ws[:, ki, :], self.hT[:, kc, :], kc == 0, kc == 21, [wk, ("hT", kc)], [pyk])
            self.copy("act" if c % 2 == 0 else "dve", self.ysb[:, c, :], py[:], [pyk], [("ysb", c)])
        self.post_norm_add(l, 1 if f == 0 else 8)

    def proj_fm(self, l, cb_list, dst_fn, rope, xk):
        win, wsw = self.W[("in", l)], self.W[("insw", l)]
        i = 0
        for cb in cb_list:
            ws, wk = self.load_slot(win, cb, 0)
            if rope:
                ss, sk_ = self.load_slot(wsw, cb - 6, 0)
            for half in range(2):
                pk_, pkk = self.bank("pj_a", [0, 1])
                for kc in range(8):
                    self.mm(pk_[:], ws[:, kc, half * 128:(half + 1) * 128], self.xnT[:, kc, :], kc == 0, kc == 7, [wk, xk[kc]], [pkk])
                dst, dk, post = dst_fn(i)
                if dst is None:
                    i += 1
                    continue
                if rope:
                    ps2, ps2k = self.bank("pj_b", [2, 3])
                    for kc in range(8):
                        self.mm(ps2[:], ss[:, kc, half * 128:(half + 1) * 128], self.xnT[:, kc, :], kc == 0, kc == 7, [sk_, xk[kc]], [ps2k])
                    t1, t1k = self.f32s.next()
                    t2, t2k = self.f32s.next()
                    self.tt("dve", t1[:], pk_[:], self.cosT[:], ALU.mult, [pkk, "cosT"], [t1k])
                    self.tt("dve", t2[:], ps2[:], self.sinT[:], ALU.mult, [ps2k, "sinT"], [t2k])
                    self.tt("pool", dst, t1[:], t2[:], ALU.add, [t1k, t2k], [dk])
                else:
                    self.copy("act" if i % 2 == 0 else "dve", dst, pk_[:], [pkk], [dk])
                if post is not None:
                    post()
                i += 1

    def stageA(self, l, blk):
        if "ffn" not in self.cfg.get("skip", ()):
            self.ffn(l, 0)
        if "kv" in self.cfg.get("skip", ()):
            return
        self.norm_to_xnT(l, 2)
        xk = [("xnT", c) for c in range(8)]
        kv = self.kvloc[l]
        t0 = blk * TB
        if not hasattr(self, "kvkeys"):
            self.kvkeys = {i: [] for i in range(L)}

        def kdst_factory(row0, only0=False):
            def fn(i):
                if only0 and i > 0:
                    return None, None, None
                t, tk = self.kst.next()
                r0 = row0 + i * 128

                def post():
                    self.dma(kv[r0:r0 + 128, t0:t0 + TB], t[:], [tk], [self.key("kvw%d" % l)], eng="pool")
                    self.kvkeys[l].append("kvw%d#%d" % (l, self.uid))
                return t[:], tk, post
            return fn

        self.proj_fm(l, [2, 3], kdst_factory(KA0), False, xk)
        self.proj_fm(l, [8], kdst_factory(KB0, True), True, xk)
        self.proj_fm(l, [11, 12], kdst_factory(KC0), True, xk)
        win = self.W[("in", l)]
        for (cbs, vrow0, ncols) in (([4, 5], VA0, 512), ([8], VB0, 128), ([13, 14], VC0, 512)):
            vt, vk = self.vst.next()
            for ci, cb in enumerate(cbs):
                ws, wk = self.load_slot(win, cb, 0)
                for tt_ in range(4):
                    pv, pvk = self.bank("pj_v", [4, 5])
                    if ncols == 128:
                        for kc in range(8):
                            self.mm(pv[:, 0:128], self.xnT[:, kc, tt_ * 128:(tt_ + 1) * 128], ws[:, kc, 128:256], kc == 0, kc == 7, [wk, xk[kc]], [pvk])
                        self.copy("act" if tt_ % 2 == 0 else "dve", vt[:, tt_, 0:128], pv[:, 0:128], [pvk], [vk])
                    else:
                        for kc in range(8):
                            self.mm(pv[:, 0:256], self.xnT[:, kc, tt_ * 128:(tt_ + 1) * 128], ws[:, kc, :], kc == 0, kc == 7, [wk, xk[kc]], [pvk])
                        self.copy("act" if tt_ % 2 == 0 else "dve", vt[:, tt_, ci * 256:(ci + 1) * 256], pv[:, 0:256], [pvk], [vk])
            nrows = ncols * NT // NT
            view = kv[vrow0:vrow0 + ncols, :].rearrange("r (q c) -> (r q) c", c=ncols)
            dstv = view[t0:t0 + TB, :].rearrange("(t p) c -> p t c", p=128)
            self.dma(dstv, vt[:, :, 0:ncols], [vk], [self.key("kvw%d" % l)], eng="pool")
            self.kvkeys[l].append("kvw%d#%d" % (l, self.uid))

    def allgather(self, l):
        if NRANK == 1:
            return
        if self.cfg.get("no_ag"):
            self.dma(self.kvall[l][0:KV_ROWS, :], self.kvloc[l], list(self.kvkeys[l]), [("kvall", l)], eng="pool")
            return
        self.P.op("pool", lambda e: e.collective_compute("AllGather", ALU.bypass, replica_groups=[[0, 1], [2, 3], [4, 5], [6, 7]],
                                                         ins=[self.kvloc[l]], outs=[self.kvall[l]]),
                  reads=list(self.kvkeys[l]), writes=[("kvall", l)], dma=True)

    def setup_layer_tables(self, l):
        pass

    def kv_rows_tok(self, l, row0, nrows, tok_lo, tok_hi):
        out = []
        for r in range(2):
            lo, hi = max(tok_lo, r * NT), min(tok_hi, (r + 1) * NT)
            if lo < hi:
                out.append((self.kvall[l][r * KV_ROWS + row0:r * KV_ROWS + row0 + nrows, lo - r * NT:hi - r * NT], lo - tok_lo, hi - lo))
        return out

    def v_view(self, l, r, vrow0, ncols):
        return self.kvall[l][r * KV_ROWS + vrow0:r * KV_ROWS + vrow0 + ncols, :].rearrange("r (q c) -> (r q) c", c=ncols)

    def na_valid(self, half, R0, krl, j):
        if NRANK == 1:
            half = 0
        qr = R0 + j + (NT // 64) * half
        kr = krl + (NT // 64) * half
        ws = min(max(qr - 4, 0), 120)
        return (0 <= kr <= 127) and (ws <= kr < ws + 8)

    def attn_na(self, l, blk):
        R0 = 8 * blk
        NR_ = NT // 64
        own_lo, own_hi = max(R0 - 4, 0), min(R0 + 12, NR_)
        kvl = self.kvloc[l]
        pieces = []
        if R0 - 4 < 0 and NRANK == 2:
            pieces.append(("prev", R0 - 4, 0))
        pieces.append(("own", own_lo, own_hi))
        if R0 + 12 > NR_ and NRANK == 2:
            pieces.append(("next", NR_, R0 + 12))
        for kind, lo, hi in pieces:
            s0 = lo - (R0 - 4)
            n = (hi - lo) * 64
            if kind == "own":
                ksrc = kvl[KA0:KA0 + 512, lo * 64:hi * 64]
                vsrc = kvl[VA0:VA0 + 512, :].rearrange("r (q c) -> (r q) c", c=512)[lo * 64:hi * 64, :]
                rk = list(self.kvkeys[l])
            elif kind == "prev":
                ksrc = self.kvall[l][KA0:KA0 + 512, NT + lo * 64:NT + hi * 64]
                vsrc = self.v_view(l, 0, VA0, 512)[NT + lo * 64:NT + hi * 64, :]
                rk = [("kvall", l)]
            else:
                ksrc = self.kvall[l][KV_ROWS + KA0:KV_ROWS + KA0 + 512, (lo - NR_) * 64:(hi - NR_) * 64]
                vsrc = self.v_view(l, 1, VA0, 512)[(lo - NR_) * 64:(hi - NR_) * 64, :]
                rk = [("kvall", l)]
            self.dma(self.kTA[:, :, s0 * 64:s0 * 64 + n], ksrc.rearrange("(c p) t -> p c t", p=128), rk, ["kTA"])
            for ph in range(2):
                self.dma(self.vA[ph * 64:(ph + 1) * 64, s0:s0 + (hi - lo), :], vsrc.rearrange("(r p) c -> p r c", p=64), rk, ["vA"])
        segs = []
        for s_ in range(16):
            krl = R0 - 4 + s_
            cats = [(self.na_valid(0, R0, krl, j), self.na_valid(1, R0, krl, j)) for j in range(8)]
            j = 0
            while j < 8:
                if cats[j] == (False, False):
                    j += 1
                    continue
                j1 = j
                while j1 + 1 < 8 and cats[j1 + 1] == cats[j]:
                    j1 += 1
                segs.append((s_, krl, j, j1, cats[j]))
                j = j1 + 1
        for ch in range(4):
            eb, ebk = self.ebh.next()
            for ph in range(2):
                self.dma(eb[ph * 64:(ph + 1) * 64, :], self.ebt[l][2 * ch + ph], [("ebt", l, 2 * ch + ph)], [ebk])
            ebv = eb[:].rearrange("p (i c) -> p i c", c=64)
            pO = [self.ps[2], self.ps[3]]
            pOk = ["ps2", "ps3"]
            pR = [self.ps[4], self.ps[5]]
            pRk = ["ps4", "ps5"]

            def na_S(si, ch=ch):
                s_, krl, j0, j1, cat = segs[si]
                nq = (j1 - j0 + 1) * 64
                pS, pSk = self.bank("att_s", [0, 1])
                for ph in range(2):
                    b = ph * 64
                    self.mm(pS[b:b + 64, 0:nq], self.kTA[b:b + 64, ch, s_ * 64:(s_ + 1) * 64], self.qT[b:b + 64, ch, j0 * 64:(j1 + 1) * 64],
                            True, True, ["kTA", ("qT", ch)], [pSk])
                return pS, pSk

            cur = na_S(0)
            for si, (s_, krl, j0, j1, cat) in enumerate(segs):
                nxt = na_S(si + 1) if si + 1 < len(segs) else None
                pS, pSk = cur
                nq = (j1 - j0 + 1) * 64
                c0, c1 = j0 * 64, (j1 + 1) * 64
                e, ek = self.bfs.next()
                self.act(e[:, 0:nq], pS[:, 0:nq], AF.Exp, [pSk], [ek], scale=0.125)
                pm, pmk = self.pm.next()
                idx0 = 7 - krl + R0 + j0
                ev = e[:, 0:nq].rearrange("p (i c) -> p i c", c=64)
                pv = pm[:, 0:nq].rearrange("p (i c) -> p i c", c=64)
                tb = ebv[:, idx0:idx0 + (j1 - j0 + 1), :]
                if cat == (True, True):
                    self.tt("dve", pv, ev, tb, ALU.mult, [ek, ebk], [pmk])
                else:
                    fl = self.flags[:, 1:2] if cat == (True, False) else self.flags[:, 0:1]
                    self.stt("dve", pv, ev, fl, tb, ALU.mult, ALU.mult, [ek, ebk, "flags"], [pmk])
                last = si == len(segs) - 1
                for ph in range(2):
                    b = ph * 64
                    h = 2 * ch + ph
                    self.mm(pO[ph][0:64, c0:c1], self.vA[b:b + 64, s_, h * 64:(h + 1) * 64], pm[b:b + 64, 0:nq], si == 0, last, ["vA", pmk], [pOk[ph]], skip_group_check=True)
                    self.mm(pR[ph][0:64, c0:c1], self.ones_b[b:b + 64, 0:64], pm[b:b + 64, 0:nq], si == 0, last, ["ones_b", pmk], [pRk[ph]], skip_group_check=True)
                cur = nxt
            for ph in range(2):
                b = ph * 64
                r, rk_ = self.f32s.next()
                self.recip(r[0:64, :], pR[ph][0:64, :], [pRk[ph]], [rk_])
                self.tt("dve", self.oT[0][b:b + 64, ch, :], pO[ph][0:64, :], r[0:64, :], ALU.mult, [pOk[ph], rk_], [("oT0", ch, ph)])

    def attn_sw(self, l, blk):
        kvl = self.kvloc[l]
        T0 = blk * 4 - 1
        tiles = []
        for s_ in range(6):
            T = T0 + s_
            kind = "prev" if T < 0 else ("next" if T >= NT // 128 else "own")
            if NRANK == 1 and kind != "own":
                kind = "none"
            tiles.append(kind)
        own_s = [s_ for s_ in range(6) if tiles[s_] == "own"]
        lo, hi = T0 + own_s[0], T0 + own_s[-1] + 1
        vview = kvl[VB0:VB0 + 128, :].rearrange("r (q c) -> (r q) c", c=128)
        rk = list(self.kvkeys[l])
        for kvh in range(2):
            for ph in range(2):
                self.dma(self.kTB[ph * 64:(ph + 1) * 64, kvh, own_s[0] * 128:(own_s[-1] + 1) * 128], kvl[KB0 + kvh * 64:KB0 + (kvh + 1) * 64, lo * 128:hi * 128], rk, ["kTB"])
        self.dma(self.vB[:, own_s[0]:own_s[-1] + 1, :], vview[lo * 128:hi * 128, :].rearrange("(t p) c -> p t c", p=128), rk, ["vB"])
        for s_ in range(6):
            if tiles[s_] in ("own", "none"):
                continue
            r = 0 if tiles[s_] == "prev" else 1
            t_lo = NT - 128 if r == 0 else 0
            for kvh in range(2):
                for ph in range(2):
                    self.dma(self.kTB[ph * 64:(ph + 1) * 64, kvh, s_ * 128:(s_ + 1) * 128],
                             self.kvall[l][r * KV_ROWS + KB0 + kvh * 64:r * KV_ROWS + KB0 + (kvh + 1) * 64, t_lo:t_lo + 128], [("kvall", l)], ["kTB"])
            self.dma(self.vB[:, s_, :], self.v_view(l, r, VB0, 128)[t_lo:t_lo + 128, :], [("kvall", l)], ["vB"])
        for h in range(8):
            ch, b, kvh = h // 2, (h % 2) * 64, h // 4
            pO, pOk = self.bank("att_o", [2, 3])
            pR, pRk = self.bank("att_r", [4, 5])
            proc = [s_ for s_ in range(6) if tiles[s_] != "none"]

            def sw_S(pi, ch=ch, b=b, kvh=kvh):
                s_ = proc[pi]
                pS, pSk = self.bank("att_s", [0, 1])
                self.mm(pS, self.kTB[b:b + 64, kvh, s_ * 128:(s_ + 1) * 128], self.qT[b:b + 64, ch, :], True, True, ["kTB", ("qT", ch)], [pSk])
                return pS, pSk

            cur = sw_S(0)
            for pi, s_ in enumerate(proc):
                nxt = sw_S(pi + 1) if pi + 1 < len(proc) else None
                pS, pSk = cur
                e, ek = self.bfs.next()
                self.act(e[:], pS, AF.Exp, [pSk], [ek], scale=0.125)
                pm, pmk = self.pm.next()
                off = 512 - (s_ - 1) * 128
                if tiles[s_] == "own":
                    self.tt("dve", pm[:], e[:], self.strip[:, off:off + 512], ALU.mult, [ek, "strip"], [pmk])
                else:
                    fl = self.flags[:, 0:1] if tiles[s_] == "prev" else self.flags[:, 1:2]
                    self.stt("dve", pm[:], e[:], fl, self.strip[:, off:off + 512], ALU.mult, ALU.mult, [ek, "strip", "flags"], [pmk])
                self.mm(pO[0:64, :], self.vB[:, s_, kvh * 64:(kvh + 1) * 64], pm[:], s_ == proc[0], s_ == proc[-1], ["vB", pmk], [pOk])
                self.mm(pR[0:64, :], self.ones_b[:, 0:64], pm[:], s_ == proc[0], s_ == proc[-1], ["ones_b", pmk], [pRk])
                cur = nxt
            r, rk_ = self.f32s.next()
            self.ts("dve", r[0:64, :], pR[0:64, :], self.esink[0:64, l * 8 + h:l * 8 + h + 1], None, ALU.add, None, [pRk, "esink"], [rk_])
            self.recip(r[0:64, :], r[0:64, :], [rk_], [rk_])
            self.tt("dve", self.oT[1][b:b + 64, ch, :], pO[0:64, :], r[0:64, :], ALU.mult, [pOk, rk_], [("oT1", ch, h % 2)])

    def attn_diff(self, l, blk):
        npr = NT // 1024
        nchunks = S // 1024
        for h in range(4):
            pO = [self.ps[4], self.ps[5]]
            pOk = ["ps4", "ps5"]
            acc = self.dacc
            self.P.op("pool", lambda e: e.memset(self.dacc[:, 0:TB], 0.0), [], ["dacc"])
            pR1, pR1k = self.bank("stat", [6, 7])
            chunks = {}

            def load_chunk(kc8, h=h):
                r, tl = kc8 // npr, (kc8 % npr) * 1024
                kt_, ktk = self.kTC.next()
                vt_, vtk = self.vC.next()
                if NRANK == 1:
                    kvl_ = self.kvloc[l]
                    rk_ = list(self.kvkeys[l])
                    self.dma(kt_[:], kvl_[KC0 + h * 128:KC0 + (h + 1) * 128, tl:tl + 1024], rk_, [ktk])
                    vv_ = kvl_[VC0:VC0 + 512, :].rearrange("r (q c) -> (r q) c", c=512)
                    self.dma(vt_[:], vv_[tl:tl + 1024, h * 128:(h + 1) * 128].rearrange("(t p) c -> p t c", p=128), rk_, [vtk])
                else:
                    self.dma(kt_[:], self.kvall[l][r * KV_ROWS + KC0 + h * 128:r * KV_ROWS + KC0 + (h + 1) * 128, tl:tl + 1024], [("kvall", l)], [ktk])
                    self.dma(vt_[:], self.v_view(l, r, VC0, 512)[tl:tl + 1024, h * 128:(h + 1) * 128].rearrange("(t p) c -> p t c", p=128), [("kvall", l)], [vtk])
                chunks[kc8] = (kt_, ktk, vt_, vtk)

            items = [(kc8, kt) for kc8 in range(nchunks) for kt in range(8)]

            def emit_S(idx, h=h):
                kc8, kt = items[idx]
                if kt == 0:
                    load_chunk(kc8)
                kt_, ktk, vt_, vtk = chunks[kc8]
                pS2, pS2k = self.ps2[self.ps2_i % 2]
                self.ps2_i += 1
                for t in range(2):
                    self.mm(pS2[:, t * 512:(t + 1) * 512], kt_[t * 64:(t + 1) * 64, kt * 128:(kt + 1) * 128], self.qT[t * 64:(t + 1) * 64, h, :],
                            True, True, [ktk, ("qT", h)], [pS2k])
                return pS2, pS2k

            cur = emit_S(0)
            n_it = len(items)
            for idx in range(n_it):
                nxt = emit_S(idx + 1) if idx + 1 < n_it else None
                kc8, kt = items[idx]
                kt_, ktk, vt_, vtk = chunks[kc8]
                pS2, pS2k = cur
                e, ek = self.e2.next()
                self.act(e[:], pS2, AF.Exp, [pS2k], [ek], scale=0.125)
                self.tt("dve", acc[:, 0:TB], acc[:, 0:TB], e[:, 0:TB], ALU.add, ["dacc", ek], ["dacc"])
                for t in range(2):
                    self.mm(pO[t], vt_[:, kt, :], e[:, t * 512:(t + 1) * 512], idx == 0, idx == n_it - 1, [vtk, ek], [pOk[t]])
                self.mm(pR1, self.ones_b[:], e[:, TB:2 * TB], idx == 0, idx == n_it - 1, ["ones_b", ek], [pR1k])
                cur = nxt
            rs = []
            for t in range(2):
                if t == 0:
                    pR, pRk = self.bank("stat", [6, 7])
                    self.mm(pR, self.ones_f[:], acc[:, 0:TB], True, True, ["ones_f", "dacc"], [pRk])
                else:
                    pR, pRk = pR1, pR1k
                r_, rk_ = self.dfb.next()
                self.recip(r_[:], pR, [pRk], [rk_])
                rs.append((r_, rk_))
            o0, o0k = self.dfb.next()
            self.tt("dve", o0[:], pO[0], rs[0][0][:], ALU.mult, [pOk[0], rs[0][1]], [o0k])
            o1, o1k = self.dfb.next()
            self.tt("dve", o1[:], pO[1], rs[1][0][:], ALU.mult, [pOk[1], rs[1][1]], [o1k])
            self.stt("dve", o0[:], o1[:], self.nlam[:, l:l + 1], o0[:], ALU.mult, ALU.add, [o1k, o0k, "nlam"], [o0k])
            rstd, rk2 = self.stats_rstd([o0[:]], [o0k], 1.0 / 128)
            self.stt("dve", self.oT[2][:, h, :], o0[:], self.gsub[:, l:l + 1], rstd[:], ALU.mult, ALU.mult, [o0k, "gsub", rk2], [("oT2", h, 0), ("oT2", h, 1)])

    def merge_out(self, l):
        win, wo = self.W[("in", l)], self.W[("out", l)]
        xk = [("xnT", c) for c in range(8)]
        for cp in range(4):
            for i in range(3):
                gs, gk_ = self.load_slot(win, 15 + i * 4 + cp, 0)
                bs, bk_ = self.load_slot(self.W[("br", l, i)], cp, 0)
                for sub in range(2):
                    c = 2 * cp + sub
                    pY, pYk = self.bank("mg_y", [0, 1])
                    pG, pGk = self.bank("mg_g", [2, 3])
                    for kc in range(4):
                        self.mm(pY[:], bs[:, kc, sub * 128:(sub + 1) * 128], self.oT[i][:, kc, :], kc == 0, kc == 3, [bk_, ("oT%d" % i, kc, 0), ("oT%d" % i, kc, 1)], [pYk])
                    for kc in range(8):
                        self.mm(pG[:], gs[:, kc, sub * 128:(sub + 1) * 128], self.xnT[:, kc, :], kc == 0, kc == 7, [gk_, xk[kc]], [pGk])
                    g, gk2 = self.f32s.next()
                    col = (l * 3 + i) * 8 + c
                    self.act(g[:], pG[:], AF.Exp, [pGk, "nbg"], [gk2], scale=-1.0, bias=self.nbg[:, col:col + 1])
                    self.act(g[:], g[:], AF.Ln, [gk2, "one_t"], [gk2], scale=1.0, bias=self.one_t[:])
                    self.act(g[:], g[:], AF.Exp, [gk2], [gk2], scale=-1.0)
                    if i == 0:
                        self.tt("dve", self.macc[:, sub, :], pY[:], g[:], ALU.mult, [pYk, gk2], [("macc", sub)])
                    elif i == 1:
                        self.tt("dve", g[:], pY[:], g[:], ALU.mult, [pYk, gk2], [gk2])
                        self.tt("dve", self.macc[:, sub, :], self.macc[:, sub, :], g[:], ALU.add, [("macc", sub), gk2], [("macc", sub)])
                    else:
                        self.tt("dve", g[:], pY[:], g[:], ALU.mult, [pYk, gk2], [gk2])
                        self.tt("dve", self.mT[:, c, :], self.macc[:, sub, :], g[:], ALU.add, [("macc", sub), gk2], [("mT", c)])
        for cb in range(4):
            ws, wk = self.load_slot(wo, cb, 0)
            for sub in range(2):
                c = 2 * cb + sub
                pM, pMk = self.bank("ffn_y", [4, 5])
                for kc in range(8):
                    self.mm(pM[:], ws[:, kc, sub * 128:(sub + 1) * 128], self.mT[:, kc, :], kc == 0, kc == 7, [wk, ("mT", kc)], [pMk])
                self.copy("act" if c % 2 == 0 else "dve", self.ysb[:, c, :], pM[:], [pMk], [("ysb", c)])
        self.post_norm_add(l, 6)

    def stageB(self, l, blk):
        cfg = self.cfg
        self.norm_to_xnT(l, 2)
        xk = [("xnT", c) for c in range(8)]
        parts = cfg.get("parts", "ABC")
        qdst = lambda i: (self.qT[:, i, :], ("qT", i), None)
        skip = cfg.get("skip", ())
        self.proj_fm(l, [0, 1], qdst, False, xk)
        if "na" not in skip:
            self.attn_na(l, blk)
        self.proj_fm(l, [6, 7], qdst, True, xk)
        if "sw" not in skip:
            self.attn_sw(l, blk)
        self.proj_fm(l, [9, 10], qdst, True, xk)
        if "diff" not in skip:
            self.attn_diff(l, blk)
        if cfg.get("dbg_o"):
            if not hasattr(self, "dbg_o"):
                self.dbg_o = [self.dram("dbg_o%d" % i, [512, NT], BF16, "ExternalOutput") for i in range(3)]
            for i in range(3):
                k_ = self.key("dbgo")
                self.dma(self.dbg_o[i].rearrange("(c p) t -> p c t", p=128)[:, :, blk * TB:(blk + 1) * TB], self.oT[i][:],
                         [("oT%d" % i, c, hh) for c in range(4) for hh in range(2)], [k_], eng="pool")
                self.outs.append(k_)
        if "merge" not in skip:
            self.merge_out(l)
        if "ffn" not in skip:
            self.ffn(l, 1)


def build_program(cfg):
    kb = KB(cfg)
    with kb.st:
        nc = kb.build()
    return nc


def _consts():
    pos = np.arange(S, dtype=np.float32)
    inv = (np.float32(10000.0) ** (-(np.arange(0, 64, 2, dtype=np.float32) / np.float32(64)))).astype(np.float32)
    ang = (pos[:, None] * inv[None, :]).astype(np.float32)
    ang = np.concatenate([ang, ang], axis=-1)
    cos = np.cos(ang).astype(np.float32)
    sin = np.sin(ang).astype(np.float32)
    sgn = np.concatenate([-np.ones(32, np.float32), np.ones(32, np.float32)])
    cosT = np.ascontiguousarray(np.concatenate([cos.T, cos.T], axis=0))
    sinT = np.ascontiguousarray(np.concatenate([(sin * sgn).T, (sin * sgn).T], axis=0))
    c = np.arange(64)
    col_start = np.clip(c - 8, 0, 48)
    col_ok = (c[None, :] >= col_start[:, None]) & (c[None, :] < col_start[:, None] + 16)
    cmask = np.ascontiguousarray(col_ok.T.astype(np.float32))
    dc = np.clip(c[:, None] - c[None, :] + 15, 0, 30)
    kl = np.arange(128)[:, None]
    xx = np.arange(1152)[None, :]
    strip = (np.abs(xx - 512 - kl) <= 128).astype(np.float32)
    sel = np.zeros((128, 4), np.float32)
    sel[:, 0] = 1.0
    sel[:, 3] = 1.0
    selb = np.zeros((2, 256), np.float32)
    selb[0, 0:128] = 1.0
    selb[1, 128:256] = 1.0
    return cosT, sinT, cmask, dc, strip, sel, selb


def _host_inputs(inputs):
    cosT, sinT, cmask, dc, strip, sel, selb = _consts()
    f = lambda a: np.ascontiguousarray(np.asarray(a, dtype=np.float32))
    gnames = {0: "ffn1_pre_g", 1: "ffn1_post_g", 2: "mix_pre_g", 6: "mix_post_g", 7: "ffn2_pre_g", 8: "ffn2_post_g"}
    gcols = np.zeros((128, 9 * L * 8), np.float32)
    for l in range(L):
        for which, nm in gnames.items():
            g = f(inputs[nm])[l]
            gcols[:, (l * 9 + which) * 8:(l * 9 + which) * 8 + 8] = g.reshape(8, 128).T
    bg = f(inputs["b_gate"])
    bgate = np.zeros((128, L * 3 * 8), np.float32)
    for l in range(L):
        for i in range(3):
            bgate[:, (l * 3 + i) * 8:(l * 3 + i) * 8 + 8] = bg[l, i].reshape(8, 128).T
    gsub = np.ascontiguousarray(f(inputs["diff_subln_g"]).T)
    sink = f(inputs["sw_sink"]).reshape(-1)
    lamv = np.concatenate([f(inputs[k]).reshape(-1) for k in ("diff_lambda_q1", "diff_lambda_k1", "diff_lambda_q2", "diff_lambda_k2")])
    rpb = f(inputs["na_rpb"])
    idx = np.arange(15)
    rpbT = rpb[:, :, (14 - idx)[None, :, None], dc[:, None, :]]
    rpbT = np.ascontiguousarray(rpbT.reshape(L, 8, 64, 15 * 64))
    shared = {
        "ffn1_w_gu": f(inputs["ffn1_w_gu"]), "ffn2_w_gu": f(inputs["ffn2_w_gu"]),
        "ffn1_w_down": f(inputs["ffn1_w_down"]), "ffn2_w_down": f(inputs["ffn2_w_down"]),
        "w_in": f(inputs["w_in"]), "w_branch": f(inputs["w_branch"]), "w_out": f(inputs["w_out"]),
        "gcols": gcols, "bgate": bgate, "gsub": gsub, "sink": sink, "lamv": lamv, "rpbT": rpbT,
        "cmask": cmask, "strip": strip, "sel": sel, "selb": selb,
    }
    x = f(inputs["x"])
    in_maps = []
    xts = {}
    for c in range(8):
        b, half = c // 2, (c % 2 if NRANK == 2 else 0)
        m = dict(shared)
        if (b, half) not in xts:
            xts[(b, half)] = np.ascontiguousarray(x[b, half * NT:(half + 1) * NT, :].T)
        m["xT"] = xts[(b, half)]
        m["cosT"] = np.ascontiguousarray(cosT[:, half * NT:(half + 1) * NT])
        m["sinT"] = np.ascontiguousarray(sinT[:, half * NT:(half + 1) * NT])
        fl = np.zeros((128, 2), np.float32)
        if NRANK == 2:
            fl[:, 0] = half
            fl[:, 1] = 1 - half
        m["flags"] = fl
        in_maps.append(m)
    return in_maps


_NC_CACHE = {}
FUSED = True


def _get_nc(cfg_key, cfg):
    if cfg_key not in _NC_CACHE:
        _NC_CACHE[cfg_key] = build_program(cfg)
    return _NC_CACHE[cfg_key]


def _run(nc, in_maps, names):
    maps = [{k: m[k] for k in names if k in m} for m in in_maps]
    res = run_bass_kernel_spmd(nc, maps, core_ids=list(range(len(maps))))
    return res.results


WNAMES = ["ffn1_w_gu", "ffn2_w_gu", "ffn1_w_down", "ffn2_w_down", "w_in", "w_branch", "w_out", "gcols", "bgate", "gsub", "sink",
          "lamv", "rpbT", "cmask", "strip", "sel", "selb", "cosT", "sinT", "flags"]


def _pair_gather(results, name):
    out = []
    for c in range(len(results)):
        p = (c // 2) * 2
        out.append(np.concatenate([np.asarray(results[p][name]), np.asarray(results[p + 1][name])], axis=0))
    return out


def kernel_split(in_maps, cfg_extra=None):
    ce = cfg_extra or {}
    n = len(in_maps)
    r1 = _run(_get_nc("l1", dict(ce, launch=1)), in_maps, WNAMES + ["xT"])
    kvall = _pair_gather(r1, "kvloc_out")
    for c in range(n):
        in_maps[c]["h_in"] = np.asarray(r1[c]["h_out"])
        in_maps[c]["kvall_in"] = kvall[c]
        in_maps[c]["kvloc_in"] = np.asarray(r1[c]["kvloc_out"])
    del r1
    r2 = _run(_get_nc("l2", dict(ce, launch=2)), in_maps, WNAMES + ["h_in", "kvall_in", "kvloc_in"])
    kvall = _pair_gather(r2, "kvloc_out")
    for c in range(n):
        in_maps[c]["h_in"] = np.asarray(r2[c]["h_out"])
        in_maps[c]["kvall_in"] = kvall[c]
        in_maps[c]["kvloc_in"] = np.asarray(r2[c]["kvloc_out"])
    del r2
    r3 = _run(_get_nc("l3", dict(ce, launch=3)), in_maps, WNAMES + ["h_in", "kvall_in", "kvloc_in"])
    return [np.asarray(r3[c]["outT"]) for c in range(n)]


def kernel(**inputs):
    if FUSED:
        set_mode(1)
        in_maps = _host_inputs(inputs)
        res = _run(_get_nc("fused", {}), in_maps, WNAMES + ["xT"])
        out = np.empty((4, S, D), np.float32)
        for b in range(4):
            out[b] = np.asarray(res[2 * b]["outT"]).T
        return out
    set_mode(2)
    in_maps = _host_inputs(inputs)
    outs = kernel_split(in_maps)
    out = np.empty((4, S, D), np.float32)
    for c in range(8):
        b, half = c // 2, c % 2
        out[b, half * NT:(half + 1) * NT, :] = outs[c].T
    return out
```

```python
import contextlib
import numpy as np
import concourse.bass as bass
import concourse.mybir as mybir
from concourse.bass_utils import run_bass_kernel_spmd

F32 = mybir.dt.float32
BF16 = mybir.dt.bfloat16
AF = mybir.ActivationFunctionType
ALU = mybir.AluOpType

D = 1024
FF = 2816
S = 8192
NT = 4096
TB = 512
NBLK = NT // TB
NRANK = 2


ROT = False


def set_mode(nrank, rot=False):
    global NT, NBLK, NRANK, ROT
    ROT = rot
    NRANK = nrank
    NT = S // nrank
    NBLK = NT // TB
L = 2
EPS = 1e-6
KV_ROWS = 2304
KA0, KB0, KC0, VA0, VB0, VC0 = 0, 512, 640, 1152, 1664, 1792

N_DMA_SEMS = 16
LOOKAHEAD = 2
SAME_ENGINE_SYNC = True


class Op:
    __slots__ = ("eng", "emit", "deps", "is_dma", "marked", "semval", "dsem", "dval")

    def __init__(self, eng, emit, is_dma):
        self.eng = eng
        self.emit = emit
        self.deps = []
        self.is_dma = is_dma
        self.marked = False
        self.semval = 0
        self.dsem = None
        self.dval = 0


class Prog:
    ENGS = ("pe", "act", "dve", "pool", "sp")

    def __init__(self, nc):
        self.nc = nc
        self.ops = []
        self.last_writer = {}
        self.readers = {}

    def op(self, eng, emit, reads=(), writes=(), dma=False):
        o = Op(eng, emit, dma)
        deps = []
        reads = _expand(reads)
        writes = _expand(writes)
        for b in reads:
            w = self.last_writer.get(b)
            if w is not None:
                deps.append(w)
        for b in writes:
            w = self.last_writer.get(b)
            if w is not None:
                deps.append(w)
            rs = self.readers.get(b)
            if rs:
                deps.extend(rs)
        for b in reads:
            self.readers.setdefault(b, []).append(o)
        for b in writes:
            self.last_writer[b] = o
            self.readers[b] = []
        seen = set()
        for d in deps:
            if id(d) not in seen and d is not o:
                seen.add(id(d))
                o.deps.append(d)
        self.ops.append(o)
        return o

    def finalize(self, stack):
        nc = self.nc
        per_eng = {e: [] for e in self.ENGS}
        for o in self.ops:
            per_eng[o.eng].append(o)
        for o in self.ops:
            for d in o.deps:
                if d.is_dma:
                    continue
                if d.eng == o.eng and (o.eng == "pe" or not SAME_ENGINE_SYNC) and not o.is_dma:
                    continue
                d.marked = True
        self.csem = {e: stack.enter_context(nc.semaphore("c_" + e)) for e in ("pe", "act", "dve", "pool")}
        dsems = {}
        for e in self.ENGS:
            ndma = sum(1 for o in per_eng[e] if o.is_dma)
            if ndma:
                dsems[e] = [stack.enter_context(nc.semaphore("d_%s_%d" % (e, i))) for i in range(min(ndma, N_DMA_SEMS))]
        for e in self.ENGS:
            cnt = 0
            j = 0
            for o in per_eng[e]:
                if o.is_dma:
                    o.dsem = dsems[e][j % N_DMA_SEMS]
                    o.dval = 16 * (j // N_DMA_SEMS + 1)
                    j += 1
                elif o.marked:
                    cnt += 1
                    o.semval = cnt
        self.per_eng = per_eng

    def emit_engine(self, ename, eng):
        waited = {}

        def wait(sem, val):
            key = id(sem)
            if waited.get(key, 0) >= val:
                return
            waited[key] = val
            eng.wait_ge(sem, val)

        for o in self.per_eng[ename]:
            for d in o.deps:
                if d.is_dma:
                    wait(d.dsem, d.dval)
                else:
                    if not d.marked:
                        continue
                    if d.eng == ename and not o.is_dma and (ename == "pe" or not SAME_ENGINE_SYNC):
                        continue
                    wait(self.csem[d.eng], d.semval)
            if o.is_dma and o.dval > 16:
                wait(o.dsem, o.dval - 16)
            if o.emit is None:
                continue
            ins = o.emit(eng)
            if o.is_dma:
                ins.then_inc(o.dsem, 16)
            elif o.marked:
                ins.then_inc(self.csem[ename], 1)

    def run_block(self):
        nc = self.nc
        with nc.Block() as block:
            @block.tensor
            def _(e):
                self.emit_engine("pe", e)

            @block.scalar
            def _(e):
                self.emit_engine("act", e)

            @block.vector
            def _(e):
                self.emit_engine("dve", e)

            @block.gpsimd
            def _(e):
                self.emit_engine("pool", e)

            @block.sync
            def _(e):
                self.emit_engine("sp", e)


class KS(list):
    pass


def _expand(keys):
    out = []
    for k in keys:
        if isinstance(k, KS):
            out.extend(k)
        else:
            out.append(k)
    return out


class AliasRot:
    def __init__(self, views, keysets):
        self.bufs, self.keys, self.i = views, keysets, 0

    def next(self):
        b, k = self.bufs[self.i], self.keys[self.i]
        self.i = (self.i + 1) % len(self.bufs)
        return b, k


class Rot:
    def __init__(self, K, name, n, shape, dt):
        self.bufs = [K.T("%s%d" % (name, i), shape, dt) for i in range(n)]
        self.keys = ["%s%d" % (name, i) for i in range(n)]
        self.i = 0

    def next(self):
        b, k = self.bufs[self.i], self.keys[self.i]
        self.i = (self.i + 1) % len(self.bufs)
        return b, k


class WSpec:
    def __init__(self, name, src2d, K, N, w, gk, swap=False):
        self.name, self.src, self.K, self.N, self.w, self.gk, self.swap = name, src2d, K, N, w, gk, swap
        self.ncb = N // w
        self.nkg = K // (128 * gk)
        assert self.ncb * w == N and self.nkg * 128 * gk == K


class KB:
    def __init__(self, cfg):
        self.cfg = cfg
        self.nc = bass.Bass("TRN2", target_bir_lowering=False)
        self.st = contextlib.ExitStack()
        self.P = Prog(self.nc)
        self.uid = 0
        self.bank_i = {}
        self.outs = []

    def T(self, name, shape, dt=F32):
        return self.st.enter_context(self.nc.sbuf_tensor(name, shape, dt))

    def dram(self, name, shape, dt, kind="Internal"):
        return self.nc.dram_tensor(name, shape, dt, kind=kind).ap()

    def key(self, p):
        self.uid += 1
        return "%s#%d" % (p, self.uid)

    def dma(self, out, in_, reads, writes, eng="sp"):
        return self.P.op(eng, lambda e: e.dma_start(out=out, in_=in_), reads, writes, dma=True)

    def mm(self, out, lhsT, rhs, start, stop, reads, writes, **kw):
        return self.P.op("pe", lambda e: e.matmul(out, lhsT=lhsT, rhs=rhs, start=start, stop=stop, **kw), reads, writes)

    def act(self, out, in_, func, reads, writes, **kw):
        return self.P.op("act", lambda e: e.activation(out=out, in_=in_, func=func, **kw), reads, writes)

    def tt(self, eng, out, in0, in1, op, reads, writes):
        return self.P.op(eng, lambda e: e.tensor_tensor(out=out, in0=in0, in1=in1, op=op), reads, writes)

    def ts(self, eng, out, in0, s1, s2, op0, op1, reads, writes):
        if op1 is None:
            return self.P.op(eng, lambda e: e.tensor_scalar(out=out, in0=in0, scalar1=s1, scalar2=None, op0=op0), reads, writes)
        return self.P.op(eng, lambda e: e.tensor_scalar(out=out, in0=in0, scalar1=s1, scalar2=s2, op0=op0, op1=op1), reads, writes)

    def stt(self, eng, out, in0, scalar, in1, op0, op1, reads, writes):
        return self.P.op(eng, lambda e: e.scalar_tensor_tensor(out=out, in0=in0, scalar=scalar, in1=in1, op0=op0, op1=op1), reads, writes)

    def copy(self, eng, out, in_, reads, writes):
        if eng == "act":
            return self.act(out, in_, AF.Copy, reads, writes)
        return self.P.op(eng, lambda e: e.tensor_copy(out=out, in_=in_), reads, writes)

    def recip(self, out, in_, reads, writes):
        return self.P.op("dve", lambda e: e.reciprocal(out=out, in_=in_), reads, writes)

    def bank(self, role, choices):
        i = self.bank_i.get(role, 0)
        self.bank_i[role] = i + 1
        b = choices[i % len(choices)]
        return self.ps[b], "ps%d" % b

    def build(self):
        nc, cfg = self.nc, self.cfg
        IN = lambda n, s, dt=F32: self.dram(n, s, dt, "ExternalInput")
        if cfg.get("launch", 0) in (0, 1):
            self.xT = IN("xT", [D, NT])
        self.w_gu = [IN("ffn1_w_gu", [L, D, 2 * FF]), IN("ffn2_w_gu", [L, D, 2 * FF])]
        self.w_dn = [IN("ffn1_w_down", [L, FF, D]), IN("ffn2_w_down", [L, FF, D])]
        self.w_in = IN("w_in", [L, D, 6912])
        self.w_br = IN("w_branch", [L, 3, 512, D])
        self.w_o = IN("w_out", [L, D, D])
        self.gcols_d = IN("gcols", [128, 9 * L * 8])
        self.bgate_d = IN("bgate", [128, L * 3 * 8])
        self.gsub_d = IN("gsub", [128, L])
        self.sink_d = IN("sink", [L * 8])
        self.lamv_d = IN("lamv", [4 * L * 64])
        self.rpbT_d = IN("rpbT", [L, 8, 64, 15 * 64])
        self.cmask_d = IN("cmask", [64, 64])
        self.cosT_d = IN("cosT", [128, NT])
        self.sinT_d = IN("sinT", [128, NT])
        self.strip_d = IN("strip", [128, 1152])
        self.sel_d = IN("sel", [128, 4])
        self.selb_d = IN("selb", [2, 256])
        self.flags_d = IN("flags", [128, 2])
        launch = cfg.get("launch", 0)
        self.launch = launch
        OUT = lambda n, s, dt=F32: self.dram(n, s, dt, "ExternalOutput")
        self.kvloc = [None] * L
        self.kvall = [None] * L
        if launch == 0:
            self.outT = OUT("outT", [D, NT // 2 if ROT else NT])
            self.hbuf = self.dram("hbuf", [D, NT], F32)
            self.kvloc = [self.dram("kvloc%d" % l, [KV_ROWS, NT], BF16) for l in range(L)]
            if NRANK == 2:
                self.kvall = [self.dram("kvall%d" % l, [2 * KV_ROWS, NT], BF16) for l in range(L)]
        else:
            self.lB = {2: 0, 3: 1}.get(launch)
            self.lA = {1: 0, 2: 1}.get(launch)
            if cfg.get("skipA"):
                self.lA = None
            if launch > 1:
                self.h_in = IN("h_in", [D, NT])
            self.h_out = OUT("outT" if launch == 3 else "h_out", [D, NT])
            if self.lA is not None:
                self.kvloc[self.lA] = OUT("kvloc_out", [KV_ROWS, NT], BF16)
            if self.lB is not None:
                self.kvall[self.lB] = IN("kvall_in", [2 * KV_ROWS, NT], BF16)
                self.kvloc[self.lB] = IN("kvloc_in", [KV_ROWS, NT], BF16)
        self.ebt = [self.dram("ebt%d" % l, [8, 64, 15 * 64], BF16) for l in range(L)]
        psA = self.st.enter_context(nc.psum_tensor("psA", [128, 2048], F32))
        self.ps = [psA[:, i * 512:(i + 1) * 512] for i in range(4)]
        self.ps += [self.st.enter_context(nc.psum_tensor("ps%d" % i, [128, 512], F32))[:] for i in range(4, 8)]
        self.ps2 = [(psA[:, 0:1024], KS(["ps0", "ps1"])), (psA[:, 1024:2048], KS(["ps2", "ps3"]))]
        self.ps2_i = 0
        self.ones_f = self.T("ones_f", [128, 128], F32)
        self.ones_b = self.T("ones_b", [128, 128], BF16)
        self.gcols = self.T("gcols_s", [128, 9 * L * 8], F32)
        self.nbg = self.T("nbg", [128, L * 3 * 8], F32)
        self.gsub = self.T("gsub_s", [128, L], F32)
        self.esink = self.T("esink", [128, L * 8], F32)
        self.lamt = self.T("lamt", [128, 8 * L], F32)
        self.lamc = self.T("lamc", [128, L], F32)
        self.nlam = self.T("nlam", [128, L], F32)
        self.dacc = self.T("dacc", [128, 2 * TB], F32)
        self.e2 = Rot(self, "e2b", 3, [128, 2 * TB], BF16)
        self.strip = self.T("strip_s", [128, 1152], BF16)
        self.sel = self.T("sel_s", [128, 4], BF16)
        self.selb = self.T("selb_s", [2, 256], F32)
        self.cmask = self.T("cmask_s", [64, 64], F32)
        self.eps_t = self.T("eps_t", [128, 1], F32)
        self.one_t = self.T("one_t", [128, 1], F32)
        self.flags = self.T("flags_s", [128, 2], F32)
        self.hb = self.T("hb", [128, 8, TB], F32)
        self.xnT = self.T("xnT", [128, 8, TB], BF16)
        self.hT = self.T("hT", [128, 22, TB], BF16)
        self.ysb = self.T("ysb", [128, 8, TB], F32)
        self.cosT = self.T("cosT_s", [128, TB], F32)
        self.sinT = self.T("sinT_s", [128, TB], F32)
        self.wslot = Rot(self, "wslot", 4, [128, 2048], BF16)
        self.f32s = Rot(self, "f32s", 3, [128, TB], F32)
        self.bfs = Rot(self, "bfs", 4, [128, TB], BF16)
        self.rstd = Rot(self, "rstd", 2, [128, TB], F32)
        self.dfb = Rot(self, "dfb", 4, [128, TB], F32)
        self.sqb = Rot(self, "sqb", 2, [128, TB], BF16)
        self.pm = Rot(self, "pm", 4, [128, TB], BF16)
        self.kst = Rot(self, "kst", 2, [128, TB], BF16)
        self.vst = Rot(self, "vst", 1, [128, 4, 512], BF16)
        self.cin = AliasRot([self.ysb[:, 0:4, :].rearrange("p a b -> p (a b)"), self.ysb[:, 4:8, :].rearrange("p a b -> p (a b)")],
                            [KS([("ysb", c) for c in range(0, 4)]), KS([("ysb", c) for c in range(4, 8)])])
        self.cout = AliasRot([self.hT[:, 0:4, :].rearrange("p a b -> p (a b)"), self.hT[:, 4:8, :].rearrange("p a b -> p (a b)")],
                             [KS([("hT", c) for c in range(0, 4)]), KS([("hT", c) for c in range(4, 8)])])
        self.qT = self.T("qT", [128, 4, TB], BF16)
        self.oT = [self.T("oT%d" % i, [128, 4, TB], BF16) for i in range(3)]
        self.mT = self.T("mT", [128, 8, TB], BF16)
        self.macc = self.T("macc", [128, 2, TB], F32)
        self.kTA = self.T("kTA", [128, 4, 1024], BF16)
        self.vA = self.T("vA", [128, 16, 512], BF16)
        self.ebh = Rot(self, "ebh", 2, [128, 15 * 64], BF16)
        self.kTB = self.T("kTB", [128, 2, 768], BF16)
        self.vB = self.T("vB", [128, 6, 128], BF16)
        self.kTC = Rot(self, "kTC", 2, [128, 1024], BF16)
        self.vC = Rot(self, "vC", 2, [128, 8, 128], BF16)

        self.kvkeys = {i: [] for i in range(L)}
        self.setup_consts()
        nl = cfg.get("layers", L)
        nblk = cfg.get("nblk", NBLK)
        stop_after = cfg.get("stop_after", None)
        if launch == 0:
            self.setup_weights(list(range(nl)), list(range(nl)))
            for blk in range(nblk):
                self.load_h(blk, self.xT)
                self.stageA(0, blk)
                self.store_h(blk, self.hbuf)
            for l in range(nl):
                self.allgather(l)
                if stop_after == ("A", l):
                    break
                for blk in range(nblk // 2 if (ROT and l == nl - 1) else nblk):
                    self.load_h(blk, self.hbuf)
                    self.stageB(l, blk)
                    last = (l == nl - 1)
                    if not last:
                        self.stageA(l + 1, blk)
                    self.store_h(blk, self.outT if (last and not cfg.get("debug")) else self.hbuf)
        else:
            self.setup_weights([] if self.lB is None else [self.lB], [] if self.lA is None else [self.lA])
            for blk in range(nblk):
                self.load_h(blk, self.xT if launch == 1 else self.h_in)
                if self.lB is not None:
                    self.stageB(self.lB, blk)
                if self.lA is not None:
                    self.stageA(self.lA, blk)
                self.store_h(blk, self.h_out)
        if cfg.get("debug") and launch == 0:
            dh = self.dram("dbg_h", [D, NT], F32, "ExternalOutput")
            self.dma(dh, self.hbuf, [("hbuf", b) for b in range(nblk)], ["dbg_h"], eng="pool")
            self.outs.append("dbg_h")
            dk = self.dram("dbg_kv", [2 * KV_ROWS, NT], BF16, "ExternalOutput")
            self.dma(dk, self.kvall[0], [("kvall", 0)], ["dbg_kv"], eng="pool")
            self.outs.append("dbg_kv")
        self.finish()
        return nc

    def setup_consts(self):
        P = self.P
        P.op("pool", lambda e: e.memset(self.ones_f[:], 1.0), writes=["ones_f"])
        P.op("pool", lambda e: e.memset(self.ones_b[:], 1.0), writes=["ones_b"])
        P.op("pool", lambda e: e.memset(self.eps_t[:], EPS), writes=["eps_t"])
        P.op("pool", lambda e: e.memset(self.one_t[:], 1.0), writes=["one_t"])
        self.dma(self.flags[:], self.flags_d, [], ["flags"])
        self.dma(self.gcols[:], self.gcols_d, [], ["gcols"])
        for l in range(L):
            for which in (1, 8):
                c0 = (l * 9 + which) * 8
                self.ts("dve", self.gcols[:, c0:c0 + 8], self.gcols[:, c0:c0 + 8], 0.5, None, ALU.mult, None, ["gcols"], ["gcols"])
        self.dma(self.nbg[:], self.bgate_d, [], ["nbg"])
        self.ts("dve", self.nbg[:], self.nbg[:], -1.0, None, ALU.mult, None, ["nbg"], ["nbg"])
        self.dma(self.gsub[:], self.gsub_d, [], ["gsub"])
        for l in range(L):
            lam_init = 0.8 - 0.6 * float(np.exp(-0.3 * l))
            self.ts("dve", self.gsub[:, l:l + 1], self.gsub[:, l:l + 1], 1.0 - lam_init, None, ALU.mult, None, ["gsub"], ["gsub"])
        self.dma(self.esink[:], self.sink_d.partition_broadcast(128), [], ["esink"])
        self.act(self.esink[:], self.esink[:], AF.Exp, ["esink"], ["esink"])
        lrt, lrk = self.cin.next()
        self.dma(lrt[:, 0:4 * L * 64], self.lamv_d.partition_broadcast(128), [], [lrk])
        lr = lrt
        n = L * 64
        for l in range(L):
            for pair in range(2):
                q = lr[:, (2 * pair) * n + l * 64:(2 * pair) * n + (l + 1) * 64]
                k = lr[:, (2 * pair + 1) * n + l * 64:(2 * pair + 1) * n + (l + 1) * 64]
                col = l * 4 + pair
                tmp, tk = self.f32s.next()
                self.tt("dve", tmp[:, 0:64], q, k, ALU.mult, [lrk], [tk])
                self.P.op("act", (lambda e, tmp=tmp, col=col: e.activation(out=tmp[:, 64:128], in_=tmp[:, 0:64], func=AF.Copy, accum_out=self.lamt[:, col:col + 1])), [tk], [tk, "lamt"])
            self.act(self.lamt[:, l * 4:l * 4 + 2], self.lamt[:, l * 4:l * 4 + 2], AF.Exp, ["lamt"], ["lamt"])
            lam_init = 0.8 - 0.6 * float(np.exp(-0.3 * l))
            self.tt("dve", self.lamt[:, l * 4 + 2:l * 4 + 3], self.lamt[:, l * 4:l * 4 + 1], self.lamt[:, l * 4 + 1:l * 4 + 2], ALU.subtract, ["lamt"], ["lamt"])
            self.ts("dve", self.lamc[:, l:l + 1], self.lamt[:, l * 4 + 2:l * 4 + 3], -1.0, -lam_init, ALU.mult, ALU.add, ["lamt"], ["lamc"])
            self.ts("dve", self.nlam[:, l:l + 1], self.lamt[:, l * 4 + 2:l * 4 + 3], -1.0, -lam_init, ALU.mult, ALU.add, ["lamt"], ["nlam"])
            self.P.op("dve", (lambda e, l=l: e.memset(self.lamc[0:1, l:l + 1], 1.0)), ["lamc"], ["lamc"])
        t, tk = self.cin.next()
        self.dma(t[:, 0:1152], self.strip_d, [], [tk])
        self.copy("dve", self.strip[:], t[:, 0:1152], [tk], ["strip"])
        t2, tk2 = self.cin.next()
        self.dma(t2[:, 0:4], self.sel_d, [], [tk2])
        self.copy("dve", self.sel[:], t2[:, 0:4], [tk2], ["sel"])
        self.dma(self.selb[:], self.selb_d, [], ["selb"])
        self.dma(self.cmask[:], self.cmask_d, [], ["cmask"])
        for l in range(L):
            for h in range(8):
                t, tk = self.cin.next()
                self.dma(t[0:64, 0:960], self.rpbT_d[l, h], [], [tk])
                self.act(t[0:64, 0:960], t[0:64, 0:960], AF.Exp, [tk], [tk])
                o, ok = self.cout.next()
                self.tt("dve", o[0:64, 0:960].rearrange("p (i c) -> p i c", c=64), t[0:64, 0:960].rearrange("p (i c) -> p i c", c=64),
                        self.cmask[:].unsqueeze(1).to_broadcast([64, 15, 64]), ALU.mult, [tk, "cmask"], [ok])
                self.dma(self.ebt[l][h], o[0:64, 0:960], [ok], [("ebt", l, h)], eng="pool")

    def conv_weight(self, spec):
        scr = self.dram("ws_" + spec.name, [spec.ncb, spec.nkg, 128, spec.gk * spec.w], BF16)
        spec.scr = scr
        n = spec.gk * spec.w
        for cb in range(spec.ncb):
            for kg in range(spec.nkg):
                src = spec.src[kg * spec.gk * 128:(kg + 1) * spec.gk * 128, cb * spec.w:(cb + 1) * spec.w].rearrange("(k p) n -> p k n", p=128)
                t, tk = self.cin.next()
                self.dma(t[:, 0:n].rearrange("p (k n) -> p k n", k=spec.gk), src, [], [tk])
                o, ok = self.cout.next()
                self.cv_i = getattr(self, "cv_i", 0) + 1
                eng = ("dve", "act", "pool")[self.cv_i % 3] if not spec.swap else ("dve", "act")[self.cv_i % 2]
                if not spec.swap:
                    self.copy(eng, o[:, 0:n], t[:, 0:n], [tk], [ok])
                else:
                    sv = t[:, 0:n].rearrange("p (a t d) -> p a t d", t=2, d=32)
                    dv = o[:, 0:n].rearrange("p (a t d) -> p a t d", t=2, d=32)
                    self.copy("dve", dv[:, :, 0, :], sv[:, :, 1, :], [tk], [ok])
                    self.copy("act", dv[:, :, 1, :], sv[:, :, 0, :], [tk], [ok])
                self.dma(scr[cb, kg], o[:, 0:n], [ok], [("ws", spec.name, cb, kg)], eng="pool")

    def setup_weights(self, layersB, layersA):
        self.W = {}
        for l in range(L):
            for f in range(2):
                s = WSpec("gu%d_%d" % (f, l), self.w_gu[f][l], D, 2 * FF, 256, 8)
                self.W[("gu", f, l)] = s
                s2 = WSpec("dn%d_%d" % (f, l), self.w_dn[f][l], FF, D, 128, 11)
                self.W[("dn", f, l)] = s2
            self.W[("in", l)] = WSpec("in_%d" % l, self.w_in[l], D, 6912, 256, 8)
            self.W[("insw", l)] = WSpec("insw_%d" % l, self.w_in[l][:, 1536:3328], D, 1792, 256, 8, swap=True)
            for i in range(3):
                self.W[("br", l, i)] = WSpec("br%d_%d" % (i, l), self.w_br[l, i], 512, D, 256, 4)
            self.W[("out", l)] = WSpec("out_%d" % l, self.w_o[l], D, D, 256, 8)
        order = []
        for l in sorted(set(layersA) | set(layersB)):
            if l in layersA:
                order += [("gu", 0, l), ("dn", 0, l)]
            order += [("in", l), ("insw", l)]
            if l in layersB:
                order += [("br", l, 0), ("br", l, 1), ("br", l, 2), ("out", l), ("gu", 1, l), ("dn", 1, l)]
        for k in order:
            self.conv_weight(self.W[k])

    def load_slot(self, spec, cb, kg):
        t, tk = self.wslot.next()
        n = spec.gk * spec.w
        self.dma(t[:, 0:n], spec.scr[cb, kg], [("ws", spec.name, cb, kg)], [tk])
        return t[:, 0:n].rearrange("p (k n) -> p k n", k=spec.gk), tk

    def load_h(self, blk, src_t):
        src = src_t.rearrange("(c p) t -> p c t", p=128)[:, :, blk * TB:(blk + 1) * TB]
        self.dma(self.hb[:], src, [("hbuf", blk)], ["hb"])
        self.dma(self.cosT[:], self.cosT_d[:, blk * TB:(blk + 1) * TB], [], ["cosT"])
        self.dma(self.sinT[:], self.sinT_d[:, blk * TB:(blk + 1) * TB], [], ["sinT"])

    def store_h(self, blk, dst_t):
        dst = dst_t.rearrange("(c p) t -> p c t", p=128)[:, :, blk * TB:(blk + 1) * TB]
        final = dst_t is not getattr(self, "hbuf", None)
        self.dma(dst, self.hb[:], ["hb"], [("out", blk) if final else ("hbuf", blk)], eng="pool")
        if final:
            self.outs.append(("out", blk))

    def finish(self):
        if self.launch != 0 and self.lA is not None:
            self.outs.extend(self.kvkeys[self.lA])
        self.P.op("sp", None, reads=self.outs)
        self.P.finalize(self.st)
        self.P.run_block()

    def stats_rstd(self, srcs, src_keys, inv_n, from_psum=False):
        psS, pk = self.bank("stat", [6, 7])
        n = len(srcs)
        for c, (s, sk) in enumerate(zip(srcs, src_keys)):
            sq, qk = self.sqb.next()
            self.act(sq[:], s, AF.Square, [sk], [qk])
            self.mm(psS[:], self.ones_b[:], sq[:], c == 0, c == n - 1, [qk, "ones_b"], [pk])
        ln, lk = self.rstd.next()
        self.act(ln[:], psS[:], AF.Ln, [pk, "eps_t"], [lk], scale=inv_n, bias=self.eps_t[:])
        self.act(ln[:], ln[:], AF.Exp, [lk], [lk], scale=-0.5)
        return ln, lk

    def norm_to_xnT(self, l, which):
        rstd, rk = self.stats_rstd([self.hb[:, c, :] for c in range(8)], ["hb"] * 8, 1.0 / D)
        g0 = (l * 9 + which) * 8
        for c in range(8):
            self.stt("dve", self.xnT[:, c, :], self.hb[:, c, :], self.gcols[:, g0 + c:g0 + c + 1], rstd[:], ALU.mult, ALU.mult,
                     ["hb", "gcols", rk], [("xnT", c)])

    def post_norm_add(self, l, which):
        rstd, rk = self.stats_rstd([self.ysb[:, c, :] for c in range(8)], [("ysb", c) for c in range(8)], 1.0 / D)
        g0 = (l * 9 + which) * 8
        for c in range(8):
            t, tk = self.f32s.next()
            self.stt("dve", t[:], self.ysb[:, c, :], self.gcols[:, g0 + c:g0 + c + 1], rstd[:], ALU.mult, ALU.mult, [("ysb", c), "gcols", rk], [tk])
            self.tt("pool", self.hb[:, c, :], self.hb[:, c, :], t[:], ALU.add, ["hb", tk], ["hb"])

    def ffn(self, l, f):
        gu, dn = self.W[("gu", f, l)], self.W[("dn", f, l)]
        self.norm_to_xnT(l, 0 if f == 0 else 7)
        xk = [("xnT", c) for c in range(8)]
        for jj in range(11):
            gs, gk_ = self.load_slot(gu, jj, 0)
            us, uk_ = self.load_slot(gu, 11 + jj, 0)
            for sub in range(2):
                j = 2 * jj + sub
                pg, pgk = self.bank("ffn_g", [0, 1])
                pu, puk = self.bank("ffn_u", [2, 3])
                for kc in range(8):
                    self.mm(pg[:], gs[:, kc, sub * 128:(sub + 1) * 128], self.xnT[:, kc, :], kc == 0, kc == 7, [gk_, xk[kc]], [pgk])
                for kc in range(8):
                    self.mm(pu[:], us[:, kc, sub * 128:(sub + 1) * 128], self.xnT[:, kc, :], kc == 0, kc == 7, [uk_, xk[kc]], [puk])
                s, sk = self.f32s.next()
                self.act(s[:], pg[:], AF.Silu, [pgk], [sk])
                self.tt("dve", self.hT[:, j, :], s[:], pu[:], ALU.mult, [sk, puk], [("hT", j)])
        for c in range(8):
            py, pyk = self.bank("ffn_y", [4, 5])
            for kg in range(2):
                ws, wk = self.load_slot(dn, c, kg)
                for ki in range(11):
                    kc = kg * 11 + ki
                    self.mm(py[:], ws[:, ki, :], self.hT[:, kc, :], kc == 0, kc == 21, [wk, ("hT", kc)], [pyk])
            self.copy("act" if c % 2 == 0 else "dve", self.ysb[:, c, :], py[:], [pyk], [("ysb", c)])
        self.post_norm_add(l, 1 if f == 0 else 8)

    def proj_fm(self, l, cb_list, dst_fn, rope, xk):
        win, wsw = self.W[("in", l)], self.W[("insw", l)]
        i = 0
        for cb in cb_list:
            ws, wk = self.load_slot(win, cb, 0)
            if rope:
                ss, sk_ = self.load_slot(wsw, cb - 6, 0)
            for half in range(2):
                pk_, pkk = self.bank("pj_a", [0, 1])
                for kc in range(8):
                    self.mm(pk_[:], ws[:, kc, half * 128:(half + 1) * 128], self.xnT[:, kc, :], kc == 0, kc == 7, [wk, xk[kc]], [pkk])
                dst, dk, post = dst_fn(i)
                if dst is None:
                    i += 1
                    continue
                if rope:
                    ps2, ps2k = self.bank("pj_b", [2, 3])
                    for kc in range(8):
                        self.mm(ps2[:], ss[:, kc, half * 128:(half + 1) * 128], self.xnT[:, kc, :], kc == 0, kc == 7, [sk_, xk[kc]], [ps2k])
                    t1, t1k = self.f32s.next()
                    t2, t2k = self.f32s.next()
                    self.tt("dve", t1[:], pk_[:], self.cosT[:], ALU.mult, [pkk, "cosT"], [t1k])
                    self.tt("dve", t2[:], ps2[:], self.sinT[:], ALU.mult, [ps2k, "sinT"], [t2k])
                    self.tt("pool", dst, t1[:], t2[:], ALU.add, [t1k, t2k], [dk])
                else:
                    self.copy("act" if i % 2 == 0 else "dve", dst, pk_[:], [pkk], [dk])
                if post is not None:
                    post()
                i += 1

    def stageA(self, l, blk):
        if "ffn" not in self.cfg.get("skip", ()):
            self.ffn(l, 0)
        if "kv" in self.cfg.get("skip", ()):
            return
        self.norm_to_xnT(l, 2)
        xk = [("xnT", c) for c in range(8)]
        kv = self.kvloc[l]
        t0 = blk * TB
        if not hasattr(self, "kvkeys"):
            self.kvkeys = {i: [] for i in range(L)}

        def kdst_factory(row0, only0=False):
            def fn(i):
                if only0 and i > 0:
                    return None, None, None
                t, tk = self.kst.next()
                r0 = row0 + i * 128

                def post():
                    self.dma(kv[r0:r0 + 128, t0:t0 + TB], t[:], [tk], [self.key("kvw%d" % l)], eng="pool")
                    self.kvkeys[l].append("kvw%d#%d" % (l, self.uid))
                return t[:], tk, post
            return fn

        self.proj_fm(l, [2, 3], kdst_factory(KA0), False, xk)
        self.proj_fm(l, [8], kdst_factory(KB0, True), True, xk)
        self.proj_fm(l, [11, 12], kdst_factory(KC0), True, xk)
        win = self.W[("in", l)]
        for (cbs, vrow0, ncols) in (([4, 5], VA0, 512), ([8], VB0, 128), ([13, 14], VC0, 512)):
            vt, vk = self.vst.next()
            for ci, cb in enumerate(cbs):
                ws, wk = self.load_slot(win, cb, 0)
                for tt_ in range(4):
                    pv, pvk = self.bank("pj_v", [4, 5])
                    if ncols == 128:
                        for kc in range(8):
                            self.mm(pv[:, 0:128], self.xnT[:, kc, tt_ * 128:(tt_ + 1) * 128], ws[:, kc, 128:256], kc == 0, kc == 7, [wk, xk[kc]], [pvk])
                        self.copy("act" if tt_ % 2 == 0 else "dve", vt[:, tt_, 0:128], pv[:, 0:128], [pvk], [vk])
                    else:
                        for kc in range(8):
                            self.mm(pv[:, 0:256], self.xnT[:, kc, tt_ * 128:(tt_ + 1) * 128], ws[:, kc, :], kc == 0, kc == 7, [wk, xk[kc]], [pvk])
                        self.copy("act" if tt_ % 2 == 0 else "dve", vt[:, tt_, ci * 256:(ci + 1) * 256], pv[:, 0:256], [pvk], [vk])
            nrows = ncols * NT // NT
            view = kv[vrow0:vrow0 + ncols, :].rearrange("r (q c) -> (r q) c", c=ncols)
            dstv = view[t0:t0 + TB, :].rearrange("(t p) c -> p t c", p=128)
            self.dma(dstv, vt[:, :, 0:ncols], [vk], [self.key("kvw%d" % l)], eng="pool")
            self.kvkeys[l].append("kvw%d#%d" % (l, self.uid))

    def allgather(self, l):
        if NRANK == 1:
            return
        if self.cfg.get("no_ag"):
            self.dma(self.kvall[l][0:KV_ROWS, :], self.kvloc[l], list(self.kvkeys[l]), [("kvall", l)], eng="pool")
            return
        self.P.op("pool", lambda e: e.collective_compute("AllGather", ALU.bypass, replica_groups=[[0, 1], [2, 3], [4, 5], [6, 7]],
                                                         ins=[self.kvloc[l]], outs=[self.kvall[l]]),
                  reads=list(self.kvkeys[l]), writes=[("kvall", l)], dma=True)

    def setup_layer_tables(self, l):
        pass

    def kv_rows_tok(self, l, row0, nrows, tok_lo, tok_hi):
        out = []
        for r in range(2):
            lo, hi = max(tok_lo, r * NT), min(tok_hi, (r + 1) * NT)
            if lo < hi:
                out.append((self.kvall[l][r * KV_ROWS + row0:r * KV_ROWS + row0 + nrows, lo - r * NT:hi - r * NT], lo - tok_lo, hi - lo))
        return out

    def v_view(self, l, r, vrow0, ncols):
        return self.kvall[l][r * KV_ROWS + vrow0:r * KV_ROWS + vrow0 + ncols, :].rearrange("r (q c) -> (r q) c", c=ncols)

    def na_valid(self, half, R0, krl, j):
        if NRANK == 1 and ROT:
            tq = (R0 + j + 64 * half) % 128
            tk = ((krl % 128) + 64 * half) % 128
            if tk - tq != krl - (R0 + j):
                return False
            ws = min(max(tq - 4, 0), 120)
            return ws <= tk < ws + 8
        if NRANK == 1:
            half = 0
        qr = R0 + j + (NT // 64) * half
        kr = krl + (NT // 64) * half
        ws = min(max(qr - 4, 0), 120)
        return (0 <= kr <= 127) and (ws <= kr < ws + 8)

    def attn_na(self, l, blk):
        R0 = 8 * blk
        NR_ = NT // 64
        own_lo, own_hi = max(R0 - 4, 0), min(R0 + 12, NR_)
        kvl = self.kvloc[l]
        pieces = []
        if R0 - 4 < 0 and NRANK == 2:
            pieces.append(("prev", R0 - 4, 0))
        pieces.append(("own", own_lo, own_hi))
        if R0 + 12 > NR_ and NRANK == 2:
            pieces.append(("next", NR_, R0 + 12))
        if NRANK == 1 and ROT:
            if R0 - 4 < 0:
                pieces.append(("wrap", R0 - 4, 0))
            if R0 + 12 > NR_:
                pieces.append(("wrap", NR_, R0 + 12))
        for kind, lo, hi in pieces:
            s0 = lo - (R0 - 4)
            n = (hi - lo) * 64
            if kind == "wrap":
                wl, wh = lo % NR_, (hi - 1) % NR_ + 1
                ksrc = kvl[KA0:KA0 + 512, wl * 64:wh * 64]
                vsrc = kvl[VA0:VA0 + 512, :].rearrange("r (q c) -> (r q) c", c=512)[wl * 64:wh * 64, :]
                rk = list(self.kvkeys[l])
            elif kind == "own":
                ksrc = kvl[KA0:KA0 + 512, lo * 64:hi * 64]
                vsrc = kvl[VA0:VA0 + 512, :].rearrange("r (q c) -> (r q) c", c=512)[lo * 64:hi * 64, :]
                rk = list(self.kvkeys[l])
            elif kind == "prev":
                ksrc = self.kvall[l][KA0:KA0 + 512, NT + lo * 64:NT + hi * 64]
                vsrc = self.v_view(l, 0, VA0, 512)[NT + lo * 64:NT + hi * 64, :]
                rk = [("kvall", l)]
            else:
                ksrc = self.kvall[l][KV_ROWS + KA0:KV_ROWS + KA0 + 512, (lo - NR_) * 64:(hi - NR_) * 64]
                vsrc = self.v_view(l, 1, VA0, 512)[(lo - NR_) * 64:(hi - NR_) * 64, :]
                rk = [("kvall", l)]
            self.dma(self.kTA[:, :, s0 * 64:s0 * 64 + n], ksrc.rearrange("(c p) t -> p c t", p=128), rk, ["kTA"])
            for ph in range(2):
                self.dma(self.vA[ph * 64:(ph + 1) * 64, s0:s0 + (hi - lo), :], vsrc.rearrange("(r p) c -> p r c", p=64), rk, ["vA"])
        segs = []
        for s_ in range(16):
            krl = R0 - 4 + s_
            cats = [(self.na_valid(0, R0, krl, j), self.na_valid(1, R0, krl, j)) for j in range(8)]
            j = 0
            while j < 8:
                if cats[j] == (False, False):
                    j += 1
                    continue
                j1 = j
                while j1 + 1 < 8 and cats[j1 + 1] == cats[j]:
                    j1 += 1
                segs.append((s_, krl, j, j1, cats[j]))
                j = j1 + 1
        for ch in range(4):
            eb, ebk = self.ebh.next()
            for ph in range(2):
                self.dma(eb[ph * 64:(ph + 1) * 64, :], self.ebt[l][2 * ch + ph], [("ebt", l, 2 * ch + ph)], [ebk])
            ebv = eb[:].rearrange("p (i c) -> p i c", c=64)
            pO = [self.ps[2], self.ps[3]]
            pOk = ["ps2", "ps3"]
            pR = [self.ps[4], self.ps[5]]
            pRk = ["ps4", "ps5"]

            def na_S(si, ch=ch):
                s_, krl, j0, j1, cat = segs[si]
                nq = (j1 - j0 + 1) * 64
                pS, pSk = self.bank("att_s", [0, 1, 6, 7])
                for ph in range(2):
                    b = ph * 64
                    self.mm(pS[b:b + 64, 0:nq], self.kTA[b:b + 64, ch, s_ * 64:(s_ + 1) * 64], self.qT[b:b + 64, ch, j0 * 64:(j1 + 1) * 64],
                            True, True, ["kTA", ("qT", ch)], [pSk])
                return pS, pSk

            pend, nemit = [], 0
            for si, (s_, krl, j0, j1, cat) in enumerate(segs):
                while nemit < len(segs) and nemit <= si + LOOKAHEAD:
                    pend.append(na_S(nemit))
                    nemit += 1
                pS, pSk = pend.pop(0)
                nq = (j1 - j0 + 1) * 64
                c0, c1 = j0 * 64, (j1 + 1) * 64
                e, ek = self.bfs.next()
                self.act(e[:, 0:nq], pS[:, 0:nq], AF.Exp, [pSk], [ek], scale=0.125)
                pm, pmk = self.pm.next()
                idx0 = 7 - krl + R0 + j0
                ev = e[:, 0:nq].rearrange("p (i c) -> p i c", c=64)
                pv = pm[:, 0:nq].rearrange("p (i c) -> p i c", c=64)
                tb = ebv[:, idx0:idx0 + (j1 - j0 + 1), :]
                if cat == (True, True):
                    self.tt("dve", pv, ev, tb, ALU.mult, [ek, ebk], [pmk])
                else:
                    fl = self.flags[:, 1:2] if cat == (True, False) else self.flags[:, 0:1]
                    self.stt("dve", pv, ev, fl, tb, ALU.mult, ALU.mult, [ek, ebk, "flags"], [pmk])
                last = si == len(segs) - 1
                for ph in range(2):
                    b = ph * 64
                    h = 2 * ch + ph
                    self.mm(pO[ph][0:64, c0:c1], self.vA[b:b + 64, s_, h * 64:(h + 1) * 64], pm[b:b + 64, 0:nq], si == 0, last, ["vA", pmk], [pOk[ph]], skip_group_check=True)
                    self.mm(pR[ph][0:64, c0:c1], self.ones_b[b:b + 64, 0:64], pm[b:b + 64, 0:nq], si == 0, last, ["ones_b", pmk], [pRk[ph]], skip_group_check=True)
            for ph in range(2):
                b = ph * 64
                r, rk_ = self.f32s.next()
                self.recip(r[0:64, :], pR[ph][0:64, :], [pRk[ph]], [rk_])
                self.tt("dve", self.oT[0][b:b + 64, ch, :], pO[ph][0:64, :], r[0:64, :], ALU.mult, [pOk[ph], rk_], [("oT0", ch, ph)])

    def attn_sw(self, l, blk):
        kvl = self.kvloc[l]
        T0 = blk * 4 - 1
        tiles = []
        vview = kvl[VB0:VB0 + 128, :].rearrange("r (q c) -> (r q) c", c=128)
        rk = list(self.kvkeys[l])
        rot_mode = (NRANK == 1 and ROT)
        if rot_mode:
            ntl = NT // 128
            tls = []
            for s_ in range(6):
                Tl = (T0 + s_) % ntl
                v = []
                for half in (0, 1):
                    tq0 = (4 * blk + 32 * half) % 64
                    tk = (Tl + 32 * half) % 64
                    v.append(tk - tq0 == s_ - 1)
                tiles.append({(True, True): "own", (False, True): "prev", (True, False): "next", (False, False): "none"}[tuple(v)])
                tls.append(Tl)
            s_ = 0
            while s_ < 6:
                e_ = s_
                while e_ + 1 < 6 and tls[e_ + 1] == tls[e_] + 1:
                    e_ += 1
                lo, hi = tls[s_], tls[e_] + 1
                for kvh in range(2):
                    for ph in range(2):
                        self.dma(self.kTB[ph * 64:(ph + 1) * 64, kvh, s_ * 128:(e_ + 1) * 128], kvl[KB0 + kvh * 64:KB0 + (kvh + 1) * 64, lo * 128:hi * 128], rk, ["kTB"])
                self.dma(self.vB[:, s_:e_ + 1, :], vview[lo * 128:hi * 128, :].rearrange("(t p) c -> p t c", p=128), rk, ["vB"])
                s_ = e_ + 1
        else:
            for s_ in range(6):
                T = T0 + s_
                kind = "prev" if T < 0 else ("next" if T >= NT // 128 else "own")
                if NRANK == 1 and kind != "own":
                    kind = "none"
                tiles.append(kind)
            own_s = [s_ for s_ in range(6) if tiles[s_] == "own"]
            lo, hi = T0 + own_s[0], T0 + own_s[-1] + 1
            for kvh in range(2):
                for ph in range(2):
                    self.dma(self.kTB[ph * 64:(ph + 1) * 64, kvh, own_s[0] * 128:(own_s[-1] + 1) * 128], kvl[KB0 + kvh * 64:KB0 + (kvh + 1) * 64, lo * 128:hi * 128], rk, ["kTB"])
            self.dma(self.vB[:, own_s[0]:own_s[-1] + 1, :], vview[lo * 128:hi * 128, :].rearrange("(t p) c -> p t c", p=128), rk, ["vB"])
        for s_ in range(6):
            if rot_mode or tiles[s_] in ("own", "none"):
                continue
            r = 0 if tiles[s_] == "prev" else 1
            t_lo = NT - 128 if r == 0 else 0
            for kvh in range(2):
                for ph in range(2):
                    self.dma(self.kTB[ph * 64:(ph + 1) * 64, kvh, s_ * 128:(s_ + 1) * 128],
                             self.kvall[l][r * KV_ROWS + KB0 + kvh * 64:r * KV_ROWS + KB0 + (kvh + 1) * 64, t_lo:t_lo + 128], [("kvall", l)], ["kTB"])
            self.dma(self.vB[:, s_, :], self.v_view(l, r, VB0, 128)[t_lo:t_lo + 128, :], [("kvall", l)], ["vB"])
        for h in range(8):
            ch, b, kvh = h // 2, (h % 2) * 64, h // 4
            pO, pOk = self.bank("att_o", [2, 3])
            pR, pRk = self.bank("att_r", [4, 5])
            proc = [s_ for s_ in range(6) if tiles[s_] != "none"]

            def sw_S(pi, ch=ch, b=b, kvh=kvh):
                s_ = proc[pi]
                pS, pSk = self.bank("att_s", [0, 1, 6, 7])
                self.mm(pS, self.kTB[b:b + 64, kvh, s_ * 128:(s_ + 1) * 128], self.qT[b:b + 64, ch, :], True, True, ["kTB", ("qT", ch)], [pSk])
                return pS, pSk

            pend, nemit = [], 0
            for pi, s_ in enumerate(proc):
                while nemit < len(proc) and nemit <= pi + LOOKAHEAD:
                    pend.append(sw_S(nemit))
                    nemit += 1
                pS, pSk = pend.pop(0)
                e, ek = self.bfs.next()
                self.act(e[:], pS, AF.Exp, [pSk], [ek], scale=0.125)
                pm, pmk = self.pm.next()
                off = 512 - (s_ - 1) * 128
                if tiles[s_] == "own":
                    self.tt("dve", pm[:], e[:], self.strip[:, off:off + 512], ALU.mult, [ek, "strip"], [pmk])
                else:
                    fl = self.flags[:, 0:1] if tiles[s_] == "prev" else self.flags[:, 1:2]
                    self.stt("dve", pm[:], e[:], fl, self.strip[:, off:off + 512], ALU.mult, ALU.mult, [ek, "strip", "flags"], [pmk])
                self.mm(pO[0:64, :], self.vB[:, s_, kvh * 64:(kvh + 1) * 64], pm[:], s_ == proc[0], s_ == proc[-1], ["vB", pmk], [pOk])
                self.mm(pR[0:64, :], self.ones_b[:, 0:64], pm[:], s_ == proc[0], s_ == proc[-1], ["ones_b", pmk], [pRk])
            r, rk_ = self.f32s.next()
            self.ts("dve", r[0:64, :], pR[0:64, :], self.esink[0:64, l * 8 + h:l * 8 + h + 1], None, ALU.add, None, [pRk, "esink"], [rk_])
            self.recip(r[0:64, :], r[0:64, :], [rk_], [rk_])
            self.tt("dve", self.oT[1][b:b + 64, ch, :], pO[0:64, :], r[0:64, :], ALU.mult, [pOk, rk_], [("oT1", ch, h % 2)])

    def attn_diff(self, l, blk):
        npr = NT // 1024
        nchunks = S // 1024
        for h in range(4):
            pO = [self.ps[4], self.ps[5]]
            pOk = ["ps4", "ps5"]
            acc = self.dacc
            self.P.op("pool", lambda e: e.memset(self.dacc[:, 0:TB], 0.0), [], ["dacc"])
            pR1, pR1k = self.bank("stat", [6, 7])
            chunks = {}

            def load_chunk(kc8, h=h):
                r, tl = kc8 // npr, (kc8 % npr) * 1024
                kt_, ktk = self.kTC.next()
                vt_, vtk = self.vC.next()
                if NRANK == 1:
                    kvl_ = self.kvloc[l]
                    rk_ = list(self.kvkeys[l])
                    self.dma(kt_[:], kvl_[KC0 + h * 128:KC0 + (h + 1) * 128, tl:tl + 1024], rk_, [ktk])
                    vv_ = kvl_[VC0:VC0 + 512, :].rearrange("r (q c) -> (r q) c", c=512)
                    self.dma(vt_[:], vv_[tl:tl + 1024, h * 128:(h + 1) * 128].rearrange("(t p) c -> p t c", p=128), rk_, [vtk])
                else:
                    self.dma(kt_[:], self.kvall[l][r * KV_ROWS + KC0 + h * 128:r * KV_ROWS + KC0 + (h + 1) * 128, tl:tl + 1024], [("kvall", l)], [ktk])
                    self.dma(vt_[:], self.v_view(l, r, VC0, 512)[tl:tl + 1024, h * 128:(h + 1) * 128].rearrange("(t p) c -> p t c", p=128), [("kvall", l)], [vtk])
                chunks[kc8] = (kt_, ktk, vt_, vtk)

            items = [(kc8, kt) for kc8 in range(nchunks) for kt in range(8)]

            def emit_S(idx, h=h):
                kc8, kt = items[idx]
                if kt == 0:
                    load_chunk(kc8)
                kt_, ktk, vt_, vtk = chunks[kc8]
                pS2, pS2k = self.ps2[self.ps2_i % 2]
                self.ps2_i += 1
                for t in range(2):
                    self.mm(pS2[:, t * 512:(t + 1) * 512], kt_[t * 64:(t + 1) * 64, kt * 128:(kt + 1) * 128], self.qT[t * 64:(t + 1) * 64, h, :],
                            True, True, [ktk, ("qT", h)], [pS2k])
                return pS2, pS2k

            cur = emit_S(0)
            n_it = len(items)
            for idx in range(n_it):
                nxt = emit_S(idx + 1) if idx + 1 < n_it else None
                kc8, kt = items[idx]
                kt_, ktk, vt_, vtk = chunks[kc8]
                pS2, pS2k = cur
                e, ek = self.e2.next()
                self.act(e[:], pS2, AF.Exp, [pS2k], [ek], scale=0.125)
                self.tt("dve", acc[:, 0:TB], acc[:, 0:TB], e[:, 0:TB], ALU.add, ["dacc", ek], ["dacc"])
                for t in range(2):
                    self.mm(pO[t], vt_[:, kt, :], e[:, t * 512:(t + 1) * 512], idx == 0, idx == n_it - 1, [vtk, ek], [pOk[t]])
                self.mm(pR1, self.ones_b[:], e[:, TB:2 * TB], idx == 0, idx == n_it - 1, ["ones_b", ek], [pR1k])
                cur = nxt
            rs = []
            for t in range(2):
                if t == 0:
                    pR, pRk = self.bank("stat", [6, 7])
                    self.mm(pR, self.ones_f[:], acc[:, 0:TB], True, True, ["ones_f", "dacc"], [pRk])
                else:
                    pR, pRk = pR1, pR1k
                r_, rk_ = self.dfb.next()
                self.recip(r_[:], pR, [pRk], [rk_])
                rs.append((r_, rk_))
            o0, o0k = self.dfb.next()
            self.tt("dve", o0[:], pO[0], rs[0][0][:], ALU.mult, [pOk[0], rs[0][1]], [o0k])
            o1, o1k = self.dfb.next()
            self.tt("dve", o1[:], pO[1], rs[1][0][:], ALU.mult, [pOk[1], rs[1][1]], [o1k])
            self.stt("dve", o0[:], o1[:], self.nlam[:, l:l + 1], o0[:], ALU.mult, ALU.add, [o1k, o0k, "nlam"], [o0k])
            rstd, rk2 = self.stats_rstd([o0[:]], [o0k], 1.0 / 128)
            self.stt("dve", self.oT[2][:, h, :], o0[:], self.gsub[:, l:l + 1], rstd[:], ALU.mult, ALU.mult, [o0k, "gsub", rk2], [("oT2", h, 0), ("oT2", h, 1)])

    def merge_out(self, l):
        win, wo = self.W[("in", l)], self.W[("out", l)]
        xk = [("xnT", c) for c in range(8)]
        for cp in range(4):
            for i in range(3):
                gs, gk_ = self.load_slot(win, 15 + i * 4 + cp, 0)
                bs, bk_ = self.load_slot(self.W[("br", l, i)], cp, 0)
                for sub in range(2):
                    c = 2 * cp + sub
                    pY, pYk = self.bank("mg_y", [0, 1])
                    pG, pGk = self.bank("mg_g", [2, 3])
                    for kc in range(4):
                        self.mm(pY[:], bs[:, kc, sub * 128:(sub + 1) * 128], self.oT[i][:, kc, :], kc == 0, kc == 3, [bk_, ("oT%d" % i, kc, 0), ("oT%d" % i, kc, 1)], [pYk])
                    for kc in range(8):
                        self.mm(pG[:], gs[:, kc, sub * 128:(sub + 1) * 128], self.xnT[:, kc, :], kc == 0, kc == 7, [gk_, xk[kc]], [pGk])
                    g, gk2 = self.f32s.next()
                    col = (l * 3 + i) * 8 + c
                    self.act(g[:], pG[:], AF.Exp, [pGk, "nbg"], [gk2], scale=-1.0, bias=self.nbg[:, col:col + 1])
                    self.act(g[:], g[:], AF.Ln, [gk2, "one_t"], [gk2], scale=1.0, bias=self.one_t[:])
                    self.act(g[:], g[:], AF.Exp, [gk2], [gk2], scale=-1.0)
                    if i == 0:
                        self.tt("dve", self.macc[:, sub, :], pY[:], g[:], ALU.mult, [pYk, gk2], [("macc", sub)])
                    elif i == 1:
                        self.tt("dve", g[:], pY[:], g[:], ALU.mult, [pYk, gk2], [gk2])
                        self.tt("dve", self.macc[:, sub, :], self.macc[:, sub, :], g[:], ALU.add, [("macc", sub), gk2], [("macc", sub)])
                    else:
                        self.tt("dve", g[:], pY[:], g[:], ALU.mult, [pYk, gk2], [gk2])
                        self.tt("dve", self.mT[:, c, :], self.macc[:, sub, :], g[:], ALU.add, [("macc", sub), gk2], [("mT", c)])
        for cb in range(4):
            ws, wk = self.load_slot(wo, cb, 0)
            for sub in range(2):
                c = 2 * cb + sub
                pM, pMk = self.bank("ffn_y", [4, 5])
                for kc in range(8):
                    self.mm(pM[:], ws[:, kc, sub * 128:(sub + 1) * 128], self.mT[:, kc, :], kc == 0, kc == 7, [wk, ("mT", kc)], [pMk])
                self.copy("act" if c % 2 == 0 else "dve", self.ysb[:, c, :], pM[:], [pMk], [("ysb", c)])
        self.post_norm_add(l, 6)

    def stageB(self, l, blk):
        cfg = self.cfg
        self.norm_to_xnT(l, 2)
        xk = [("xnT", c) for c in range(8)]
        parts = cfg.get("parts", "ABC")
        qdst = lambda i: (self.qT[:, i, :], ("qT", i), None)
        skip = cfg.get("skip", ())
        self.proj_fm(l, [0, 1], qdst, False, xk)
        if "na" not in skip:
            self.attn_na(l, blk)
        self.proj_fm(l, [6, 7], qdst, True, xk)
        if "sw" not in skip:
            self.attn_sw(l, blk)
        self.proj_fm(l, [9, 10], qdst, True, xk)
        if "diff" not in skip:
            self.attn_diff(l, blk)
        if cfg.get("dbg_o"):
            if not hasattr(self, "dbg_o"):
                self.dbg_o = [self.dram("dbg_o%d" % i, [512, NT], BF16, "ExternalOutput") for i in range(3)]
            for i in range(3):
                k_ = self.key("dbgo")
                self.dma(self.dbg_o[i].rearrange("(c p) t -> p c t", p=128)[:, :, blk * TB:(blk + 1) * TB], self.oT[i][:],
                         [("oT%d" % i, c, hh) for c in range(4) for hh in range(2)], [k_], eng="pool")
                self.outs.append(k_)
        if "merge" not in skip:
            self.merge_out(l)
        if "ffn" not in skip:
            self.ffn(l, 1)


def build_program(cfg):
    kb = KB(cfg)
    with kb.st:
        nc = kb.build()
    return nc


def _consts():
    pos = np.arange(S, dtype=np.float32)
    inv = (np.float32(10000.0) ** (-(np.arange(0, 64, 2, dtype=np.float32) / np.float32(64)))).astype(np.float32)
    ang = (pos[:, None] * inv[None, :]).astype(np.float32)
    ang = np.concatenate([ang, ang], axis=-1)
    cos = np.cos(ang).astype(np.float32)
    sin = np.sin(ang).astype(np.float32)
    sgn = np.concatenate([-np.ones(32, np.float32), np.ones(32, np.float32)])
    cosT = np.ascontiguousarray(np.concatenate([cos.T, cos.T], axis=0))
    sinT = np.ascontiguousarray(np.concatenate([(sin * sgn).T, (sin * sgn).T], axis=0))
    c = np.arange(64)
    col_start = np.clip(c - 8, 0, 48)
    col_ok = (c[None, :] >= col_start[:, None]) & (c[None, :] < col_start[:, None] + 16)
    cmask = np.ascontiguousarray(col_ok.T.astype(np.float32))
    dc = np.clip(c[:, None] - c[None, :] + 15, 0, 30)
    kl = np.arange(128)[:, None]
    xx = np.arange(1152)[None, :]
    strip = (np.abs(xx - 512 - kl) <= 128).astype(np.float32)
    sel = np.zeros((128, 4), np.float32)
    sel[:, 0] = 1.0
    sel[:, 3] = 1.0
    selb = np.zeros((2, 256), np.float32)
    selb[0, 0:128] = 1.0
    selb[1, 128:256] = 1.0
    return cosT, sinT, cmask, dc, strip, sel, selb


def _host_inputs(inputs):
    cosT, sinT, cmask, dc, strip, sel, selb = _consts()
    f = lambda a: np.ascontiguousarray(np.asarray(a, dtype=np.float32))
    gnames = {0: "ffn1_pre_g", 1: "ffn1_post_g", 2: "mix_pre_g", 6: "mix_post_g", 7: "ffn2_pre_g", 8: "ffn2_post_g"}
    gcols = np.zeros((128, 9 * L * 8), np.float32)
    for l in range(L):
        for which, nm in gnames.items():
            g = f(inputs[nm])[l]
            gcols[:, (l * 9 + which) * 8:(l * 9 + which) * 8 + 8] = g.reshape(8, 128).T
    bg = f(inputs["b_gate"])
    bgate = np.zeros((128, L * 3 * 8), np.float32)
    for l in range(L):
        for i in range(3):
            bgate[:, (l * 3 + i) * 8:(l * 3 + i) * 8 + 8] = bg[l, i].reshape(8, 128).T
    gsub = np.ascontiguousarray(f(inputs["diff_subln_g"]).T)
    sink = f(inputs["sw_sink"]).reshape(-1)
    lamv = np.concatenate([f(inputs[k]).reshape(-1) for k in ("diff_lambda_q1", "diff_lambda_k1", "diff_lambda_q2", "diff_lambda_k2")])
    rpb = f(inputs["na_rpb"])
    idx = np.arange(15)
    rpbT = rpb[:, :, (14 - idx)[None, :, None], dc[:, None, :]]
    rpbT = np.ascontiguousarray(rpbT.reshape(L, 8, 64, 15 * 64))
    shared = {
        "ffn1_w_gu": f(inputs["ffn1_w_gu"]), "ffn2_w_gu": f(inputs["ffn2_w_gu"]),
        "ffn1_w_down": f(inputs["ffn1_w_down"]), "ffn2_w_down": f(inputs["ffn2_w_down"]),
        "w_in": f(inputs["w_in"]), "w_branch": f(inputs["w_branch"]), "w_out": f(inputs["w_out"]),
        "gcols": gcols, "bgate": bgate, "gsub": gsub, "sink": sink, "lamv": lamv, "rpbT": rpbT,
        "cmask": cmask, "strip": strip, "sel": sel, "selb": selb,
    }
    x = f(inputs["x"])
    in_maps = []
    xts = {}
    for c in range(8):
        b, half = c // 2, (c % 2 if NRANK == 2 else 0)
        m = dict(shared)
        if NRANK == 1 and ROT:
            half = c % 2
            perm = (np.arange(S) + half * (S // 2)) % S
            if (b, half) not in xts:
                xts[(b, half)] = (np.ascontiguousarray(x[b][perm].T), np.ascontiguousarray(cosT[:, perm]), np.ascontiguousarray(sinT[:, perm]))
            m["xT"], m["cosT"], m["sinT"] = xts[(b, half)]
            fl = np.zeros((128, 2), np.float32)
            fl[:, 0] = half
            fl[:, 1] = 1 - half
            m["flags"] = fl
            in_maps.append(m)
            continue
        if (b, half) not in xts:
            xts[(b, half)] = np.ascontiguousarray(x[b, half * NT:(half + 1) * NT, :].T)
        m["xT"] = xts[(b, half)]
        m["cosT"] = np.ascontiguousarray(cosT[:, half * NT:(half + 1) * NT])
        m["sinT"] = np.ascontiguousarray(sinT[:, half * NT:(half + 1) * NT])
        fl = np.zeros((128, 2), np.float32)
        if NRANK == 2:
            fl[:, 0] = half
            fl[:, 1] = 1 - half
        m["flags"] = fl
        in_maps.append(m)
    return in_maps


_NC_CACHE = {}
FUSED = True


def _get_nc(cfg_key, cfg):
    if cfg_key not in _NC_CACHE:
        _NC_CACHE[cfg_key] = build_program(cfg)
    return _NC_CACHE[cfg_key]


def _run(nc, in_maps, names):
    maps = [{k: m[k] for k in names if k in m} for m in in_maps]
    res = run_bass_kernel_spmd(nc, maps, core_ids=list(range(len(maps))))
    return res.results


WNAMES = ["ffn1_w_gu", "ffn2_w_gu", "ffn1_w_down", "ffn2_w_down", "w_in", "w_branch", "w_out", "gcols", "bgate", "gsub", "sink",
          "lamv", "rpbT", "cmask", "strip", "sel", "selb", "cosT", "sinT", "flags"]


def _pair_gather(results, name):
    out = []
    for c in range(len(results)):
        p = (c // 2) * 2
        out.append(np.concatenate([np.asarray(results[p][name]), np.asarray(results[p + 1][name])], axis=0))
    return out


def kernel_split(in_maps, cfg_extra=None):
    ce = cfg_extra or {}
    n = len(in_maps)
    r1 = _run(_get_nc("l1", dict(ce, launch=1)), in_maps, WNAMES + ["xT"])
    kvall = _pair_gather(r1, "kvloc_out")
    for c in range(n):
        in_maps[c]["h_in"] = np.asarray(r1[c]["h_out"])
        in_maps[c]["kvall_in"] = kvall[c]
        in_maps[c]["kvloc_in"] = np.asarray(r1[c]["kvloc_out"])
    del r1
    r2 = _run(_get_nc("l2", dict(ce, launch=2)), in_maps, WNAMES + ["h_in", "kvall_in", "kvloc_in"])
    kvall = _pair_gather(r2, "kvloc_out")
    for c in range(n):
        in_maps[c]["h_in"] = np.asarray(r2[c]["h_out"])
        in_maps[c]["kvall_in"] = kvall[c]
        in_maps[c]["kvloc_in"] = np.asarray(r2[c]["kvloc_out"])
    del r2
    r3 = _run(_get_nc("l3", dict(ce, launch=3)), in_maps, WNAMES + ["h_in", "kvall_in", "kvloc_in"])
    return [np.asarray(r3[c]["outT"]) for c in range(n)]


def kernel(**inputs):
    if FUSED:
        set_mode(1, rot=True)
        in_maps = _host_inputs(inputs)
        res = _run(_get_nc("fused", {}), in_maps, WNAMES + ["xT"])
        out = np.empty((4, S, D), np.float32)
        for c in range(8):
            b, half = c // 2, c % 2
            out[b, half * (S // 2):(half + 1) * (S // 2), :] = np.asarray(res[c]["outT"]).T
        return out
    set_mode(2)
    in_maps = _host_inputs(inputs)
    outs = kernel_split(in_maps)
    out = np.empty((4, S, D), np.float32)
    for c in range(8):
        b, half = c // 2, c % 2
        out[b, half * NT:(half + 1) * NT, :] = outs[c].T
    return out
```

```python
import contextlib
import numpy as np
import concourse.bass as bass
import concourse.mybir as mybir
from concourse.bass_utils import run_bass_kernel_spmd

F32 = mybir.dt.float32
BF16 = mybir.dt.bfloat16
AF = mybir.ActivationFunctionType
ALU = mybir.AluOpType

D = 1024
FF = 2816
S = 8192
NT = 4096
TB = 512
NBLK = NT // TB
NRANK = 2


ROT = False


def set_mode(nrank, rot=False):
    global NT, NBLK, NRANK, ROT
    ROT = rot
    NRANK = nrank
    NT = S // nrank
    NBLK = NT // TB
L = 2
EPS = 1e-6
KV_ROWS = 2304
KA0, KB0, KC0, VA0, VB0, VC0 = 0, 512, 640, 1152, 1664, 1792

N_DMA_SEMS = 16
LOOKAHEAD = 2
SAME_ENGINE_SYNC = True


class Op:
    __slots__ = ("eng", "emit", "deps", "is_dma", "marked", "semval", "dsem", "dval")

    def __init__(self, eng, emit, is_dma):
        self.eng = eng
        self.emit = emit
        self.deps = []
        self.is_dma = is_dma
        self.marked = False
        self.semval = 0
        self.dsem = None
        self.dval = 0


class Prog:
    ENGS = ("pe", "act", "dve", "pool", "sp")

    def __init__(self, nc):
        self.nc = nc
        self.ops = []
        self.last_writer = {}
        self.readers = {}

    def op(self, eng, emit, reads=(), writes=(), dma=False):
        o = Op(eng, emit, dma)
        deps = []
        reads = _expand(reads)
        writes = _expand(writes)
        for b in reads:
            w = self.last_writer.get(b)
            if w is not None:
                deps.append(w)
        for b in writes:
            w = self.last_writer.get(b)
            if w is not None:
                deps.append(w)
            rs = self.readers.get(b)
            if rs:
                deps.extend(rs)
        for b in reads:
            self.readers.setdefault(b, []).append(o)
        for b in writes:
            self.last_writer[b] = o
            self.readers[b] = []
        seen = set()
        for d in deps:
            if id(d) not in seen and d is not o:
                seen.add(id(d))
                o.deps.append(d)
        self.ops.append(o)
        return o

    def finalize(self, stack):
        nc = self.nc
        per_eng = {e: [] for e in self.ENGS}
        for o in self.ops:
            per_eng[o.eng].append(o)
        for o in self.ops:
            for d in o.deps:
                if d.is_dma:
                    continue
                if d.eng == o.eng and (o.eng == "pe" or not SAME_ENGINE_SYNC) and not o.is_dma:
                    continue
                d.marked = True
        self.csem = {e: stack.enter_context(nc.semaphore("c_" + e)) for e in ("pe", "act", "dve", "pool")}
        dsems = {}
        for e in self.ENGS:
            ndma = sum(1 for o in per_eng[e] if o.is_dma)
            if ndma:
                dsems[e] = [stack.enter_context(nc.semaphore("d_%s_%d" % (e, i))) for i in range(min(ndma, N_DMA_SEMS))]
        for e in self.ENGS:
            cnt = 0
            j = 0
            for o in per_eng[e]:
                if o.is_dma:
                    o.dsem = dsems[e][j % N_DMA_SEMS]
                    o.dval = 16 * (j // N_DMA_SEMS + 1)
                    j += 1
                elif o.marked:
                    cnt += 1
                    o.semval = cnt
        self.per_eng = per_eng

    def emit_engine(self, ename, eng):
        waited = {}

        def wait(sem, val):
            key = id(sem)
            if waited.get(key, 0) >= val:
                return
            waited[key] = val
            eng.wait_ge(sem, val)

        for o in self.per_eng[ename]:
            for d in o.deps:
                if d.is_dma:
                    wait(d.dsem, d.dval)
                else:
                    if not d.marked:
                        continue
                    if d.eng == ename and not o.is_dma and (ename == "pe" or not SAME_ENGINE_SYNC):
                        continue
                    wait(self.csem[d.eng], d.semval)
            if o.is_dma and o.dval > 16:
                wait(o.dsem, o.dval - 16)
            if o.emit is None:
                continue
            ins = o.emit(eng)
            if o.is_dma:
                ins.then_inc(o.dsem, 16)
            elif o.marked:
                ins.then_inc(self.csem[ename], 1)

    def run_block(self):
        nc = self.nc
        with nc.Block() as block:
            @block.tensor
            def _(e):
                self.emit_engine("pe", e)

            @block.scalar
            def _(e):
                self.emit_engine("act", e)

            @block.vector
            def _(e):
                self.emit_engine("dve", e)

            @block.gpsimd
            def _(e):
                self.emit_engine("pool", e)

            @block.sync
            def _(e):
                self.emit_engine("sp", e)


class KS(list):
    pass


def _expand(keys):
    out = []
    for k in keys:
        if isinstance(k, KS):
            out.extend(k)
        else:
            out.append(k)
    return out


class AliasRot:
    def __init__(self, views, keysets):
        self.bufs, self.keys, self.i = views, keysets, 0

    def next(self):
        b, k = self.bufs[self.i], self.keys[self.i]
        self.i = (self.i + 1) % len(self.bufs)
        return b, k


class Rot:
    def __init__(self, K, name, n, shape, dt):
        self.bufs = [K.T("%s%d" % (name, i), shape, dt) for i in range(n)]
        self.keys = ["%s%d" % (name, i) for i in range(n)]
        self.i = 0

    def next(self):
        b, k = self.bufs[self.i], self.keys[self.i]
        self.i = (self.i + 1) % len(self.bufs)
        return b, k


class WSpec:
    def __init__(self, name, src2d, K, N, w, gk, swap=False):
        self.name, self.src, self.K, self.N, self.w, self.gk, self.swap = name, src2d, K, N, w, gk, swap
        self.ncb = N // w
        self.nkg = K // (128 * gk)
        assert self.ncb * w == N and self.nkg * 128 * gk == K


class KB:
    def __init__(self, cfg):
        self.cfg = cfg
        self.nc = bass.Bass("TRN2", target_bir_lowering=False)
        self.st = contextlib.ExitStack()
        self.P = Prog(self.nc)
        self.uid = 0
        self.bank_i = {}
        self.outs = []

    def T(self, name, shape, dt=F32):
        return self.st.enter_context(self.nc.sbuf_tensor(name, shape, dt))

    def dram(self, name, shape, dt, kind="Internal"):
        return self.nc.dram_tensor(name, shape, dt, kind=kind).ap()

    def key(self, p):
        self.uid += 1
        return "%s#%d" % (p, self.uid)

    def dma(self, out, in_, reads, writes, eng="sp"):
        return self.P.op(eng, lambda e: e.dma_start(out=out, in_=in_), reads, writes, dma=True)

    def mm(self, out, lhsT, rhs, start, stop, reads, writes, **kw):
        return self.P.op("pe", lambda e: e.matmul(out, lhsT=lhsT, rhs=rhs, start=start, stop=stop, **kw), reads, writes)

    def act(self, out, in_, func, reads, writes, **kw):
        return self.P.op("act", lambda e: e.activation(out=out, in_=in_, func=func, **kw), reads, writes)

    def tt(self, eng, out, in0, in1, op, reads, writes):
        return self.P.op(eng, lambda e: e.tensor_tensor(out=out, in0=in0, in1=in1, op=op), reads, writes)

    def ts(self, eng, out, in0, s1, s2, op0, op1, reads, writes):
        if op1 is None:
            return self.P.op(eng, lambda e: e.tensor_scalar(out=out, in0=in0, scalar1=s1, scalar2=None, op0=op0), reads, writes)
        return self.P.op(eng, lambda e: e.tensor_scalar(out=out, in0=in0, scalar1=s1, scalar2=s2, op0=op0, op1=op1), reads, writes)

    def stt(self, eng, out, in0, scalar, in1, op0, op1, reads, writes):
        return self.P.op(eng, lambda e: e.scalar_tensor_tensor(out=out, in0=in0, scalar=scalar, in1=in1, op0=op0, op1=op1), reads, writes)

    def copy(self, eng, out, in_, reads, writes):
        if eng == "act":
            return self.act(out, in_, AF.Copy, reads, writes)
        return self.P.op(eng, lambda e: e.tensor_copy(out=out, in_=in_), reads, writes)

    def recip(self, out, in_, reads, writes):
        return self.P.op("dve", lambda e: e.reciprocal(out=out, in_=in_), reads, writes)

    def bank(self, role, choices):
        i = self.bank_i.get(role, 0)
        self.bank_i[role] = i + 1
        b = choices[i % len(choices)]
        return self.ps[b], "ps%d" % b

    def build(self):
        nc, cfg = self.nc, self.cfg
        IN = lambda n, s, dt=F32: self.dram(n, s, dt, "ExternalInput")
        if cfg.get("launch", 0) in (0, 1):
            self.xT = IN("xT", [D, NT])
        self.w_gu = [IN("ffn1_w_gu", [L, D, 2 * FF]), IN("ffn2_w_gu", [L, D, 2 * FF])]
        self.w_dn = [IN("ffn1_w_down", [L, FF, D]), IN("ffn2_w_down", [L, FF, D])]
        self.w_in = IN("w_in", [L, D, 6912])
        self.w_br = IN("w_branch", [L, 3, 512, D])
        self.w_o = IN("w_out", [L, D, D])
        self.gcols_d = IN("gcols", [128, 9 * L * 8])
        self.bgate_d = IN("bgate", [128, L * 3 * 8])
        self.gsub_d = IN("gsub", [128, L])
        self.sink_d = IN("sink", [L * 8])
        self.lamv_d = IN("lamv", [4 * L * 64])
        self.rpbT_d = IN("rpbT", [L, 8, 64, 15 * 64])
        self.cmask_d = IN("cmask", [64, 64])
        self.cosT_d = IN("cosT", [128, NT])
        self.sinT_d = IN("sinT", [128, NT])
        self.strip_d = IN("strip", [128, 1152])
        self.sel_d = IN("sel", [128, 4])
        self.selb_d = IN("selb", [2, 256])
        self.flags_d = IN("flags", [128, 2])
        launch = cfg.get("launch", 0)
        self.launch = launch
        OUT = lambda n, s, dt=F32: self.dram(n, s, dt, "ExternalOutput")
        self.kvloc = [None] * L
        self.kvall = [None] * L
        if launch == 0:
            self.outT = OUT("outT", [D, NT // 2 if ROT else NT])
            self.hbuf = self.dram("hbuf", [D, NT], F32)
            self.kvloc = [self.dram("kvloc%d" % l, [KV_ROWS, NT], BF16) for l in range(L)]
            if NRANK == 2:
                self.kvall = [self.dram("kvall%d" % l, [2 * KV_ROWS, NT], BF16) for l in range(L)]
        else:
            self.lB = {2: 0, 3: 1}.get(launch)
            self.lA = {1: 0, 2: 1}.get(launch)
            if cfg.get("skipA"):
                self.lA = None
            if launch > 1:
                self.h_in = IN("h_in", [D, NT])
            self.h_out = OUT("outT" if launch == 3 else "h_out", [D, NT])
            if self.lA is not None:
                self.kvloc[self.lA] = OUT("kvloc_out", [KV_ROWS, NT], BF16)
            if self.lB is not None:
                self.kvall[self.lB] = IN("kvall_in", [2 * KV_ROWS, NT], BF16)
                self.kvloc[self.lB] = IN("kvloc_in", [KV_ROWS, NT], BF16)
        self.ebt = [self.dram("ebt%d" % l, [8, 64, 15 * 64], BF16) for l in range(L)]
        psA = self.st.enter_context(nc.psum_tensor("psA", [128, 2048], F32))
        self.ps = [psA[:, i * 512:(i + 1) * 512] for i in range(4)]
        self.ps += [self.st.enter_context(nc.psum_tensor("ps%d" % i, [128, 512], F32))[:] for i in range(4, 8)]
        self.ps2 = [(psA[:, 0:1024], KS(["ps0", "ps1"])), (psA[:, 1024:2048], KS(["ps2", "ps3"]))]
        self.ps2_i = 0
        self.ones_f = self.T("ones_f", [128, 128], F32)
        self.ones_b = self.T("ones_b", [128, 128], BF16)
        self.gcols = self.T("gcols_s", [128, 9 * L * 8], F32)
        self.nbg = self.T("nbg", [128, L * 3 * 8], F32)
        self.gsub = self.T("gsub_s", [128, L], F32)
        self.esink = self.T("esink", [128, L * 8], F32)
        self.lamt = self.T("lamt", [128, 8 * L], F32)
        self.lamc = self.T("lamc", [128, L], F32)
        self.nlam = self.T("nlam", [128, L], F32)
        self.dacc = self.T("dacc", [128, 2 * TB], F32)
        self.e2 = Rot(self, "e2b", 3, [128, 2 * TB], BF16)
        self.strip = self.T("strip_s", [128, 1152], BF16)
        self.sel = self.T("sel_s", [128, 4], BF16)
        self.selb = self.T("selb_s", [2, 256], F32)
        self.cmask = self.T("cmask_s", [64, 64], F32)
        self.eps_t = self.T("eps_t", [128, 1], F32)
        self.one_t = self.T("one_t", [128, 1], F32)
        self.flags = self.T("flags_s", [128, 2], F32)
        self.hb = self.T("hb", [128, 8, TB], F32)
        self.xnT = self.T("xnT", [128, 8, TB], BF16)
        self.hT = self.T("hT", [128, 22, TB], BF16)
        self.ysb = self.T("ysb", [128, 8, TB], F32)
        self.cosT = self.T("cosT_s", [128, TB], F32)
        self.sinT = self.T("sinT_s", [128, TB], F32)
        self.wslot = Rot(self, "wslot", 4, [128, 2048], BF16)
        self.f32s = Rot(self, "f32s", 3, [128, TB], F32)
        self.bfs = Rot(self, "bfs", 4, [128, TB], BF16)
        self.rstd = Rot(self, "rstd", 2, [128, TB], F32)
        self.dfb = Rot(self, "dfb", 4, [128, TB], F32)
        self.sqb = Rot(self, "sqb", 2, [128, TB], BF16)
        self.pm = Rot(self, "pm", 4, [128, TB], BF16)
        self.kst = Rot(self, "kst", 2, [128, TB], BF16)
        self.vst = Rot(self, "vst", 1, [128, 4, 512], BF16)
        self.cin = AliasRot([self.ysb[:, 0:4, :].rearrange("p a b -> p (a b)"), self.ysb[:, 4:8, :].rearrange("p a b -> p (a b)")],
                            [KS([("ysb", c) for c in range(0, 4)]), KS([("ysb", c) for c in range(4, 8)])])
        self.cout = AliasRot([self.hT[:, 0:4, :].rearrange("p a b -> p (a b)"), self.hT[:, 4:8, :].rearrange("p a b -> p (a b)")],
                             [KS([("hT", c) for c in range(0, 4)]), KS([("hT", c) for c in range(4, 8)])])
        self.qT = self.T("qT", [128, 4, TB], BF16)
        self.oT = [self.T("oT%d" % i, [128, 4, TB], BF16) for i in range(3)]
        self.mT = self.T("mT", [128, 8, TB], BF16)
        self.macc = self.T("macc", [128, 2, TB], F32)
        self.kTA = self.T("kTA", [128, 4, 1024], BF16)
        self.vA = self.T("vA", [128, 16, 512], BF16)
        self.ebh = Rot(self, "ebh", 2, [128, 15 * 64], BF16)
        self.kTB = self.T("kTB", [128, 2, 768], BF16)
        self.vB = self.T("vB", [128, 6, 128], BF16)
        self.kTC = Rot(self, "kTC", 2, [128, 1024], BF16)
        self.vC = Rot(self, "vC", 2, [128, 8, 128], BF16)

        self.kvkeys = {i: [] for i in range(L)}
        self.setup_consts()
        nl = cfg.get("layers", L)
        nblk = cfg.get("nblk", NBLK)
        stop_after = cfg.get("stop_after", None)
        if launch == 0:
            self.setup_weights(list(range(nl)), list(range(nl)))
            for blk in range(nblk):
                self.load_h(blk, self.xT)
                self.stageA(0, blk)
                self.store_h(blk, self.hbuf)
            for l in range(nl):
                self.allgather(l)
                if stop_after == ("A", l):
                    break
                for blk in range(nblk // 2 if (ROT and l == nl - 1) else nblk):
                    self.load_h(blk, self.hbuf)
                    self.stageB(l, blk)
                    last = (l == nl - 1)
                    if not last:
                        self.stageA(l + 1, blk)
                    self.store_h(blk, self.outT if (last and not cfg.get("debug")) else self.hbuf)
        else:
            self.setup_weights([] if self.lB is None else [self.lB], [] if self.lA is None else [self.lA])
            for blk in range(nblk):
                self.load_h(blk, self.xT if launch == 1 else self.h_in)
                if self.lB is not None:
                    self.stageB(self.lB, blk)
                if self.lA is not None:
                    self.stageA(self.lA, blk)
                self.store_h(blk, self.h_out)
        if cfg.get("debug") and launch == 0:
            dh = self.dram("dbg_h", [D, NT], F32, "ExternalOutput")
            self.dma(dh, self.hbuf, [("hbuf", b) for b in range(nblk)], ["dbg_h"], eng="pool")
            self.outs.append("dbg_h")
            dk = self.dram("dbg_kv", [2 * KV_ROWS, NT], BF16, "ExternalOutput")
            self.dma(dk, self.kvall[0], [("kvall", 0)], ["dbg_kv"], eng="pool")
            self.outs.append("dbg_kv")
        self.finish()
        return nc

    def setup_consts(self):
        P = self.P
        P.op("pool", lambda e: e.memset(self.ones_f[:], 1.0), writes=["ones_f"])
        P.op("pool", lambda e: e.memset(self.ones_b[:], 1.0), writes=["ones_b"])
        P.op("pool", lambda e: e.memset(self.eps_t[:], EPS), writes=["eps_t"])
        P.op("pool", lambda e: e.memset(self.one_t[:], 1.0), writes=["one_t"])
        self.dma(self.flags[:], self.flags_d, [], ["flags"])
        self.dma(self.gcols[:], self.gcols_d, [], ["gcols"])
        for l in range(L):
            for which in (1, 8):
                c0 = (l * 9 + which) * 8
                self.ts("dve", self.gcols[:, c0:c0 + 8], self.gcols[:, c0:c0 + 8], 0.5, None, ALU.mult, None, ["gcols"], ["gcols"])
        self.dma(self.nbg[:], self.bgate_d, [], ["nbg"])
        self.ts("dve", self.nbg[:], self.nbg[:], -1.0, None, ALU.mult, None, ["nbg"], ["nbg"])
        self.dma(self.gsub[:], self.gsub_d, [], ["gsub"])
        for l in range(L):
            lam_init = 0.8 - 0.6 * float(np.exp(-0.3 * l))
            self.ts("dve", self.gsub[:, l:l + 1], self.gsub[:, l:l + 1], 1.0 - lam_init, None, ALU.mult, None, ["gsub"], ["gsub"])
        self.dma(self.esink[:], self.sink_d.partition_broadcast(128), [], ["esink"])
        self.act(self.esink[:], self.esink[:], AF.Exp, ["esink"], ["esink"])
        lrt, lrk = self.cin.next()
        self.dma(lrt[:, 0:4 * L * 64], self.lamv_d.partition_broadcast(128), [], [lrk])
        lr = lrt
        n = L * 64
        for l in range(L):
            for pair in range(2):
                q = lr[:, (2 * pair) * n + l * 64:(2 * pair) * n + (l + 1) * 64]
                k = lr[:, (2 * pair + 1) * n + l * 64:(2 * pair + 1) * n + (l + 1) * 64]
                col = l * 4 + pair
                tmp, tk = self.f32s.next()
                self.tt("dve", tmp[:, 0:64], q, k, ALU.mult, [lrk], [tk])
                self.P.op("act", (lambda e, tmp=tmp, col=col: e.activation(out=tmp[:, 64:128], in_=tmp[:, 0:64], func=AF.Copy, accum_out=self.lamt[:, col:col + 1])), [tk], [tk, "lamt"])
            self.act(self.lamt[:, l * 4:l * 4 + 2], self.lamt[:, l * 4:l * 4 + 2], AF.Exp, ["lamt"], ["lamt"])
            lam_init = 0.8 - 0.6 * float(np.exp(-0.3 * l))
            self.tt("dve", self.lamt[:, l * 4 + 2:l * 4 + 3], self.lamt[:, l * 4:l * 4 + 1], self.lamt[:, l * 4 + 1:l * 4 + 2], ALU.subtract, ["lamt"], ["lamt"])
            self.ts("dve", self.lamc[:, l:l + 1], self.lamt[:, l * 4 + 2:l * 4 + 3], -1.0, -lam_init, ALU.mult, ALU.add, ["lamt"], ["lamc"])
            self.ts("dve", self.nlam[:, l:l + 1], self.lamt[:, l * 4 + 2:l * 4 + 3], -1.0, -lam_init, ALU.mult, ALU.add, ["lamt"], ["nlam"])
            self.P.op("dve", (lambda e, l=l: e.memset(self.lamc[0:1, l:l + 1], 1.0)), ["lamc"], ["lamc"])
        t, tk = self.cin.next()
        self.dma(t[:, 0:1152], self.strip_d, [], [tk])
        self.copy("dve", self.strip[:], t[:, 0:1152], [tk], ["strip"])
        t2, tk2 = self.cin.next()
        self.dma(t2[:, 0:4], self.sel_d, [], [tk2])
        self.copy("dve", self.sel[:], t2[:, 0:4], [tk2], ["sel"])
        self.dma(self.selb[:], self.selb_d, [], ["selb"])
        self.dma(self.cmask[:], self.cmask_d, [], ["cmask"])
        for l in range(L):
            for h in range(8):
                t, tk = self.cin.next()
                self.dma(t[0:64, 0:960], self.rpbT_d[l, h], [], [tk])
                self.act(t[0:64, 0:960], t[0:64, 0:960], AF.Exp, [tk], [tk])
                o, ok = self.cout.next()
                self.tt("dve", o[0:64, 0:960].rearrange("p (i c) -> p i c", c=64), t[0:64, 0:960].rearrange("p (i c) -> p i c", c=64),
                        self.cmask[:].unsqueeze(1).to_broadcast([64, 15, 64]), ALU.mult, [tk, "cmask"], [ok])
                self.dma(self.ebt[l][h], o[0:64, 0:960], [ok], [("ebt", l, h)], eng="pool")

    def conv_weight(self, spec):
        scr = self.dram("ws_" + spec.name, [spec.ncb, spec.nkg, 128, spec.gk * spec.w], BF16)
        spec.scr = scr
        n = spec.gk * spec.w
        for cb in range(spec.ncb):
            for kg in range(spec.nkg):
                src = spec.src[kg * spec.gk * 128:(kg + 1) * spec.gk * 128, cb * spec.w:(cb + 1) * spec.w].rearrange("(k p) n -> p k n", p=128)
                t, tk = self.cin.next()
                self.dma(t[:, 0:n].rearrange("p (k n) -> p k n", k=spec.gk), src, [], [tk])
                o, ok = self.cout.next()
                self.cv_i = getattr(self, "cv_i", 0) + 1
                eng = ("dve", "act", "pool")[self.cv_i % 3] if not spec.swap else ("dve", "act")[self.cv_i % 2]
                if not spec.swap:
                    self.copy(eng, o[:, 0:n], t[:, 0:n], [tk], [ok])
                else:
                    sv = t[:, 0:n].rearrange("p (a t d) -> p a t d", t=2, d=32)
                    dv = o[:, 0:n].rearrange("p (a t d) -> p a t d", t=2, d=32)
                    self.copy("dve", dv[:, :, 0, :], sv[:, :, 1, :], [tk], [ok])
                    self.copy("act", dv[:, :, 1, :], sv[:, :, 0, :], [tk], [ok])
                self.dma(scr[cb, kg], o[:, 0:n], [ok], [("ws", spec.name, cb, kg)], eng="pool")

    def setup_weights(self, layersB, layersA):
        self.W = {}
        for l in range(L):
            for f in range(2):
                s = WSpec("gu%d_%d" % (f, l), self.w_gu[f][l], D, 2 * FF, 256, 8)
                self.W[("gu", f, l)] = s
                s2 = WSpec("dn%d_%d" % (f, l), self.w_dn[f][l], FF, D, 128, 11)
                self.W[("dn", f, l)] = s2
            self.W[("in", l)] = WSpec("in_%d" % l, self.w_in[l], D, 6912, 256, 8)
            self.W[("insw", l)] = WSpec("insw_%d" % l, self.w_in[l][:, 1536:3328], D, 1792, 256, 8, swap=True)
            for i in range(3):
                self.W[("br", l, i)] = WSpec("br%d_%d" % (i, l), self.w_br[l, i], 512, D, 256, 4)
            self.W[("out", l)] = WSpec("out_%d" % l, self.w_o[l], D, D, 256, 8)
        order = []
        for l in sorted(set(layersA) | set(layersB)):
            if l in layersA:
                order += [("gu", 0, l), ("dn", 0, l)]
            order += [("in", l), ("insw", l)]
            if l in layersB:
                order += [("br", l, 0), ("br", l, 1), ("br", l, 2), ("out", l), ("gu", 1, l), ("dn", 1, l)]
        for k in order:
            self.conv_weight(self.W[k])

    def load_slot(self, spec, cb, kg):
        t, tk = self.wslot.next()
        n = spec.gk * spec.w
        self.dma(t[:, 0:n], spec.scr[cb, kg], [("ws", spec.name, cb, kg)], [tk])
        return t[:, 0:n].rearrange("p (k n) -> p k n", k=spec.gk), tk

    def load_h(self, blk, src_t):
        src = src_t.rearrange("(c p) t -> p c t", p=128)[:, :, blk * TB:(blk + 1) * TB]
        self.dma(self.hb[:], src, [("hbuf", blk)], [KS([("hb", c) for c in range(8)])])
        self.dma(self.cosT[:], self.cosT_d[:, blk * TB:(blk + 1) * TB], [], ["cosT"])
        self.dma(self.sinT[:], self.sinT_d[:, blk * TB:(blk + 1) * TB], [], ["sinT"])

    def store_h(self, blk, dst_t):
        dst = dst_t.rearrange("(c p) t -> p c t", p=128)[:, :, blk * TB:(blk + 1) * TB]
        final = dst_t is not getattr(self, "hbuf", None)
        self.dma(dst, self.hb[:], [KS([("hb", c) for c in range(8)])], [("out", blk) if final else ("hbuf", blk)], eng="pool")
        if final:
            self.outs.append(("out", blk))

    def finish(self):
        if self.launch != 0 and self.lA is not None:
            self.outs.extend(self.kvkeys[self.lA])
        self.P.op("sp", None, reads=self.outs)
        self.P.finalize(self.st)
        self.P.run_block()

    def stats_rstd(self, srcs, src_keys, inv_n, from_psum=False):
        psS, pk = self.bank("stat", [6, 7])
        n = len(srcs)
        for c, (s, sk) in enumerate(zip(srcs, src_keys)):
            sq, qk = self.sqb.next()
            self.act(sq[:], s, AF.Square, [sk], [qk])
            self.mm(psS[:], self.ones_b[:], sq[:], c == 0, c == n - 1, [qk, "ones_b"], [pk])
        ln, lk = self.rstd.next()
        self.act(ln[:], psS[:], AF.Ln, [pk, "eps_t"], [lk], scale=inv_n, bias=self.eps_t[:])
        self.act(ln[:], ln[:], AF.Exp, [lk], [lk], scale=-0.5)
        return ln, lk

    def norm_to_xnT(self, l, which):
        rstd, rk = self.stats_rstd([self.hb[:, c, :] for c in range(8)], [("hb", c) for c in range(8)], 1.0 / D)
        g0 = (l * 9 + which) * 8
        for c in range(8):
            self.stt("dve", self.xnT[:, c, :], self.hb[:, c, :], self.gcols[:, g0 + c:g0 + c + 1], rstd[:], ALU.mult, ALU.mult,
                     [("hb", c), "gcols", rk], [("xnT", c)])

    def post_norm_add(self, l, which):
        rstd, rk = self.stats_rstd([self.ysb[:, c, :] for c in range(8)], [("ysb", c) for c in range(8)], 1.0 / D)
        g0 = (l * 9 + which) * 8
        for c in range(8):
            t, tk = self.f32s.next()
            self.stt("dve", t[:], self.ysb[:, c, :], self.gcols[:, g0 + c:g0 + c + 1], rstd[:], ALU.mult, ALU.mult, [("ysb", c), "gcols", rk], [tk])
            self.tt("pool", self.hb[:, c, :], self.hb[:, c, :], t[:], ALU.add, [("hb", c), tk], [("hb", c)])

    def ffn(self, l, f):
        gu, dn = self.W[("gu", f, l)], self.W[("dn", f, l)]
        self.norm_to_xnT(l, 0 if f == 0 else 7)
        xk = [("xnT", c) for c in range(8)]
        for jj in range(11):
            gs, gk_ = self.load_slot(gu, jj, 0)
            us, uk_ = self.load_slot(gu, 11 + jj, 0)
            for sub in range(2):
                j = 2 * jj + sub
                pg, pgk = self.bank("ffn_g", [0, 1])
                pu, puk = self.bank("ffn_u", [2, 3])
                for kc in range(8):
                    self.mm(pg[:], gs[:, kc, sub * 128:(sub + 1) * 128], self.xnT[:, kc, :], kc == 0, kc == 7, [gk_, xk[kc]], [pgk])
                for kc in range(8):
                    self.mm(pu[:], us[:, kc, sub * 128:(sub + 1) * 128], self.xnT[:, kc, :], kc == 0, kc == 7, [uk_, xk[kc]], [puk])
                s, sk = self.f32s.next()
                self.act(s[:], pg[:], AF.Silu, [pgk], [sk])
                self.tt("dve", self.hT[:, j, :], s[:], pu[:], ALU.mult, [sk, puk], [("hT", j)])
        for c in range(8):
            py, pyk = self.bank("ffn_y", [4, 5])
            for kg in range(2):
                ws, wk = self.load_slot(dn, c, kg)
                for ki in range(11):
                    kc = kg * 11 + ki
                    self.mm(py[:], ws[:, ki, :], self.hT[:, kc, :], kc == 0, kc == 21, [wk, ("hT", kc)], [pyk])
            self.copy("act" if c % 2 == 0 else "dve", self.ysb[:, c, :], py[:], [pyk], [("ysb", c)])
        self.post_norm_add(l, 1 if f == 0 else 8)

    def proj_fm(self, l, cb_list, dst_fn, rope, xk):
        win, wsw = self.W[("in", l)], self.W[("insw", l)]
        i = 0
        for cb in cb_list:
            ws, wk = self.load_slot(win, cb, 0)
            if rope:
                ss, sk_ = self.load_slot(wsw, cb - 6, 0)
            for half in range(2):
                pk_, pkk = self.bank("pj_a", [0, 1])
                for kc in range(8):
                    self.mm(pk_[:], ws[:, kc, half * 128:(half + 1) * 128], self.xnT[:, kc, :], kc == 0, kc == 7, [wk, xk[kc]], [pkk])
                dst, dk, post = dst_fn(i)
                if dst is None:
                    i += 1
                    continue
                if rope:
                    ps2, ps2k = self.bank("pj_b", [2, 3])
                    for kc in range(8):
                        self.mm(ps2[:], ss[:, kc, half * 128:(half + 1) * 128], self.xnT[:, kc, :], kc == 0, kc == 7, [sk_, xk[kc]], [ps2k])
                    t1, t1k = self.f32s.next()
                    t2, t2k = self.f32s.next()
                    self.tt("dve", t1[:], pk_[:], self.cosT[:], ALU.mult, [pkk, "cosT"], [t1k])
                    self.tt("dve", t2[:], ps2[:], self.sinT[:], ALU.mult, [ps2k, "sinT"], [t2k])
                    self.tt("pool", dst, t1[:], t2[:], ALU.add, [t1k, t2k], [dk])
                else:
                    self.copy("act" if i % 2 == 0 else "dve", dst, pk_[:], [pkk], [dk])
                if post is not None:
                    post()
                i += 1

    def stageA(self, l, blk):
        if "ffn" not in self.cfg.get("skip", ()):
            self.ffn(l, 0)
        if "kv" in self.cfg.get("skip", ()):
            return
        self.norm_to_xnT(l, 2)
        xk = [("xnT", c) for c in range(8)]
        kv = self.kvloc[l]
        t0 = blk * TB
        if not hasattr(self, "kvkeys"):
            self.kvkeys = {i: [] for i in range(L)}

        def kdst_factory(row0, only0=False):
            def fn(i):
                if only0 and i > 0:
                    return None, None, None
                t, tk = self.kst.next()
                r0 = row0 + i * 128

                def post():
                    self.dma(kv[r0:r0 + 128, t0:t0 + TB], t[:], [tk], [self.key("kvw%d" % l)], eng="pool")
                    self.kvkeys[l].append("kvw%d#%d" % (l, self.uid))
                return t[:], tk, post
            return fn

        self.proj_fm(l, [2, 3], kdst_factory(KA0), False, xk)
        self.proj_fm(l, [8], kdst_factory(KB0, True), True, xk)
        self.proj_fm(l, [11, 12], kdst_factory(KC0), True, xk)
        win = self.W[("in", l)]
        for (cbs, vrow0, ncols) in (([4, 5], VA0, 512), ([8], VB0, 128), ([13, 14], VC0, 512)):
            vt, vk = self.vst.next()
            for ci, cb in enumerate(cbs):
                ws, wk = self.load_slot(win, cb, 0)
                for tt_ in range(4):
                    pv, pvk = self.bank("pj_v", [4, 5])
                    if ncols == 128:
                        for kc in range(8):
                            self.mm(pv[:, 0:128], self.xnT[:, kc, tt_ * 128:(tt_ + 1) * 128], ws[:, kc, 128:256], kc == 0, kc == 7, [wk, xk[kc]], [pvk])
                        self.copy("act" if tt_ % 2 == 0 else "dve", vt[:, tt_, 0:128], pv[:, 0:128], [pvk], [vk])
                    else:
                        for kc in range(8):
                            self.mm(pv[:, 0:256], self.xnT[:, kc, tt_ * 128:(tt_ + 1) * 128], ws[:, kc, :], kc == 0, kc == 7, [wk, xk[kc]], [pvk])
                        self.copy("act" if tt_ % 2 == 0 else "dve", vt[:, tt_, ci * 256:(ci + 1) * 256], pv[:, 0:256], [pvk], [vk])
            nrows = ncols * NT // NT
            view = kv[vrow0:vrow0 + ncols, :].rearrange("r (q c) -> (r q) c", c=ncols)
            dstv = view[t0:t0 + TB, :].rearrange("(t p) c -> p t c", p=128)
            self.dma(dstv, vt[:, :, 0:ncols], [vk], [self.key("kvw%d" % l)], eng="pool")
            self.kvkeys[l].append("kvw%d#%d" % (l, self.uid))

    def allgather(self, l):
        if NRANK == 1:
            return
        if self.cfg.get("no_ag"):
            self.dma(self.kvall[l][0:KV_ROWS, :], self.kvloc[l], list(self.kvkeys[l]), [("kvall", l)], eng="pool")
            return
        self.P.op("pool", lambda e: e.collective_compute("AllGather", ALU.bypass, replica_groups=[[0, 1], [2, 3], [4, 5], [6, 7]],
                                                         ins=[self.kvloc[l]], outs=[self.kvall[l]]),
                  reads=list(self.kvkeys[l]), writes=[("kvall", l)], dma=True)

    def setup_layer_tables(self, l):
        pass

    def kv_rows_tok(self, l, row0, nrows, tok_lo, tok_hi):
        out = []
        for r in range(2):
            lo, hi = max(tok_lo, r * NT), min(tok_hi, (r + 1) * NT)
            if lo < hi:
                out.append((self.kvall[l][r * KV_ROWS + row0:r * KV_ROWS + row0 + nrows, lo - r * NT:hi - r * NT], lo - tok_lo, hi - lo))
        return out

    def v_view(self, l, r, vrow0, ncols):
        return self.kvall[l][r * KV_ROWS + vrow0:r * KV_ROWS + vrow0 + ncols, :].rearrange("r (q c) -> (r q) c", c=ncols)

    def na_valid(self, half, R0, krl, j):
        if NRANK == 1 and ROT:
            tq = (R0 + j + 64 * half) % 128
            tk = ((krl % 128) + 64 * half) % 128
            if tk - tq != krl - (R0 + j):
                return False
            ws = min(max(tq - 4, 0), 120)
            return ws <= tk < ws + 8
        if NRANK == 1:
            half = 0
        qr = R0 + j + (NT // 64) * half
        kr = krl + (NT // 64) * half
        ws = min(max(qr - 4, 0), 120)
        return (0 <= kr <= 127) and (ws <= kr < ws + 8)

    def attn_na(self, l, blk):
        R0 = 8 * blk
        NR_ = NT // 64
        own_lo, own_hi = max(R0 - 4, 0), min(R0 + 12, NR_)
        kvl = self.kvloc[l]
        pieces = []
        if R0 - 4 < 0 and NRANK == 2:
            pieces.append(("prev", R0 - 4, 0))
        pieces.append(("own", own_lo, own_hi))
        if R0 + 12 > NR_ and NRANK == 2:
            pieces.append(("next", NR_, R0 + 12))
        if NRANK == 1 and ROT:
            if R0 - 4 < 0:
                pieces.append(("wrap", R0 - 4, 0))
            if R0 + 12 > NR_:
                pieces.append(("wrap", NR_, R0 + 12))
        for kind, lo, hi in pieces:
            s0 = lo - (R0 - 4)
            n = (hi - lo) * 64
            if kind == "wrap":
                wl, wh = lo % NR_, (hi - 1) % NR_ + 1
                ksrc = kvl[KA0:KA0 + 512, wl * 64:wh * 64]
                vsrc = kvl[VA0:VA0 + 512, :].rearrange("r (q c) -> (r q) c", c=512)[wl * 64:wh * 64, :]
                rk = list(self.kvkeys[l])
            elif kind == "own":
                ksrc = kvl[KA0:KA0 + 512, lo * 64:hi * 64]
                vsrc = kvl[VA0:VA0 + 512, :].rearrange("r (q c) -> (r q) c", c=512)[lo * 64:hi * 64, :]
                rk = list(self.kvkeys[l])
            elif kind == "prev":
                ksrc = self.kvall[l][KA0:KA0 + 512, NT + lo * 64:NT + hi * 64]
                vsrc = self.v_view(l, 0, VA0, 512)[NT + lo * 64:NT + hi * 64, :]
                rk = [("kvall", l)]
            else:
                ksrc = self.kvall[l][KV_ROWS + KA0:KV_ROWS + KA0 + 512, (lo - NR_) * 64:(hi - NR_) * 64]
                vsrc = self.v_view(l, 1, VA0, 512)[(lo - NR_) * 64:(hi - NR_) * 64, :]
                rk = [("kvall", l)]
            self.dma(self.kTA[:, :, s0 * 64:s0 * 64 + n], ksrc.rearrange("(c p) t -> p c t", p=128), rk, ["kTA"])
            for ph in range(2):
                self.dma(self.vA[ph * 64:(ph + 1) * 64, s0:s0 + (hi - lo), :], vsrc.rearrange("(r p) c -> p r c", p=64), rk, ["vA"])
        segs = []
        for s_ in range(16):
            krl = R0 - 4 + s_
            cats = [(self.na_valid(0, R0, krl, j), self.na_valid(1, R0, krl, j)) for j in range(8)]
            j = 0
            while j < 8:
                if cats[j] == (False, False):
                    j += 1
                    continue
                j1 = j
                while j1 + 1 < 8 and cats[j1 + 1] == cats[j]:
                    j1 += 1
                segs.append((s_, krl, j, j1, cats[j]))
                j = j1 + 1
        for ch in range(4):
            eb, ebk = self.ebh.next()
            for ph in range(2):
                self.dma(eb[ph * 64:(ph + 1) * 64, :], self.ebt[l][2 * ch + ph], [("ebt", l, 2 * ch + ph)], [ebk])
            ebv = eb[:].rearrange("p (i c) -> p i c", c=64)
            pO = [self.ps[2], self.ps[3]]
            pOk = ["ps2", "ps3"]
            pR = [self.ps[4], self.ps[5]]
            pRk = ["ps4", "ps5"]

            def na_S(si, ch=ch):
                s_, krl, j0, j1, cat = segs[si]
                nq = (j1 - j0 + 1) * 64
                pS, pSk = self.bank("att_s", [0, 1, 6, 7])
                for ph in range(2):
                    b = ph * 64
                    self.mm(pS[b:b + 64, 0:nq], self.kTA[b:b + 64, ch, s_ * 64:(s_ + 1) * 64], self.qT[b:b + 64, ch, j0 * 64:(j1 + 1) * 64],
                            True, True, ["kTA", ("qT", ch)], [pSk])
                return pS, pSk

            pend, nemit = [], 0
            for si, (s_, krl, j0, j1, cat) in enumerate(segs):
                while nemit < len(segs) and nemit <= si + LOOKAHEAD:
                    pend.append(na_S(nemit))
                    nemit += 1
                pS, pSk = pend.pop(0)
                nq = (j1 - j0 + 1) * 64
                c0, c1 = j0 * 64, (j1 + 1) * 64
                e, ek = self.bfs.next()
                self.act(e[:, 0:nq], pS[:, 0:nq], AF.Exp, [pSk], [ek], scale=0.125)
                pm, pmk = self.pm.next()
                idx0 = 7 - krl + R0 + j0
                ev = e[:, 0:nq].rearrange("p (i c) -> p i c", c=64)
                pv = pm[:, 0:nq].rearrange("p (i c) -> p i c", c=64)
                tb = ebv[:, idx0:idx0 + (j1 - j0 + 1), :]
                if cat == (True, True):
                    self.tt("dve", pv, ev, tb, ALU.mult, [ek, ebk], [pmk])
                else:
                    fl = self.flags[:, 1:2] if cat == (True, False) else self.flags[:, 0:1]
                    self.stt("dve", pv, ev, fl, tb, ALU.mult, ALU.mult, [ek, ebk, "flags"], [pmk])
                last = si == len(segs) - 1
                for ph in range(2):
                    b = ph * 64
                    h = 2 * ch + ph
                    self.mm(pO[ph][0:64, c0:c1], self.vA[b:b + 64, s_, h * 64:(h + 1) * 64], pm[b:b + 64, 0:nq], si == 0, last, ["vA", pmk], [pOk[ph]], skip_group_check=True)
                    self.mm(pR[ph][0:64, c0:c1], self.ones_b[b:b + 64, 0:64], pm[b:b + 64, 0:nq], si == 0, last, ["ones_b", pmk], [pRk[ph]], skip_group_check=True)
            for ph in range(2):
                b = ph * 64
                r, rk_ = self.f32s.next()
                self.recip(r[0:64, :], pR[ph][0:64, :], [pRk[ph]], [rk_])
                self.tt("dve", self.oT[0][b:b + 64, ch, :], pO[ph][0:64, :], r[0:64, :], ALU.mult, [pOk[ph], rk_], [("oT0", ch, ph)])

    def attn_sw(self, l, blk):
        kvl = self.kvloc[l]
        T0 = blk * 4 - 1
        tiles = []
        vview = kvl[VB0:VB0 + 128, :].rearrange("r (q c) -> (r q) c", c=128)
        rk = list(self.kvkeys[l])
        rot_mode = (NRANK == 1 and ROT)
        if rot_mode:
            ntl = NT // 128
            tls = []
            for s_ in range(6):
                Tl = (T0 + s_) % ntl
                v = []
                for half in (0, 1):
                    tq0 = (4 * blk + 32 * half) % 64
                    tk = (Tl + 32 * half) % 64
                    v.append(tk - tq0 == s_ - 1)
                tiles.append({(True, True): "own", (False, True): "prev", (True, False): "next", (False, False): "none"}[tuple(v)])
                tls.append(Tl)
            s_ = 0
            while s_ < 6:
                e_ = s_
                while e_ + 1 < 6 and tls[e_ + 1] == tls[e_] + 1:
                    e_ += 1
                lo, hi = tls[s_], tls[e_] + 1
                for kvh in range(2):
                    for ph in range(2):
                        self.dma(self.kTB[ph * 64:(ph + 1) * 64, kvh, s_ * 128:(e_ + 1) * 128], kvl[KB0 + kvh * 64:KB0 + (kvh + 1) * 64, lo * 128:hi * 128], rk, ["kTB"])
                self.dma(self.vB[:, s_:e_ + 1, :], vview[lo * 128:hi * 128, :].rearrange("(t p) c -> p t c", p=128), rk, ["vB"])
                s_ = e_ + 1
        else:
            for s_ in range(6):
                T = T0 + s_
                kind = "prev" if T < 0 else ("next" if T >= NT // 128 else "own")
                if NRANK == 1 and kind != "own":
                    kind = "none"
                tiles.append(kind)
            own_s = [s_ for s_ in range(6) if tiles[s_] == "own"]
            lo, hi = T0 + own_s[0], T0 + own_s[-1] + 1
            for kvh in range(2):
                for ph in range(2):
                    self.dma(self.kTB[ph * 64:(ph + 1) * 64, kvh, own_s[0] * 128:(own_s[-1] + 1) * 128], kvl[KB0 + kvh * 64:KB0 + (kvh + 1) * 64, lo * 128:hi * 128], rk, ["kTB"])
            self.dma(self.vB[:, own_s[0]:own_s[-1] + 1, :], vview[lo * 128:hi * 128, :].rearrange("(t p) c -> p t c", p=128), rk, ["vB"])
        for s_ in range(6):
            if rot_mode or tiles[s_] in ("own", "none"):
                continue
            r = 0 if tiles[s_] == "prev" else 1
            t_lo = NT - 128 if r == 0 else 0
            for kvh in range(2):
                for ph in range(2):
                    self.dma(self.kTB[ph * 64:(ph + 1) * 64, kvh, s_ * 128:(s_ + 1) * 128],
                             self.kvall[l][r * KV_ROWS + KB0 + kvh * 64:r * KV_ROWS + KB0 + (kvh + 1) * 64, t_lo:t_lo + 128], [("kvall", l)], ["kTB"])
            self.dma(self.vB[:, s_, :], self.v_view(l, r, VB0, 128)[t_lo:t_lo + 128, :], [("kvall", l)], ["vB"])
        for h in range(8):
            ch, b, kvh = h // 2, (h % 2) * 64, h // 4
            pO, pOk = self.bank("att_o", [2, 3])
            pR, pRk = self.bank("att_r", [4, 5])
            proc = [s_ for s_ in range(6) if tiles[s_] != "none"]

            def sw_S(pi, ch=ch, b=b, kvh=kvh):
                s_ = proc[pi]
                pS, pSk = self.bank("att_s", [0, 1, 6, 7])
                self.mm(pS, self.kTB[b:b + 64, kvh, s_ * 128:(s_ + 1) * 128], self.qT[b:b + 64, ch, :], True, True, ["kTB", ("qT", ch)], [pSk])
                return pS, pSk

            pend, nemit = [], 0
            for pi, s_ in enumerate(proc):
                while nemit < len(proc) and nemit <= pi + LOOKAHEAD:
                    pend.append(sw_S(nemit))
                    nemit += 1
                pS, pSk = pend.pop(0)
                e, ek = self.bfs.next()
                self.act(e[:], pS, AF.Exp, [pSk], [ek], scale=0.125)
                pm, pmk = self.pm.next()
                off = 512 - (s_ - 1) * 128
                if tiles[s_] == "own":
                    self.tt("dve", pm[:], e[:], self.strip[:, off:off + 512], ALU.mult, [ek, "strip"], [pmk])
                else:
                    fl = self.flags[:, 0:1] if tiles[s_] == "prev" else self.flags[:, 1:2]
                    self.stt("dve", pm[:], e[:], fl, self.strip[:, off:off + 512], ALU.mult, ALU.mult, [ek, "strip", "flags"], [pmk])
                self.mm(pO[0:64, :], self.vB[:, s_, kvh * 64:(kvh + 1) * 64], pm[:], s_ == proc[0], s_ == proc[-1], ["vB", pmk], [pOk])
                self.mm(pR[0:64, :], self.ones_b[:, 0:64], pm[:], s_ == proc[0], s_ == proc[-1], ["ones_b", pmk], [pRk])
            r, rk_ = self.f32s.next()
            self.ts("dve", r[0:64, :], pR[0:64, :], self.esink[0:64, l * 8 + h:l * 8 + h + 1], None, ALU.add, None, [pRk, "esink"], [rk_])
            self.recip(r[0:64, :], r[0:64, :], [rk_], [rk_])
            self.tt("dve", self.oT[1][b:b + 64, ch, :], pO[0:64, :], r[0:64, :], ALU.mult, [pOk, rk_], [("oT1", ch, h % 2)])

    def attn_diff(self, l, blk):
        npr = NT // 1024
        nchunks = S // 1024
        for h in range(4):
            pO = [self.ps[4], self.ps[5]]
            pOk = ["ps4", "ps5"]
            acc = self.dacc
            self.P.op("pool", lambda e: e.memset(self.dacc[:, 0:TB], 0.0), [], ["dacc"])
            pR1, pR1k = self.bank("stat", [6, 7])
            chunks = {}

            def load_chunk(kc8, h=h):
                r, tl = kc8 // npr, (kc8 % npr) * 1024
                kt_, ktk = self.kTC.next()
                vt_, vtk = self.vC.next()
                if NRANK == 1:
                    kvl_ = self.kvloc[l]
                    rk_ = list(self.kvkeys[l])
                    self.dma(kt_[:], kvl_[KC0 + h * 128:KC0 + (h + 1) * 128, tl:tl + 1024], rk_, [ktk])
                    vv_ = kvl_[VC0:VC0 + 512, :].rearrange("r (q c) -> (r q) c", c=512)
                    self.dma(vt_[:], vv_[tl:tl + 1024, h * 128:(h + 1) * 128].rearrange("(t p) c -> p t c", p=128), rk_, [vtk])
                else:
                    self.dma(kt_[:], self.kvall[l][r * KV_ROWS + KC0 + h * 128:r * KV_ROWS + KC0 + (h + 1) * 128, tl:tl + 1024], [("kvall", l)], [ktk])
                    self.dma(vt_[:], self.v_view(l, r, VC0, 512)[tl:tl + 1024, h * 128:(h + 1) * 128].rearrange("(t p) c -> p t c", p=128), [("kvall", l)], [vtk])
                chunks[kc8] = (kt_, ktk, vt_, vtk)

            items = [(kc8, kt) for kc8 in range(nchunks) for kt in range(8)]

            def emit_S(idx, h=h):
                kc8, kt = items[idx]
                if kt == 0:
                    load_chunk(kc8)
                kt_, ktk, vt_, vtk = chunks[kc8]
                pS2, pS2k = self.ps2[self.ps2_i % 2]
                self.ps2_i += 1
                for t in range(2):
                    self.mm(pS2[:, t * 512:(t + 1) * 512], kt_[t * 64:(t + 1) * 64, kt * 128:(kt + 1) * 128], self.qT[t * 64:(t + 1) * 64, h, :],
                            True, True, [ktk, ("qT", h)], [pS2k])
                return pS2, pS2k

            cur = emit_S(0)
            n_it = len(items)
            for idx in range(n_it):
                nxt = emit_S(idx + 1) if idx + 1 < n_it else None
                kc8, kt = items[idx]
                kt_, ktk, vt_, vtk = chunks[kc8]
                pS2, pS2k = cur
                e, ek = self.e2.next()
                self.act(e[:], pS2, AF.Exp, [pS2k], [ek], scale=0.125)
                self.tt("dve", acc[:, 0:TB], acc[:, 0:TB], e[:, 0:TB], ALU.add, ["dacc", ek], ["dacc"])
                for t in range(2):
                    self.mm(pO[t], vt_[:, kt, :], e[:, t * 512:(t + 1) * 512], idx == 0, idx == n_it - 1, [vtk, ek], [pOk[t]])
                self.mm(pR1, self.ones_b[:], e[:, TB:2 * TB], idx == 0, idx == n_it - 1, ["ones_b", ek], [pR1k])
                cur = nxt
            rs = []
            for t in range(2):
                if t == 0:
                    pR, pRk = self.bank("stat", [6, 7])
                    self.mm(pR, self.ones_f[:], acc[:, 0:TB], True, True, ["ones_f", "dacc"], [pRk])
                else:
                    pR, pRk = pR1, pR1k
                r_, rk_ = self.dfb.next()
                self.recip(r_[:], pR, [pRk], [rk_])
                rs.append((r_, rk_))
            o0, o0k = self.dfb.next()
            self.tt("dve", o0[:], pO[0], rs[0][0][:], ALU.mult, [pOk[0], rs[0][1]], [o0k])
            o1, o1k = self.dfb.next()
            self.tt("dve", o1[:], pO[1], rs[1][0][:], ALU.mult, [pOk[1], rs[1][1]], [o1k])
            self.stt("dve", o0[:], o1[:], self.nlam[:, l:l + 1], o0[:], ALU.mult, ALU.add, [o1k, o0k, "nlam"], [o0k])
            rstd, rk2 = self.stats_rstd([o0[:]], [o0k], 1.0 / 128)
            self.stt("dve", self.oT[2][:, h, :], o0[:], self.gsub[:, l:l + 1], rstd[:], ALU.mult, ALU.mult, [o0k, "gsub", rk2], [("oT2", h, 0), ("oT2", h, 1)])

    def merge_out(self, l):
        win, wo = self.W[("in", l)], self.W[("out", l)]
        xk = [("xnT", c) for c in range(8)]
        for cp in range(4):
            for i in range(3):
                gs, gk_ = self.load_slot(win, 15 + i * 4 + cp, 0)
                bs, bk_ = self.load_slot(self.W[("br", l, i)], cp, 0)
                for sub in range(2):
                    c = 2 * cp + sub
                    pY, pYk = self.bank("mg_y", [0, 1])
                    pG, pGk = self.bank("mg_g", [2, 3])
                    for kc in range(4):
                        self.mm(pY[:], bs[:, kc, sub * 128:(sub + 1) * 128], self.oT[i][:, kc, :], kc == 0, kc == 3, [bk_, ("oT%d" % i, kc, 0), ("oT%d" % i, kc, 1)], [pYk])
                    for kc in range(8):
                        self.mm(pG[:], gs[:, kc, sub * 128:(sub + 1) * 128], self.xnT[:, kc, :], kc == 0, kc == 7, [gk_, xk[kc]], [pGk])
                    g, gk2 = self.f32s.next()
                    col = (l * 3 + i) * 8 + c
                    self.act(g[:], pG[:], AF.Exp, [pGk, "nbg"], [gk2], scale=-1.0, bias=self.nbg[:, col:col + 1])
                    self.act(g[:], g[:], AF.Ln, [gk2, "one_t"], [gk2], scale=1.0, bias=self.one_t[:])
                    self.act(g[:], g[:], AF.Exp, [gk2], [gk2], scale=-1.0)
                    if i == 0:
                        self.tt("dve", self.macc[:, sub, :], pY[:], g[:], ALU.mult, [pYk, gk2], [("macc", sub)])
                    elif i == 1:
                        self.tt("dve", g[:], pY[:], g[:], ALU.mult, [pYk, gk2], [gk2])
                        self.tt("dve", self.macc[:, sub, :], self.macc[:, sub, :], g[:], ALU.add, [("macc", sub), gk2], [("macc", sub)])
                    else:
                        self.tt("dve", g[:], pY[:], g[:], ALU.mult, [pYk, gk2], [gk2])
                        self.tt("dve", self.mT[:, c, :], self.macc[:, sub, :], g[:], ALU.add, [("macc", sub), gk2], [("mT", c)])
        for cb in range(4):
            ws, wk = self.load_slot(wo, cb, 0)
            for sub in range(2):
                c = 2 * cb + sub
                pM, pMk = self.bank("ffn_y", [4, 5])
                for kc in range(8):
                    self.mm(pM[:], ws[:, kc, sub * 128:(sub + 1) * 128], self.mT[:, kc, :], kc == 0, kc == 7, [wk, ("mT", kc)], [pMk])
                self.copy("act" if c % 2 == 0 else "dve", self.ysb[:, c, :], pM[:], [pMk], [("ysb", c)])
        self.post_norm_add(l, 6)

    def stageB(self, l, blk):
        cfg = self.cfg
        self.norm_to_xnT(l, 2)
        xk = [("xnT", c) for c in range(8)]
        parts = cfg.get("parts", "ABC")
        qdst = lambda i: (self.qT[:, i, :], ("qT", i), None)
        skip = cfg.get("skip", ())
        self.proj_fm(l, [0, 1], qdst, False, xk)
        if "na" not in skip:
            self.attn_na(l, blk)
        self.proj_fm(l, [6, 7], qdst, True, xk)
        if "sw" not in skip:
            self.attn_sw(l, blk)
        self.proj_fm(l, [9, 10], qdst, True, xk)
        if "diff" not in skip:
            self.attn_diff(l, blk)
        if cfg.get("dbg_o"):
            if not hasattr(self, "dbg_o"):
                self.dbg_o = [self.dram("dbg_o%d" % i, [512, NT], BF16, "ExternalOutput") for i in range(3)]
            for i in range(3):
                k_ = self.key("dbgo")
                self.dma(self.dbg_o[i].rearrange("(c p) t -> p c t", p=128)[:, :, blk * TB:(blk + 1) * TB], self.oT[i][:],
                         [("oT%d" % i, c, hh) for c in range(4) for hh in range(2)], [k_], eng="pool")
                self.outs.append(k_)
        if "merge" not in skip:
            self.merge_out(l)
        if "ffn" not in skip:
            self.ffn(l, 1)


def build_program(cfg):
    kb = KB(cfg)
    with kb.st:
        nc = kb.build()
    return nc


def _consts():
    pos = np.arange(S, dtype=np.float32)
    inv = (np.float32(10000.0) ** (-(np.arange(0, 64, 2, dtype=np.float32) / np.float32(64)))).astype(np.float32)
    ang = (pos[:, None] * inv[None, :]).astype(np.float32)
    ang = np.concatenate([ang, ang], axis=-1)
    cos = np.cos(ang).astype(np.float32)
    sin = np.sin(ang).astype(np.float32)
    sgn = np.concatenate([-np.ones(32, np.float32), np.ones(32, np.float32)])
    cosT = np.ascontiguousarray(np.concatenate([cos.T, cos.T], axis=0))
    sinT = np.ascontiguousarray(np.concatenate([(sin * sgn).T, (sin * sgn).T], axis=0))
    c = np.arange(64)
    col_start = np.clip(c - 8, 0, 48)
    col_ok = (c[None, :] >= col_start[:, None]) & (c[None, :] < col_start[:, None] + 16)
    cmask = np.ascontiguousarray(col_ok.T.astype(np.float32))
    dc = np.clip(c[:, None] - c[None, :] + 15, 0, 30)
    kl = np.arange(128)[:, None]
    xx = np.arange(1152)[None, :]
    strip = (np.abs(xx - 512 - kl) <= 128).astype(np.float32)
    sel = np.zeros((128, 4), np.float32)
    sel[:, 0] = 1.0
    sel[:, 3] = 1.0
    selb = np.zeros((2, 256), np.float32)
    selb[0, 0:128] = 1.0
    selb[1, 128:256] = 1.0
    return cosT, sinT, cmask, dc, strip, sel, selb


def _host_inputs(inputs):
    cosT, sinT, cmask, dc, strip, sel, selb = _consts()
    f = lambda a: np.ascontiguousarray(np.asarray(a, dtype=np.float32))
    gnames = {0: "ffn1_pre_g", 1: "ffn1_post_g", 2: "mix_pre_g", 6: "mix_post_g", 7: "ffn2_pre_g", 8: "ffn2_post_g"}
    gcols = np.zeros((128, 9 * L * 8), np.float32)
    for l in range(L):
        for which, nm in gnames.items():
            g = f(inputs[nm])[l]
            gcols[:, (l * 9 + which) * 8:(l * 9 + which) * 8 + 8] = g.reshape(8, 128).T
    bg = f(inputs["b_gate"])
    bgate = np.zeros((128, L * 3 * 8), np.float32)
    for l in range(L):
        for i in range(3):
            bgate[:, (l * 3 + i) * 8:(l * 3 + i) * 8 + 8] = bg[l, i].reshape(8, 128).T
    gsub = np.ascontiguousarray(f(inputs["diff_subln_g"]).T)
    sink = f(inputs["sw_sink"]).reshape(-1)
    lamv = np.concatenate([f(inputs[k]).reshape(-1) for k in ("diff_lambda_q1", "diff_lambda_k1", "diff_lambda_q2", "diff_lambda_k2")])
    rpb = f(inputs["na_rpb"])
    idx = np.arange(15)
    rpbT = rpb[:, :, (14 - idx)[None, :, None], dc[:, None, :]]
    rpbT = np.ascontiguousarray(rpbT.reshape(L, 8, 64, 15 * 64))
    shared = {
        "ffn1_w_gu": f(inputs["ffn1_w_gu"]), "ffn2_w_gu": f(inputs["ffn2_w_gu"]),
        "ffn1_w_down": f(inputs["ffn1_w_down"]), "ffn2_w_down": f(inputs["ffn2_w_down"]),
        "w_in": f(inputs["w_in"]), "w_branch": f(inputs["w_branch"]), "w_out": f(inputs["w_out"]),
        "gcols": gcols, "bgate": bgate, "gsub": gsub, "sink": sink, "lamv": lamv, "rpbT": rpbT,
        "cmask": cmask, "strip": strip, "sel": sel, "selb": selb,
    }
    x = f(inputs["x"])
    in_maps = []
    xts = {}
    for c in range(8):
        b, half = c // 2, (c % 2 if NRANK == 2 else 0)
        m = dict(shared)
        if NRANK == 1 and ROT:
            half = c % 2
            perm = (np.arange(S) + half * (S // 2)) % S
            if (b, half) not in xts:
                xts[(b, half)] = (np.ascontiguousarray(x[b][perm].T), np.ascontiguousarray(cosT[:, perm]), np.ascontiguousarray(sinT[:, perm]))
            m["xT"], m["cosT"], m["sinT"] = xts[(b, half)]
            fl = np.zeros((128, 2), np.float32)
            fl[:, 0] = half
            fl[:, 1] = 1 - half
            m["flags"] = fl
            in_maps.append(m)
            continue
        if (b, half) not in xts:
            xts[(b, half)] = np.ascontiguousarray(x[b, half * NT:(half + 1) * NT, :].T)
        m["xT"] = xts[(b, half)]
        m["cosT"] = np.ascontiguousarray(cosT[:, half * NT:(half + 1) * NT])
        m["sinT"] = np.ascontiguousarray(sinT[:, half * NT:(half + 1) * NT])
        fl = np.zeros((128, 2), np.float32)
        if NRANK == 2:
            fl[:, 0] = half
            fl[:, 1] = 1 - half
        m["flags"] = fl
        in_maps.append(m)
    return in_maps


_NC_CACHE = {}
FUSED = True


def _get_nc(cfg_key, cfg):
    if cfg_key not in _NC_CACHE:
        _NC_CACHE[cfg_key] = build_program(cfg)
    return _NC_CACHE[cfg_key]


def _run(nc, in_maps, names):
    maps = [{k: m[k] for k in names if k in m} for m in in_maps]
    res = run_bass_kernel_spmd(nc, maps, core_ids=list(range(len(maps))))
    return res.results


WNAMES = ["ffn1_w_gu", "ffn2_w_gu", "ffn1_w_down", "ffn2_w_down", "w_in", "w_branch", "w_out", "gcols", "bgate", "gsub", "sink",
          "lamv", "rpbT", "cmask", "strip", "sel", "selb", "cosT", "sinT", "flags"]


def _pair_gather(results, name):
    out = []
    for c in range(len(results)):
        p = (c // 2) * 2
        out.append(np.concatenate([np.asarray(results[p][name]), np.asarray(results[p + 1][name])], axis=0))
    return out


def kernel_split(in_maps, cfg_extra=None):
    ce = cfg_extra or {}
    n = len(in_maps)
    r1 = _run(_get_nc("l1", dict(ce, launch=1)), in_maps, WNAMES + ["xT"])
    kvall = _pair_gather(r1, "kvloc_out")
    for c in range(n):
        in_maps[c]["h_in"] = np.asarray(r1[c]["h_out"])
        in_maps[c]["kvall_in"] = kvall[c]
        in_maps[c]["kvloc_in"] = np.asarray(r1[c]["kvloc_out"])
    del r1
    r2 = _run(_get_nc("l2", dict(ce, launch=2)), in_maps, WNAMES + ["h_in", "kvall_in", "kvloc_in"])
    kvall = _pair_gather(r2, "kvloc_out")
    for c in range(n):
        in_maps[c]["h_in"] = np.asarray(r2[c]["h_out"])
        in_maps[c]["kvall_in"] = kvall[c]
        in_maps[c]["kvloc_in"] = np.asarray(r2[c]["kvloc_out"])
    del r2
    r3 = _run(_get_nc("l3", dict(ce, launch=3)), in_maps, WNAMES + ["h_in", "kvall_in", "kvloc_in"])
    return [np.asarray(r3[c]["outT"]) for c in range(n)]


def kernel(**inputs):
    if FUSED:
        set_mode(1, rot=True)
        in_maps = _host_inputs(inputs)
        res = _run(_get_nc("fused", {}), in_maps, WNAMES + ["xT"])
        out = np.empty((4, S, D), np.float32)
        for c in range(8):
            b, half = c // 2, c % 2
            out[b, half * (S // 2):(half + 1) * (S // 2), :] = np.asarray(res[c]["outT"]).T
        return out
    set_mode(2)
    in_maps = _host_inputs(inputs)
    outs = kernel_split(in_maps)
    out = np.empty((4, S, D), np.float32)
    for c in range(8):
        b, half = c // 2, c % 2
        out[b, half * NT:(half + 1) * NT, :] = outs[c].T
    return out
```

```python
import contextlib
import numpy as np
import concourse.bass as bass
import concourse.mybir as mybir
from concourse.bass_utils import run_bass_kernel_spmd

F32 = mybir.dt.float32
BF16 = mybir.dt.bfloat16
AF = mybir.ActivationFunctionType
ALU = mybir.AluOpType

D = 1024
FF = 2816
S = 8192
NT = 4096
TB = 512
NBLK = NT // TB
NRANK = 2


ROT = False


def set_mode(nrank, rot=False):
    global NT, NBLK, NRANK, ROT
    ROT = rot
    NRANK = nrank
    NT = S // nrank
    NBLK = NT // TB
L = 2
EPS = 1e-6
KV_ROWS = 2304
KA0, KB0, KC0, VA0, VB0, VC0 = 0, 512, 640, 1152, 1664, 1792

N_DMA_SEMS = 16
LOOKAHEAD = 2
SAME_ENGINE_SYNC = True


class Op:
    __slots__ = ("eng", "emit", "deps", "is_dma", "marked", "semval", "dsem", "dval")

    def __init__(self, eng, emit, is_dma):
        self.eng = eng
        self.emit = emit
        self.deps = []
        self.is_dma = is_dma
        self.marked = False
        self.semval = 0
        self.dsem = None
        self.dval = 0


class Prog:
    ENGS = ("pe", "act", "dve", "pool", "sp")

    def __init__(self, nc):
        self.nc = nc
        self.ops = []
        self.last_writer = {}
        self.readers = {}

    def op(self, eng, emit, reads=(), writes=(), dma=False):
        o = Op(eng, emit, dma)
        deps = []
        reads = _expand(reads)
        writes = _expand(writes)
        for b in reads:
            w = self.last_writer.get(b)
            if w is not None:
                deps.append(w)
        for b in writes:
            w = self.last_writer.get(b)
            if w is not None:
                deps.append(w)
            rs = self.readers.get(b)
            if rs:
                deps.extend(rs)
        for b in reads:
            self.readers.setdefault(b, []).append(o)
        for b in writes:
            self.last_writer[b] = o
            self.readers[b] = []
        seen = set()
        for d in deps:
            if id(d) not in seen and d is not o:
                seen.add(id(d))
                o.deps.append(d)
        self.ops.append(o)
        return o

    def finalize(self, stack):
        nc = self.nc
        per_eng = {e: [] for e in self.ENGS}
        for o in self.ops:
            per_eng[o.eng].append(o)
        for o in self.ops:
            for d in o.deps:
                if d.is_dma:
                    continue
                if d.eng == o.eng and (o.eng == "pe" or not SAME_ENGINE_SYNC) and not o.is_dma:
                    continue
                d.marked = True
        self.csem = {e: stack.enter_context(nc.semaphore("c_" + e)) for e in ("pe", "act", "dve", "pool")}
        dsems = {}
        for e in self.ENGS:
            ndma = sum(1 for o in per_eng[e] if o.is_dma)
            if ndma:
                dsems[e] = [stack.enter_context(nc.semaphore("d_%s_%d" % (e, i))) for i in range(min(ndma, N_DMA_SEMS))]
        for e in self.ENGS:
            cnt = 0
            j = 0
            for o in per_eng[e]:
                if o.is_dma:
                    o.dsem = dsems[e][j % N_DMA_SEMS]
                    o.dval = 16 * (j // N_DMA_SEMS + 1)
                    j += 1
                elif o.marked:
                    cnt += 1
                    o.semval = cnt
        self.per_eng = per_eng

    def emit_engine(self, ename, eng):
        waited = {}

        def wait(sem, val):
            key = id(sem)
            if waited.get(key, 0) >= val:
                return
            waited[key] = val
            eng.wait_ge(sem, val)

        for o in self.per_eng[ename]:
            for d in o.deps:
                if d.is_dma:
                    wait(d.dsem, d.dval)
                else:
                    if not d.marked:
                        continue
                    if d.eng == ename and not o.is_dma and (ename == "pe" or not SAME_ENGINE_SYNC):
                        continue
                    wait(self.csem[d.eng], d.semval)
            if o.is_dma and o.dval > 16:
                wait(o.dsem, o.dval - 16)
            if o.emit is None:
                continue
            ins = o.emit(eng)
            if o.is_dma:
                ins.then_inc(o.dsem, 16)
            elif o.marked:
                ins.then_inc(self.csem[ename], 1)

    def run_block(self):
        nc = self.nc
        with nc.Block() as block:
            @block.tensor
            def _(e):
                self.emit_engine("pe", e)

            @block.scalar
            def _(e):
                self.emit_engine("act", e)

            @block.vector
            def _(e):
                self.emit_engine("dve", e)

            @block.gpsimd
            def _(e):
                self.emit_engine("pool", e)

            @block.sync
            def _(e):
                self.emit_engine("sp", e)


class KS(list):
    pass


def _expand(keys):
    out = []
    for k in keys:
        if isinstance(k, KS):
            out.extend(k)
        else:
            out.append(k)
    return out


class AliasRot:
    def __init__(self, views, keysets):
        self.bufs, self.keys, self.i = views, keysets, 0

    def next(self):
        b, k = self.bufs[self.i], self.keys[self.i]
        self.i = (self.i + 1) % len(self.bufs)
        return b, k


class Rot:
    def __init__(self, K, name, n, shape, dt):
        self.bufs = [K.T("%s%d" % (name, i), shape, dt) for i in range(n)]
        self.keys = ["%s%d" % (name, i) for i in range(n)]
        self.i = 0

    def next(self):
        b, k = self.bufs[self.i], self.keys[self.i]
        self.i = (self.i + 1) % len(self.bufs)
        return b, k


class WSpec:
    def __init__(self, name, src2d, K, N, w, gk, swap=False):
        self.name, self.src, self.K, self.N, self.w, self.gk, self.swap = name, src2d, K, N, w, gk, swap
        self.ncb = N // w
        self.nkg = K // (128 * gk)
        assert self.ncb * w == N and self.nkg * 128 * gk == K


class KB:
    def __init__(self, cfg):
        self.cfg = cfg
        self.nc = bass.Bass("TRN2", target_bir_lowering=False)
        self.st = contextlib.ExitStack()
        self.P = Prog(self.nc)
        self.uid = 0
        self.bank_i = {}
        self.outs = []

    def T(self, name, shape, dt=F32):
        return self.st.enter_context(self.nc.sbuf_tensor(name, shape, dt))

    def dram(self, name, shape, dt, kind="Internal"):
        return self.nc.dram_tensor(name, shape, dt, kind=kind).ap()

    def key(self, p):
        self.uid += 1
        return "%s#%d" % (p, self.uid)

    def dma(self, out, in_, reads, writes, eng="sp"):
        return self.P.op(eng, lambda e: e.dma_start(out=out, in_=in_), reads, writes, dma=True)

    def mm(self, out, lhsT, rhs, start, stop, reads, writes, **kw):
        return self.P.op("pe", lambda e: e.matmul(out, lhsT=lhsT, rhs=rhs, start=start, stop=stop, **kw), reads, writes)

    def act(self, out, in_, func, reads, writes, **kw):
        return self.P.op("act", lambda e: e.activation(out=out, in_=in_, func=func, **kw), reads, writes)

    def tt(self, eng, out, in0, in1, op, reads, writes):
        return self.P.op(eng, lambda e: e.tensor_tensor(out=out, in0=in0, in1=in1, op=op), reads, writes)

    def ts(self, eng, out, in0, s1, s2, op0, op1, reads, writes):
        if op1 is None:
            return self.P.op(eng, lambda e: e.tensor_scalar(out=out, in0=in0, scalar1=s1, scalar2=None, op0=op0), reads, writes)
        return self.P.op(eng, lambda e: e.tensor_scalar(out=out, in0=in0, scalar1=s1, scalar2=s2, op0=op0, op1=op1), reads, writes)

    def stt(self, eng, out, in0, scalar, in1, op0, op1, reads, writes):
        return self.P.op(eng, lambda e: e.scalar_tensor_tensor(out=out, in0=in0, scalar=scalar, in1=in1, op0=op0, op1=op1), reads, writes)

    def copy(self, eng, out, in_, reads, writes):
        if eng == "act":
            return self.act(out, in_, AF.Copy, reads, writes)
        return self.P.op(eng, lambda e: e.tensor_copy(out=out, in_=in_), reads, writes)

    def recip(self, out, in_, reads, writes):
        return self.P.op("dve", lambda e: e.reciprocal(out=out, in_=in_), reads, writes)

    def bank(self, role, choices):
        i = self.bank_i.get(role, 0)
        self.bank_i[role] = i + 1
        b = choices[i % len(choices)]
        return self.ps[b], "ps%d" % b

    def build(self):
        nc, cfg = self.nc, self.cfg
        IN = lambda n, s, dt=F32: self.dram(n, s, dt, "ExternalInput")
        if cfg.get("launch", 0) in (0, 1):
            self.xT = IN("xT", [D, NT])
        self.w_gu = [IN("ffn1_w_gu", [L, D, 2 * FF]), IN("ffn2_w_gu", [L, D, 2 * FF])]
        self.w_dn = [IN("ffn1_w_down", [L, FF, D]), IN("ffn2_w_down", [L, FF, D])]
        self.w_in = IN("w_in", [L, D, 6912])
        self.w_br = IN("w_branch", [L, 3, 512, D])
        self.w_o = IN("w_out", [L, D, D])
        self.gcols_d = IN("gcols", [128, 9 * L * 8])
        self.bgate_d = IN("bgate", [128, L * 3 * 8])
        self.gsub_d = IN("gsub", [128, L])
        self.sink_d = IN("sink", [L * 8])
        self.lamv_d = IN("lamv", [4 * L * 64])
        self.rpbT_d = IN("rpbT", [L, 8, 64, 15 * 64])
        self.cmask_d = IN("cmask", [64, 64])
        self.cosT_d = IN("cosT", [128, NT])
        self.sinT_d = IN("sinT", [128, NT])
        self.strip_d = IN("strip", [128, 1152])
        self.sel_d = IN("sel", [128, 4])
        self.selb_d = IN("selb", [2, 256])
        self.flags_d = IN("flags", [128, 2])
        launch = cfg.get("launch", 0)
        self.launch = launch
        OUT = lambda n, s, dt=F32: self.dram(n, s, dt, "ExternalOutput")
        self.kvloc = [None] * L
        self.kvall = [None] * L
        if launch == 0:
            self.outT = OUT("outT", [D, NT // 2 if ROT else NT])
            self.hbuf = self.dram("hbuf", [D, NT], F32)
            self.kvloc = [self.dram("kvloc%d" % l, [KV_ROWS, NT], BF16) for l in range(L)]
            if NRANK == 2:
                self.kvall = [self.dram("kvall%d" % l, [2 * KV_ROWS, NT], BF16) for l in range(L)]
        else:
            self.lB = {2: 0, 3: 1}.get(launch)
            self.lA = {1: 0, 2: 1}.get(launch)
            if cfg.get("skipA"):
                self.lA = None
            if launch > 1:
                self.h_in = IN("h_in", [D, NT])
            self.h_out = OUT("outT" if launch == 3 else "h_out", [D, NT])
            if self.lA is not None:
                self.kvloc[self.lA] = OUT("kvloc_out", [KV_ROWS, NT], BF16)
            if self.lB is not None:
                self.kvall[self.lB] = IN("kvall_in", [2 * KV_ROWS, NT], BF16)
                self.kvloc[self.lB] = IN("kvloc_in", [KV_ROWS, NT], BF16)
        self.ebt = [self.dram("ebt%d" % l, [8, 64, 15 * 64], BF16) for l in range(L)]
        psA = self.st.enter_context(nc.psum_tensor("psA", [128, 2048], F32))
        self.ps = [psA[:, i * 512:(i + 1) * 512] for i in range(4)]
        self.ps += [self.st.enter_context(nc.psum_tensor("ps%d" % i, [128, 512], F32))[:] for i in range(4, 8)]
        self.ps2 = [(psA[:, 0:1024], KS(["ps0", "ps1"])), (psA[:, 1024:2048], KS(["ps2", "ps3"]))]
        self.ps2_i = 0
        self.ones_f = self.T("ones_f", [128, 128], F32)
        self.ones_b = self.T("ones_b", [128, 128], BF16)
        self.gcols = self.T("gcols_s", [128, 9 * L * 8], F32)
        self.nbg = self.T("nbg", [128, L * 3 * 8], F32)
        self.gsub = self.T("gsub_s", [128, L], F32)
        self.esink = self.T("esink", [128, L * 8], F32)
        self.lamt = self.T("lamt", [128, 8 * L], F32)
        self.lamc = self.T("lamc", [128, L], F32)
        self.nlam = self.T("nlam", [128, L], F32)
        self.dacc = self.T("dacc", [128, 2 * TB], F32)
        self.e2 = Rot(self, "e2b", 3, [128, 2 * TB], BF16)
        self.strip = self.T("strip_s", [128, 1152], BF16)
        self.sel = self.T("sel_s", [128, 4], BF16)
        self.selb = self.T("selb_s", [2, 256], F32)
        self.cmask = self.T("cmask_s", [64, 64], F32)
        self.eps_t = self.T("eps_t", [128, 1], F32)
        self.one_t = self.T("one_t", [128, 1], F32)
        self.flags = self.T("flags_s", [128, 2], F32)
        self.hb = self.T("hb", [128, 8, TB], F32)
        self.xnT = self.T("xnT", [128, 8, TB], BF16)
        self.hT = self.T("hT", [128, 22, TB], BF16)
        self.ysb = self.T("ysb", [128, 8, TB], F32)
        self.cosT = self.T("cosT_s", [128, TB], F32)
        self.sinT = self.T("sinT_s", [128, TB], F32)
        self.wslot = Rot(self, "wslot", 4, [128, 2048], BF16)
        self.f32s = Rot(self, "f32s", 3, [128, TB], F32)
        self.bfs = Rot(self, "bfs", 4, [128, TB], BF16)
        self.rstd = Rot(self, "rstd", 2, [128, TB], F32)
        self.dfb = Rot(self, "dfb", 4, [128, TB], F32)
        self.sqb = Rot(self, "sqb", 2, [128, TB], BF16)
        self.pm = Rot(self, "pm", 4, [128, TB], BF16)
        self.kst = Rot(self, "kst", 2, [128, TB], BF16)
        self.vst = Rot(self, "vst", 1, [128, 4, 512], BF16)
        self.cin = AliasRot([self.ysb[:, 0:4, :].rearrange("p a b -> p (a b)"), self.ysb[:, 4:8, :].rearrange("p a b -> p (a b)")],
                            [KS([("ysb", c) for c in range(0, 4)]), KS([("ysb", c) for c in range(4, 8)])])
        self.cout = AliasRot([self.hT[:, 0:4, :].rearrange("p a b -> p (a b)"), self.hT[:, 4:8, :].rearrange("p a b -> p (a b)")],
                             [KS([("hT", c) for c in range(0, 4)]), KS([("hT", c) for c in range(4, 8)])])
        self.qT = self.T("qT", [128, 4, TB], BF16)
        self.oT = [self.T("oT%d" % i, [128, 4, TB], BF16) for i in range(3)]
        self.mT = self.T("mT", [128, 8, TB], BF16)
        self.macc = self.T("macc", [128, 2, TB], F32)
        self.kTA = self.T("kTA", [128, 4, 1024], BF16)
        self.vA = self.T("vA", [128, 16, 512], BF16)
        self.ebh = Rot(self, "ebh", 2, [128, 15 * 64], BF16)
        self.kTB = self.T("kTB", [128, 2, 768], BF16)
        self.vB = self.T("vB", [128, 6, 128], BF16)
        self.kTC = Rot(self, "kTC", 2, [128, 1024], BF16)
        self.vC = Rot(self, "vC", 2, [128, 8, 128], BF16)

        self.kvkeys = {i: [] for i in range(L)}
        self.setup_consts()
        nl = cfg.get("layers", L)
        nblk = cfg.get("nblk", NBLK)
        stop_after = cfg.get("stop_after", None)
        if launch == 0:
            self.setup_weights(list(range(nl)), list(range(nl)))
            for blk in range(nblk):
                self.load_h(blk, self.xT)
                self.stageA(0, blk)
                self.store_h(blk, self.hbuf)
            for l in range(nl):
                self.allgather(l)
                if stop_after == ("A", l):
                    break
                for blk in range(nblk // 2 if (ROT and l == nl - 1) else nblk):
                    self.load_h(blk, self.hbuf)
                    self.stageB(l, blk)
                    last = (l == nl - 1)
                    if not last:
                        self.stageA(l + 1, blk)
                    self.store_h(blk, self.outT if (last and not cfg.get("debug")) else self.hbuf)
        else:
            self.setup_weights([] if self.lB is None else [self.lB], [] if self.lA is None else [self.lA])
            for blk in range(nblk):
                self.load_h(blk, self.xT if launch == 1 else self.h_in)
                if self.lB is not None:
                    self.stageB(self.lB, blk)
                if self.lA is not None:
                    self.stageA(self.lA, blk)
                self.store_h(blk, self.h_out)
        if cfg.get("debug") and launch == 0:
            dh = self.dram("dbg_h", [D, NT], F32, "ExternalOutput")
            self.dma(dh, self.hbuf, [("hbuf", b) for b in range(nblk)], ["dbg_h"], eng="pool")
            self.outs.append("dbg_h")
            dk = self.dram("dbg_kv", [2 * KV_ROWS, NT], BF16, "ExternalOutput")
            self.dma(dk, self.kvall[0], [("kvall", 0)], ["dbg_kv"], eng="pool")
            self.outs.append("dbg_kv")
        self.finish()
        return nc

    def setup_consts(self):
        P = self.P
        P.op("pool", lambda e: e.memset(self.ones_f[:], 1.0), writes=["ones_f"])
        P.op("pool", lambda e: e.memset(self.ones_b[:], 1.0), writes=["ones_b"])
        P.op("pool", lambda e: e.memset(self.eps_t[:], EPS), writes=["eps_t"])
        P.op("pool", lambda e: e.memset(self.one_t[:], 1.0), writes=["one_t"])
        self.dma(self.flags[:], self.flags_d, [], ["flags"])
        self.dma(self.gcols[:], self.gcols_d, [], ["gcols"])
        for l in range(L):
            for which in (1, 8):
                c0 = (l * 9 + which) * 8
                self.ts("dve", self.gcols[:, c0:c0 + 8], self.gcols[:, c0:c0 + 8], 0.5, None, ALU.mult, None, ["gcols"], ["gcols"])
        self.dma(self.nbg[:], self.bgate_d, [], ["nbg"])
        self.ts("dve", self.nbg[:], self.nbg[:], -1.0, None, ALU.mult, None, ["nbg"], ["nbg"])
        self.dma(self.gsub[:], self.gsub_d, [], ["gsub"])
        for l in range(L):
            lam_init = 0.8 - 0.6 * float(np.exp(-0.3 * l))
            self.ts("dve", self.gsub[:, l:l + 1], self.gsub[:, l:l + 1], 1.0 - lam_init, None, ALU.mult, None, ["gsub"], ["gsub"])
        self.dma(self.esink[:], self.sink_d.partition_broadcast(128), [], ["esink"])
        self.act(self.esink[:], self.esink[:], AF.Exp, ["esink"], ["esink"])
        lrt, lrk = self.cin.next()
        self.dma(lrt[:, 0:4 * L * 64], self.lamv_d.partition_broadcast(128), [], [lrk])
        lr = lrt
        n = L * 64
        for l in range(L):
            for pair in range(2):
                q = lr[:, (2 * pair) * n + l * 64:(2 * pair) * n + (l + 1) * 64]
                k = lr[:, (2 * pair + 1) * n + l * 64:(2 * pair + 1) * n + (l + 1) * 64]
                col = l * 4 + pair
                tmp, tk = self.f32s.next()
                self.tt("dve", tmp[:, 0:64], q, k, ALU.mult, [lrk], [tk])
                self.P.op("act", (lambda e, tmp=tmp, col=col: e.activation(out=tmp[:, 64:128], in_=tmp[:, 0:64], func=AF.Copy, accum_out=self.lamt[:, col:col + 1])), [tk], [tk, "lamt"])
            self.act(self.lamt[:, l * 4:l * 4 + 2], self.lamt[:, l * 4:l * 4 + 2], AF.Exp, ["lamt"], ["lamt"])
            lam_init = 0.8 - 0.6 * float(np.exp(-0.3 * l))
            self.tt("dve", self.lamt[:, l * 4 + 2:l * 4 + 3], self.lamt[:, l * 4:l * 4 + 1], self.lamt[:, l * 4 + 1:l * 4 + 2], ALU.subtract, ["lamt"], ["lamt"])
            self.ts("dve", self.lamc[:, l:l + 1], self.lamt[:, l * 4 + 2:l * 4 + 3], -1.0, -lam_init, ALU.mult, ALU.add, ["lamt"], ["lamc"])
            self.ts("dve", self.nlam[:, l:l + 1], self.lamt[:, l * 4 + 2:l * 4 + 3], -1.0, -lam_init, ALU.mult, ALU.add, ["lamt"], ["nlam"])
            self.P.op("dve", (lambda e, l=l: e.memset(self.lamc[0:1, l:l + 1], 1.0)), ["lamc"], ["lamc"])
        t, tk = self.cin.next()
        self.dma(t[:, 0:1152], self.strip_d, [], [tk])
        self.copy("dve", self.strip[:], t[:, 0:1152], [tk], ["strip"])
        t2, tk2 = self.cin.next()
        self.dma(t2[:, 0:4], self.sel_d, [], [tk2])
        self.copy("dve", self.sel[:], t2[:, 0:4], [tk2], ["sel"])
        self.dma(self.selb[:], self.selb_d, [], ["selb"])
        self.dma(self.cmask[:], self.cmask_d, [], ["cmask"])
        for l in range(L):
            for h in range(8):
                t, tk = self.cin.next()
                self.dma(t[0:64, 0:960], self.rpbT_d[l, h], [], [tk])
                self.act(t[0:64, 0:960], t[0:64, 0:960], AF.Exp, [tk], [tk])
                o, ok = self.cout.next()
                self.tt("dve", o[0:64, 0:960].rearrange("p (i c) -> p i c", c=64), t[0:64, 0:960].rearrange("p (i c) -> p i c", c=64),
                        self.cmask[:].unsqueeze(1).to_broadcast([64, 15, 64]), ALU.mult, [tk, "cmask"], [ok])
                self.dma(self.ebt[l][h], o[0:64, 0:960], [ok], [("ebt", l, h)], eng="pool")

    def conv_weight(self, spec):
        scr = self.dram("ws_" + spec.name, [spec.ncb, spec.nkg, 128, spec.gk * spec.w], BF16)
        spec.scr = scr
        n = spec.gk * spec.w
        for cb in range(spec.ncb):
            for kg in range(spec.nkg):
                src = spec.src[kg * spec.gk * 128:(kg + 1) * spec.gk * 128, cb * spec.w:(cb + 1) * spec.w].rearrange("(k p) n -> p k n", p=128)
                t, tk = self.cin.next()
                self.dma(t[:, 0:n].rearrange("p (k n) -> p k n", k=spec.gk), src, [], [tk])
                o, ok = self.cout.next()
                self.cv_i = getattr(self, "cv_i", 0) + 1
                eng = ("dve", "act", "pool")[self.cv_i % 3] if not spec.swap else ("dve", "act")[self.cv_i % 2]
                if not spec.swap:
                    self.copy(eng, o[:, 0:n], t[:, 0:n], [tk], [ok])
                else:
                    sv = t[:, 0:n].rearrange("p (a t d) -> p a t d", t=2, d=32)
                    dv = o[:, 0:n].rearrange("p (a t d) -> p a t d", t=2, d=32)
                    self.copy("dve", dv[:, :, 0, :], sv[:, :, 1, :], [tk], [ok])
                    self.copy("act", dv[:, :, 1, :], sv[:, :, 0, :], [tk], [ok])
                self.dma(scr[cb, kg], o[:, 0:n], [ok], [("ws", spec.name, cb, kg)], eng="pool")

    def setup_weights(self, layersB, layersA):
        self.W = {}
        for l in range(L):
            for f in range(2):
                s = WSpec("gu%d_%d" % (f, l), self.w_gu[f][l], D, 2 * FF, 256, 8)
                self.W[("gu", f, l)] = s
                s2 = WSpec("dn%d_%d" % (f, l), self.w_dn[f][l], FF, D, 128, 11)
                self.W[("dn", f, l)] = s2
            self.W[("in", l)] = WSpec("in_%d" % l, self.w_in[l], D, 6912, 256, 8)
            self.W[("insw", l)] = WSpec("insw_%d" % l, self.w_in[l][:, 1536:3328], D, 1792, 256, 8, swap=True)
            for i in range(3):
                self.W[("br", l, i)] = WSpec("br%d_%d" % (i, l), self.w_br[l, i], 512, D, 256, 4)
            self.W[("out", l)] = WSpec("out_%d" % l, self.w_o[l], D, D, 256, 8)
        order = []
        for l in sorted(set(layersA) | set(layersB)):
            if l in layersA:
                order += [("gu", 0, l), ("dn", 0, l)]
            order += [("in", l), ("insw", l)]
            if l in layersB:
                order += [("br", l, 0), ("br", l, 1), ("br", l, 2), ("out", l), ("gu", 1, l), ("dn", 1, l)]
        for k in order:
            self.conv_weight(self.W[k])

    def load_slot(self, spec, cb, kg):
        t, tk = self.wslot.next()
        n = spec.gk * spec.w
        self.dma(t[:, 0:n], spec.scr[cb, kg], [("ws", spec.name, cb, kg)], [tk])
        return t[:, 0:n].rearrange("p (k n) -> p k n", k=spec.gk), tk

    def load_h(self, blk, src_t):
        src = src_t.rearrange("(c p) t -> p c t", p=128)[:, :, blk * TB:(blk + 1) * TB]
        self.dma(self.hb[:], src, [("hbuf", blk)], [KS([("hb", c) for c in range(8)])])
        self.dma(self.cosT[:], self.cosT_d[:, blk * TB:(blk + 1) * TB], [], ["cosT"])
        self.dma(self.sinT[:], self.sinT_d[:, blk * TB:(blk + 1) * TB], [], ["sinT"])

    def store_h(self, blk, dst_t):
        dst = dst_t.rearrange("(c p) t -> p c t", p=128)[:, :, blk * TB:(blk + 1) * TB]
        final = dst_t is not getattr(self, "hbuf", None)
        self.dma(dst, self.hb[:], [KS([("hb", c) for c in range(8)])], [("out", blk) if final else ("hbuf", blk)], eng="pool")
        if final:
            self.outs.append(("out", blk))

    def finish(self):
        if self.launch != 0 and self.lA is not None:
            self.outs.extend(self.kvkeys[self.lA])
        self.P.op("sp", None, reads=self.outs)
        self.P.finalize(self.st)
        self.P.run_block()

    def stats_rstd(self, srcs, src_keys, inv_n, from_psum=False):
        psS, pk = self.bank("stat", [6, 7])
        n = len(srcs)
        for c, (s, sk) in enumerate(zip(srcs, src_keys)):
            sq, qk = self.sqb.next()
            self.act(sq[:], s, AF.Square, [sk], [qk])
            self.mm(psS[:], self.ones_b[:], sq[:], c == 0, c == n - 1, [qk, "ones_b"], [pk])
        ln, lk = self.rstd.next()
        self.act(ln[:], psS[:], AF.Ln, [pk, "eps_t"], [lk], scale=inv_n, bias=self.eps_t[:])
        self.act(ln[:], ln[:], AF.Exp, [lk], [lk], scale=-0.5)
        return ln, lk

    def norm_to_xnT(self, l, which):
        rstd, rk = self.stats_rstd([self.hb[:, c, :] for c in range(8)], [("hb", c) for c in range(8)], 1.0 / D)
        g0 = (l * 9 + which) * 8
        for c in range(8):
            self.stt("dve", self.xnT[:, c, :], self.hb[:, c, :], self.gcols[:, g0 + c:g0 + c + 1], rstd[:], ALU.mult, ALU.mult,
                     [("hb", c), "gcols", rk], [("xnT", c)])

    def post_norm_add(self, l, which):
        rstd, rk = self.stats_rstd([self.ysb[:, c, :] for c in range(8)], [("ysb", c) for c in range(8)], 1.0 / D)
        g0 = (l * 9 + which) * 8
        for c in range(8):
            t, tk = self.f32s.next()
            self.stt("dve", t[:], self.ysb[:, c, :], self.gcols[:, g0 + c:g0 + c + 1], rstd[:], ALU.mult, ALU.mult, [("ysb", c), "gcols", rk], [tk])
            self.tt("pool", self.hb[:, c, :], self.hb[:, c, :], t[:], ALU.add, [("hb", c), tk], [("hb", c)])

    def ffn(self, l, f):
        gu, dn = self.W[("gu", f, l)], self.W[("dn", f, l)]
        self.norm_to_xnT(l, 0 if f == 0 else 7)
        xk = [("xnT", c) for c in range(8)]
        for jj in range(11):
            gs, gk_ = self.load_slot(gu, jj, 0)
            us, uk_ = self.load_slot(gu, 11 + jj, 0)
            for sub in range(2):
                j = 2 * jj + sub
                pg, pgk = self.bank("ffn_g", [0, 1])
                pu, puk = self.bank("ffn_u", [2, 3])
                for kc in range(8):
                    self.mm(pg[:], gs[:, kc, sub * 128:(sub + 1) * 128], self.xnT[:, kc, :], kc == 0, kc == 7, [gk_, xk[kc]], [pgk])
                for kc in range(8):
                    self.mm(pu[:], us[:, kc, sub * 128:(sub + 1) * 128], self.xnT[:, kc, :], kc == 0, kc == 7, [uk_, xk[kc]], [puk])
                s, sk = self.f32s.next()
                self.act(s[:], pg[:], AF.Silu, [pgk], [sk])
                self.tt("dve", self.hT[:, j, :], s[:], pu[:], ALU.mult, [sk, puk], [("hT", j)])
        for c in range(8):
            py, pyk = self.bank("ffn_y", [4, 5])
            for kg in range(2):
                ws, wk = self.load_slot(dn, c, kg)
                for ki in range(11):
                    kc = kg * 11 + ki
                    self.mm(py[:], ws[:, ki, :], self.hT[:, kc, :], kc == 0, kc == 21, [wk, ("hT", kc)], [pyk])
            self.copy("act" if c % 2 == 0 else "dve", self.ysb[:, c, :], py[:], [pyk], [("ysb", c)])
        self.post_norm_add(l, 1 if f == 0 else 8)

    def proj_fm(self, l, cb_list, dst_fn, rope, xk):
        win, wsw = self.W[("in", l)], self.W[("insw", l)]
        i = 0
        for cb in cb_list:
            ws, wk = self.load_slot(win, cb, 0)
            if rope:
                ss, sk_ = self.load_slot(wsw, cb - 6, 0)
            for half in range(2):
                pk_, pkk = self.bank("pj_a", [0, 1])
                for kc in range(8):
                    self.mm(pk_[:], ws[:, kc, half * 128:(half + 1) * 128], self.xnT[:, kc, :], kc == 0, kc == 7, [wk, xk[kc]], [pkk])
                dst, dk, post = dst_fn(i)
                if dst is None:
                    i += 1
                    continue
                if rope:
                    ps2, ps2k = self.bank("pj_b", [2, 3])
                    for kc in range(8):
                        self.mm(ps2[:], ss[:, kc, half * 128:(half + 1) * 128], self.xnT[:, kc, :], kc == 0, kc == 7, [sk_, xk[kc]], [ps2k])
                    t1, t1k = self.f32s.next()
                    t2, t2k = self.f32s.next()
                    self.tt("dve", t1[:], pk_[:], self.cosT[:], ALU.mult, [pkk, "cosT"], [t1k])
                    self.tt("dve", t2[:], ps2[:], self.sinT[:], ALU.mult, [ps2k, "sinT"], [t2k])
                    self.tt("pool", dst, t1[:], t2[:], ALU.add, [t1k, t2k], [dk])
                else:
                    self.copy("act" if i % 2 == 0 else "dve", dst, pk_[:], [pkk], [dk])
                if post is not None:
                    post()
                i += 1

    def stageA(self, l, blk):
        if "ffn" not in self.cfg.get("skip", ()):
            self.ffn(l, 0)
        if "kv" in self.cfg.get("skip", ()):
            return
        self.norm_to_xnT(l, 2)
        xk = [("xnT", c) for c in range(8)]
        kv = self.kvloc[l]
        t0 = blk * TB
        if not hasattr(self, "kvkeys"):
            self.kvkeys = {i: [] for i in range(L)}

        def kdst_factory(row0, only0=False):
            def fn(i):
                if only0 and i > 0:
                    return None, None, None
                t, tk = self.kst.next()
                r0 = row0 + i * 128

                def post():
                    self.dma(kv[r0:r0 + 128, t0:t0 + TB], t[:], [tk], [self.key("kvw%d" % l)], eng="pool")
                    self.kvkeys[l].append("kvw%d#%d" % (l, self.uid))
                return t[:], tk, post
            return fn

        self.proj_fm(l, [2, 3], kdst_factory(KA0), False, xk)
        self.proj_fm(l, [8], kdst_factory(KB0, True), True, xk)
        self.proj_fm(l, [11, 12], kdst_factory(KC0), True, xk)
        win = self.W[("in", l)]
        for (cbs, vrow0, ncols) in (([4, 5], VA0, 512), ([8], VB0, 128), ([13, 14], VC0, 512)):
            vt, vk = self.vst.next()
            for ci, cb in enumerate(cbs):
                ws, wk = self.load_slot(win, cb, 0)
                for tt_ in range(4):
                    pv, pvk = self.bank("pj_v", [4, 5])
                    if ncols == 128:
                        for kc in range(8):
                            self.mm(pv[:, 0:128], self.xnT[:, kc, tt_ * 128:(tt_ + 1) * 128], ws[:, kc, 128:256], kc == 0, kc == 7, [wk, xk[kc]], [pvk])
                        self.copy("act" if tt_ % 2 == 0 else "dve", vt[:, tt_, 0:128], pv[:, 0:128], [pvk], [vk])
                    else:
                        for kc in range(8):
                            self.mm(pv[:, 0:256], self.xnT[:, kc, tt_ * 128:(tt_ + 1) * 128], ws[:, kc, :], kc == 0, kc == 7, [wk, xk[kc]], [pvk])
                        self.copy("act" if tt_ % 2 == 0 else "dve", vt[:, tt_, ci * 256:(ci + 1) * 256], pv[:, 0:256], [pvk], [vk])
            nrows = ncols * NT // NT
            view = kv[vrow0:vrow0 + ncols, :].rearrange("r (q c) -> (r q) c", c=ncols)
            dstv = view[t0:t0 + TB, :].rearrange("(t p) c -> p t c", p=128)
            self.dma(dstv, vt[:, :, 0:ncols], [vk], [self.key("kvw%d" % l)], eng="pool")
            self.kvkeys[l].append("kvw%d#%d" % (l, self.uid))

    def allgather(self, l):
        if NRANK == 1:
            return
        if self.cfg.get("no_ag"):
            self.dma(self.kvall[l][0:KV_ROWS, :], self.kvloc[l], list(self.kvkeys[l]), [("kvall", l)], eng="pool")
            return
        self.P.op("pool", lambda e: e.collective_compute("AllGather", ALU.bypass, replica_groups=[[0, 1], [2, 3], [4, 5], [6, 7]],
                                                         ins=[self.kvloc[l]], outs=[self.kvall[l]]),
                  reads=list(self.kvkeys[l]), writes=[("kvall", l)], dma=True)

    def setup_layer_tables(self, l):
        pass

    def kv_rows_tok(self, l, row0, nrows, tok_lo, tok_hi):
        out = []
        for r in range(2):
            lo, hi = max(tok_lo, r * NT), min(tok_hi, (r + 1) * NT)
            if lo < hi:
                out.append((self.kvall[l][r * KV_ROWS + row0:r * KV_ROWS + row0 + nrows, lo - r * NT:hi - r * NT], lo - tok_lo, hi - lo))
        return out

    def v_view(self, l, r, vrow0, ncols):
        return self.kvall[l][r * KV_ROWS + vrow0:r * KV_ROWS + vrow0 + ncols, :].rearrange("r (q c) -> (r q) c", c=ncols)

    def na_valid(self, half, R0, krl, j):
        if NRANK == 1 and ROT:
            tq = (R0 + j + 64 * half) % 128
            tk = ((krl % 128) + 64 * half) % 128
            if tk - tq != krl - (R0 + j):
                return False
            ws = min(max(tq - 4, 0), 120)
            return ws <= tk < ws + 8
        if NRANK == 1:
            half = 0
        qr = R0 + j + (NT // 64) * half
        kr = krl + (NT // 64) * half
        ws = min(max(qr - 4, 0), 120)
        return (0 <= kr <= 127) and (ws <= kr < ws + 8)

    def attn_na(self, l, blk):
        R0 = 8 * blk
        NR_ = NT // 64
        own_lo, own_hi = max(R0 - 4, 0), min(R0 + 12, NR_)
        kvl = self.kvloc[l]
        pieces = []
        if R0 - 4 < 0 and NRANK == 2:
            pieces.append(("prev", R0 - 4, 0))
        pieces.append(("own", own_lo, own_hi))
        if R0 + 12 > NR_ and NRANK == 2:
            pieces.append(("next", NR_, R0 + 12))
        if NRANK == 1 and ROT:
            if R0 - 4 < 0:
                pieces.append(("wrap", R0 - 4, 0))
            if R0 + 12 > NR_:
                pieces.append(("wrap", NR_, R0 + 12))
        for kind, lo, hi in pieces:
            s0 = lo - (R0 - 4)
            n = (hi - lo) * 64
            if kind == "wrap":
                wl, wh = lo % NR_, (hi - 1) % NR_ + 1
                ksrc = kvl[KA0:KA0 + 512, wl * 64:wh * 64]
                vsrc = kvl[VA0:VA0 + 512, :].rearrange("r (q c) -> (r q) c", c=512)[wl * 64:wh * 64, :]
                rk = list(self.kvkeys[l])
            elif kind == "own":
                ksrc = kvl[KA0:KA0 + 512, lo * 64:hi * 64]
                vsrc = kvl[VA0:VA0 + 512, :].rearrange("r (q c) -> (r q) c", c=512)[lo * 64:hi * 64, :]
                rk = list(self.kvkeys[l])
            elif kind == "prev":
                ksrc = self.kvall[l][KA0:KA0 + 512, NT + lo * 64:NT + hi * 64]
                vsrc = self.v_view(l, 0, VA0, 512)[NT + lo * 64:NT + hi * 64, :]
                rk = [("kvall", l)]
            else:
                ksrc = self.kvall[l][KV_ROWS + KA0:KV_ROWS + KA0 + 512, (lo - NR_) * 64:(hi - NR_) * 64]
                vsrc = self.v_view(l, 1, VA0, 512)[(lo - NR_) * 64:(hi - NR_) * 64, :]
                rk = [("kvall", l)]
            self.dma(self.kTA[:, :, s0 * 64:s0 * 64 + n], ksrc.rearrange("(c p) t -> p c t", p=128), rk, ["kTA"])
            for ph in range(2):
                self.dma(self.vA[ph * 64:(ph + 1) * 64, s0:s0 + (hi - lo), :], vsrc.rearrange("(r p) c -> p r c", p=64), rk, ["vA"])
        segs = []
        for s_ in range(16):
            krl = R0 - 4 + s_
            cats = [(self.na_valid(0, R0, krl, j), self.na_valid(1, R0, krl, j)) for j in range(8)]
            j = 0
            while j < 8:
                if cats[j] == (False, False):
                    j += 1
                    continue
                j1 = j
                while j1 + 1 < 8 and cats[j1 + 1] == cats[j]:
                    j1 += 1
                segs.append((s_, krl, j, j1, cats[j]))
                j = j1 + 1
        for ch in range(4):
            eb, ebk = self.ebh.next()
            for ph in range(2):
                self.dma(eb[ph * 64:(ph + 1) * 64, :], self.ebt[l][2 * ch + ph], [("ebt", l, 2 * ch + ph)], [ebk])
            ebv = eb[:].rearrange("p (i c) -> p i c", c=64)
            pO = [self.ps[2], self.ps[3]]
            pOk = ["ps2", "ps3"]
            pR = [self.ps[4], self.ps[5]]
            pRk = ["ps4", "ps5"]

            def na_S(si, ch=ch):
                s_, krl, j0, j1, cat = segs[si]
                nq = (j1 - j0 + 1) * 64
                pS, pSk = self.bank("att_s", [0, 1, 6, 7])
                for ph in range(2):
                    b = ph * 64
                    self.mm(pS[b:b + 64, 0:nq], self.kTA[b:b + 64, ch, s_ * 64:(s_ + 1) * 64], self.qT[b:b + 64, ch, j0 * 64:(j1 + 1) * 64],
                            True, True, ["kTA", ("qT", ch)], [pSk])
                return pS, pSk

            pend, nemit = [], 0
            for si, (s_, krl, j0, j1, cat) in enumerate(segs):
                while nemit < len(segs) and nemit <= si + LOOKAHEAD:
                    pend.append(na_S(nemit))
                    nemit += 1
                pS, pSk = pend.pop(0)
                nq = (j1 - j0 + 1) * 64
                c0, c1 = j0 * 64, (j1 + 1) * 64
                e, ek = self.bfs.next()
                self.act(e[:, 0:nq], pS[:, 0:nq], AF.Exp, [pSk], [ek], scale=0.125)
                pm, pmk = self.pm.next()
                idx0 = 7 - krl + R0 + j0
                ev = e[:, 0:nq].rearrange("p (i c) -> p i c", c=64)
                pv = pm[:, 0:nq].rearrange("p (i c) -> p i c", c=64)
                tb = ebv[:, idx0:idx0 + (j1 - j0 + 1), :]
                if cat == (True, True):
                    self.tt("dve", pv, ev, tb, ALU.mult, [ek, ebk], [pmk])
                else:
                    fl = self.flags[:, 1:2] if cat == (True, False) else self.flags[:, 0:1]
                    self.stt("dve", pv, ev, fl, tb, ALU.mult, ALU.mult, [ek, ebk, "flags"], [pmk])
                last = si == len(segs) - 1
                for ph in range(2):
                    b = ph * 64
                    h = 2 * ch + ph
                    self.mm(pO[ph][0:64, c0:c1], self.vA[b:b + 64, s_, h * 64:(h + 1) * 64], pm[b:b + 64, 0:nq], si == 0, last, ["vA", pmk], [pOk[ph]], skip_group_check=True)
                    self.mm(pR[ph][0:64, c0:c1], self.ones_b[b:b + 64, 0:64], pm[b:b + 64, 0:nq], si == 0, last, ["ones_b", pmk], [pRk[ph]], skip_group_check=True)
            for ph in range(2):
                b = ph * 64
                r, rk_ = self.f32s.next()
                self.recip(r[0:64, :], pR[ph][0:64, :], [pRk[ph]], [rk_])
                self.tt("dve", self.oT[0][b:b + 64, ch, :], pO[ph][0:64, :], r[0:64, :], ALU.mult, [pOk[ph], rk_], [("oT0", ch, ph)])

    def attn_sw(self, l, blk):
        kvl = self.kvloc[l]
        T0 = blk * 4 - 1
        tiles = []
        vview = kvl[VB0:VB0 + 128, :].rearrange("r (q c) -> (r q) c", c=128)
        rk = list(self.kvkeys[l])
        rot_mode = (NRANK == 1 and ROT)
        if rot_mode:
            ntl = NT // 128
            tls = []
            for s_ in range(6):
                Tl = (T0 + s_) % ntl
                v = []
                for half in (0, 1):
                    tq0 = (4 * blk + 32 * half) % 64
                    tk = (Tl + 32 * half) % 64
                    v.append(tk - tq0 == s_ - 1)
                tiles.append({(True, True): "own", (False, True): "prev", (True, False): "next", (False, False): "none"}[tuple(v)])
                tls.append(Tl)
            s_ = 0
            while s_ < 6:
                e_ = s_
                while e_ + 1 < 6 and tls[e_ + 1] == tls[e_] + 1:
                    e_ += 1
                lo, hi = tls[s_], tls[e_] + 1
                for kvh in range(2):
                    for ph in range(2):
                        self.dma(self.kTB[ph * 64:(ph + 1) * 64, kvh, s_ * 128:(e_ + 1) * 128], kvl[KB0 + kvh * 64:KB0 + (kvh + 1) * 64, lo * 128:hi * 128], rk, ["kTB"])
                self.dma(self.vB[:, s_:e_ + 1, :], vview[lo * 128:hi * 128, :].rearrange("(t p) c -> p t c", p=128), rk, ["vB"])
                s_ = e_ + 1
        else:
            for s_ in range(6):
                T = T0 + s_
                kind = "prev" if T < 0 else ("next" if T >= NT // 128 else "own")
                if NRANK == 1 and kind != "own":
                    kind = "none"
                tiles.append(kind)
            own_s = [s_ for s_ in range(6) if tiles[s_] == "own"]
            lo, hi = T0 + own_s[0], T0 + own_s[-1] + 1
            for kvh in range(2):
                for ph in range(2):
                    self.dma(self.kTB[ph * 64:(ph + 1) * 64, kvh, own_s[0] * 128:(own_s[-1] + 1) * 128], kvl[KB0 + kvh * 64:KB0 + (kvh + 1) * 64, lo * 128:hi * 128], rk, ["kTB"])
            self.dma(self.vB[:, own_s[0]:own_s[-1] + 1, :], vview[lo * 128:hi * 128, :].rearrange("(t p) c -> p t c", p=128), rk, ["vB"])
        for s_ in range(6):
            if rot_mode or tiles[s_] in ("own", "none"):
                continue
            r = 0 if tiles[s_] == "prev" else 1
            t_lo = NT - 128 if r == 0 else 0
            for kvh in range(2):
                for ph in range(2):
                    self.dma(self.kTB[ph * 64:(ph + 1) * 64, kvh, s_ * 128:(s_ + 1) * 128],
                             self.kvall[l][r * KV_ROWS + KB0 + kvh * 64:r * KV_ROWS + KB0 + (kvh + 1) * 64, t_lo:t_lo + 128], [("kvall", l)], ["kTB"])
            self.dma(self.vB[:, s_, :], self.v_view(l, r, VB0, 128)[t_lo:t_lo + 128, :], [("kvall", l)], ["vB"])
        for h in range(8):
            ch, b, kvh = h // 2, (h % 2) * 64, h // 4
            pO, pOk = self.bank("att_o", [2, 3])
            pR, pRk = self.bank("att_r", [4, 5])
            proc = [s_ for s_ in range(6) if tiles[s_] != "none"]

            def crange(s_):
                return max(0, s_ - 2) * 128, (min(3, s_) + 1) * 128

            def sw_S(pi, ch=ch, b=b, kvh=kvh):
                s_ = proc[pi]
                c_lo, c_hi = crange(s_)
                pS, pSk = self.bank("att_s", [0, 1, 6, 7])
                self.mm(pS[:, c_lo:c_hi], self.kTB[b:b + 64, kvh, s_ * 128:(s_ + 1) * 128], self.qT[b:b + 64, ch, c_lo:c_hi], True, True, ["kTB", ("qT", ch)], [pSk])
                return pS, pSk

            pend, nemit = [], 0
            for pi, s_ in enumerate(proc):
                while nemit < len(proc) and nemit <= pi + LOOKAHEAD:
                    pend.append(sw_S(nemit))
                    nemit += 1
                pS, pSk = pend.pop(0)
                c_lo, c_hi = crange(s_)
                e, ek = self.bfs.next()
                self.act(e[:, c_lo:c_hi], pS[:, c_lo:c_hi], AF.Exp, [pSk], [ek], scale=0.125)
                pm, pmk = self.pm.next()
                off = 512 - (s_ - 1) * 128
                if tiles[s_] == "own":
                    self.tt("dve", pm[:, c_lo:c_hi], e[:, c_lo:c_hi], self.strip[:, off + c_lo:off + c_hi], ALU.mult, [ek, "strip"], [pmk])
                else:
                    fl = self.flags[:, 0:1] if tiles[s_] == "prev" else self.flags[:, 1:2]
                    self.stt("dve", pm[:, c_lo:c_hi], e[:, c_lo:c_hi], fl, self.strip[:, off + c_lo:off + c_hi], ALU.mult, ALU.mult, [ek, "strip", "flags"], [pmk])
                self.mm(pO[0:64, c_lo:c_hi], self.vB[:, s_, kvh * 64:(kvh + 1) * 64], pm[:, c_lo:c_hi], s_ == proc[0], s_ == proc[-1], ["vB", pmk], [pOk], skip_group_check=True)
                self.mm(pR[0:64, c_lo:c_hi], self.ones_b[:, 0:64], pm[:, c_lo:c_hi], s_ == proc[0], s_ == proc[-1], ["ones_b", pmk], [pRk], skip_group_check=True)
            r, rk_ = self.f32s.next()
            self.ts("dve", r[0:64, :], pR[0:64, :], self.esink[0:64, l * 8 + h:l * 8 + h + 1], None, ALU.add, None, [pRk, "esink"], [rk_])
            self.recip(r[0:64, :], r[0:64, :], [rk_], [rk_])
            self.tt("dve", self.oT[1][b:b + 64, ch, :], pO[0:64, :], r[0:64, :], ALU.mult, [pOk, rk_], [("oT1", ch, h % 2)])

    def attn_diff(self, l, blk):
        npr = NT // 1024
        nchunks = S // 1024
        for h in range(4):
            pO = [self.ps[4], self.ps[5]]
            pOk = ["ps4", "ps5"]
            acc = self.dacc
            self.P.op("pool", lambda e: e.memset(self.dacc[:, 0:TB], 0.0), [], ["dacc"])
            pR1, pR1k = self.bank("stat", [6, 7])
            chunks = {}

            def load_chunk(kc8, h=h):
                r, tl = kc8 // npr, (kc8 % npr) * 1024
                kt_, ktk = self.kTC.next()
                vt_, vtk = self.vC.next()
                if NRANK == 1:
                    kvl_ = self.kvloc[l]
                    rk_ = list(self.kvkeys[l])
                    self.dma(kt_[:], kvl_[KC0 + h * 128:KC0 + (h + 1) * 128, tl:tl + 1024], rk_, [ktk])
                    vv_ = kvl_[VC0:VC0 + 512, :].rearrange("r (q c) -> (r q) c", c=512)
                    self.dma(vt_[:], vv_[tl:tl + 1024, h * 128:(h + 1) * 128].rearrange("(t p) c -> p t c", p=128), rk_, [vtk])
                else:
                    self.dma(kt_[:], self.kvall[l][r * KV_ROWS + KC0 + h * 128:r * KV_ROWS + KC0 + (h + 1) * 128, tl:tl + 1024], [("kvall", l)], [ktk])
                    self.dma(vt_[:], self.v_view(l, r, VC0, 512)[tl:tl + 1024, h * 128:(h + 1) * 128].rearrange("(t p) c -> p t c", p=128), [("kvall", l)], [vtk])
                chunks[kc8] = (kt_, ktk, vt_, vtk)

            items = [(kc8, kt) for kc8 in range(nchunks) for kt in range(8)]

            def emit_S(idx, h=h):
                kc8, kt = items[idx]
                if kt == 0:
                    load_chunk(kc8)
                kt_, ktk, vt_, vtk = chunks[kc8]
                pS2, pS2k = self.ps2[self.ps2_i % 2]
                self.ps2_i += 1
                for t in range(2):
                    self.mm(pS2[:, t * 512:(t + 1) * 512], kt_[t * 64:(t + 1) * 64, kt * 128:(kt + 1) * 128], self.qT[t * 64:(t + 1) * 64, h, :],
                            True, True, [ktk, ("qT", h)], [pS2k])
                return pS2, pS2k

            cur = emit_S(0)
            n_it = len(items)
            for idx in range(n_it):
                nxt = emit_S(idx + 1) if idx + 1 < n_it else None
                kc8, kt = items[idx]
                kt_, ktk, vt_, vtk = chunks[kc8]
                pS2, pS2k = cur
                e, ek = self.e2.next()
                self.act(e[:], pS2, AF.Exp, [pS2k], [ek], scale=0.125)
                self.tt("dve", acc[:, 0:TB], acc[:, 0:TB], e[:, 0:TB], ALU.add, ["dacc", ek], ["dacc"])
                for t in range(2):
                    self.mm(pO[t], vt_[:, kt, :], e[:, t * 512:(t + 1) * 512], idx == 0, idx == n_it - 1, [vtk, ek], [pOk[t]])
                self.mm(pR1, self.ones_b[:], e[:, TB:2 * TB], idx == 0, idx == n_it - 1, ["ones_b", ek], [pR1k])
                cur = nxt
            rs = []
            for t in range(2):
                if t == 0:
                    pR, pRk = self.bank("stat", [6, 7])
                    self.mm(pR, self.ones_f[:], acc[:, 0:TB], True, True, ["ones_f", "dacc"], [pRk])
                else:
                    pR, pRk = pR1, pR1k
                r_, rk_ = self.dfb.next()
                self.recip(r_[:], pR, [pRk], [rk_])
                rs.append((r_, rk_))
            o0, o0k = self.dfb.next()
            self.tt("dve", o0[:], pO[0], rs[0][0][:], ALU.mult, [pOk[0], rs[0][1]], [o0k])
            o1, o1k = self.dfb.next()
            self.tt("dve", o1[:], pO[1], rs[1][0][:], ALU.mult, [pOk[1], rs[1][1]], [o1k])
            self.stt("dve", o0[:], o1[:], self.nlam[:, l:l + 1], o0[:], ALU.mult, ALU.add, [o1k, o0k, "nlam"], [o0k])
            rstd, rk2 = self.stats_rstd([o0[:]], [o0k], 1.0 / 128)
            self.stt("dve", self.oT[2][:, h, :], o0[:], self.gsub[:, l:l + 1], rstd[:], ALU.mult, ALU.mult, [o0k, "gsub", rk2], [("oT2", h, 0), ("oT2", h, 1)])

    def merge_out(self, l):
        win, wo = self.W[("in", l)], self.W[("out", l)]
        xk = [("xnT", c) for c in range(8)]
        for cp in range(4):
            for i in range(3):
                gs, gk_ = self.load_slot(win, 15 + i * 4 + cp, 0)
                bs, bk_ = self.load_slot(self.W[("br", l, i)], cp, 0)
                for sub in range(2):
                    c = 2 * cp + sub
                    pY, pYk = self.bank("mg_y", [0, 1])
                    pG, pGk = self.bank("mg_g", [2, 3])
                    for kc in range(4):
                        self.mm(pY[:], bs[:, kc, sub * 128:(sub + 1) * 128], self.oT[i][:, kc, :], kc == 0, kc == 3, [bk_, ("oT%d" % i, kc, 0), ("oT%d" % i, kc, 1)], [pYk])
                    for kc in range(8):
                        self.mm(pG[:], gs[:, kc, sub * 128:(sub + 1) * 128], self.xnT[:, kc, :], kc == 0, kc == 7, [gk_, xk[kc]], [pGk])
                    g, gk2 = self.f32s.next()
                    col = (l * 3 + i) * 8 + c
                    self.act(g[:], pG[:], AF.Exp, [pGk, "nbg"], [gk2], scale=-1.0, bias=self.nbg[:, col:col + 1])
                    self.act(g[:], g[:], AF.Ln, [gk2, "one_t"], [gk2], scale=1.0, bias=self.one_t[:])
                    self.act(g[:], g[:], AF.Exp, [gk2], [gk2], scale=-1.0)
                    if i == 0:
                        self.tt("dve", self.macc[:, sub, :], pY[:], g[:], ALU.mult, [pYk, gk2], [("macc", sub)])
                    elif i == 1:
                        self.tt("dve", g[:], pY[:], g[:], ALU.mult, [pYk, gk2], [gk2])
                        self.tt("dve", self.macc[:, sub, :], self.macc[:, sub, :], g[:], ALU.add, [("macc", sub), gk2], [("macc", sub)])
                    else:
                        self.tt("dve", g[:], pY[:], g[:], ALU.mult, [pYk, gk2], [gk2])
                        self.tt("dve", self.mT[:, c, :], self.macc[:, sub, :], g[:], ALU.add, [("macc", sub), gk2], [("mT", c)])
        for cb in range(4):
            ws, wk = self.load_slot(wo, cb, 0)
            for sub in range(2):
                c = 2 * cb + sub
                pM, pMk = self.bank("ffn_y", [4, 5])
                for kc in range(8):
                    self.mm(pM[:], ws[:, kc, sub * 128:(sub + 1) * 128], self.mT[:, kc, :], kc == 0, kc == 7, [wk, ("mT", kc)], [pMk])
                self.copy("act" if c % 2 == 0 else "dve", self.ysb[:, c, :], pM[:], [pMk], [("ysb", c)])
        self.post_norm_add(l, 6)

    def stageB(self, l, blk):
        cfg = self.cfg
        self.norm_to_xnT(l, 2)
        xk = [("xnT", c) for c in range(8)]
        parts = cfg.get("parts", "ABC")
        qdst = lambda i: (self.qT[:, i, :], ("qT", i), None)
        skip = cfg.get("skip", ())
        self.proj_fm(l, [0, 1], qdst, False, xk)
        if "na" not in skip:
            self.attn_na(l, blk)
        self.proj_fm(l, [6, 7], qdst, True, xk)
        if "sw" not in skip:
            self.attn_sw(l, blk)
        self.proj_fm(l, [9, 10], qdst, True, xk)
        if "diff" not in skip:
            self.attn_diff(l, blk)
        if cfg.get("dbg_o"):
            if not hasattr(self, "dbg_o"):
                self.dbg_o = [self.dram("dbg_o%d" % i, [512, NT], BF16, "ExternalOutput") for i in range(3)]
            for i in range(3):
                k_ = self.key("dbgo")
                self.dma(self.dbg_o[i].rearrange("(c p) t -> p c t", p=128)[:, :, blk * TB:(blk + 1) * TB], self.oT[i][:],
                         [("oT%d" % i, c, hh) for c in range(4) for hh in range(2)], [k_], eng="pool")
                self.outs.append(k_)
        if "merge" not in skip:
            self.merge_out(l)
        if "ffn" not in skip:
            self.ffn(l, 1)


def build_program(cfg):
    kb = KB(cfg)
    with kb.st:
        nc = kb.build()
    return nc


def _consts():
    pos = np.arange(S, dtype=np.float32)
    inv = (np.float32(10000.0) ** (-(np.arange(0, 64, 2, dtype=np.float32) / np.float32(64)))).astype(np.float32)
    ang = (pos[:, None] * inv[None, :]).astype(np.float32)
    ang = np.concatenate([ang, ang], axis=-1)
    cos = np.cos(ang).astype(np.float32)
    sin = np.sin(ang).astype(np.float32)
    sgn = np.concatenate([-np.ones(32, np.float32), np.ones(32, np.float32)])
    cosT = np.ascontiguousarray(np.concatenate([cos.T, cos.T], axis=0))
    sinT = np.ascontiguousarray(np.concatenate([(sin * sgn).T, (sin * sgn).T], axis=0))
    c = np.arange(64)
    col_start = np.clip(c - 8, 0, 48)
    col_ok = (c[None, :] >= col_start[:, None]) & (c[None, :] < col_start[:, None] + 16)
    cmask = np.ascontiguousarray(col_ok.T.astype(np.float32))
    dc = np.clip(c[:, None] - c[None, :] + 15, 0, 30)
    kl = np.arange(128)[:, None]
    xx = np.arange(1152)[None, :]
    strip = (np.abs(xx - 512 - kl) <= 128).astype(np.float32)
    sel = np.zeros((128, 4), np.float32)
    sel[:, 0] = 1.0
    sel[:, 3] = 1.0
    selb = np.zeros((2, 256), np.float32)
    selb[0, 0:128] = 1.0
    selb[1, 128:256] = 1.0
    return cosT, sinT, cmask, dc, strip, sel, selb


def _host_inputs(inputs):
    cosT, sinT, cmask, dc, strip, sel, selb = _consts()
    f = lambda a: np.ascontiguousarray(np.asarray(a, dtype=np.float32))
    gnames = {0: "ffn1_pre_g", 1: "ffn1_post_g", 2: "mix_pre_g", 6: "mix_post_g", 7: "ffn2_pre_g", 8: "ffn2_post_g"}
    gcols = np.zeros((128, 9 * L * 8), np.float32)
    for l in range(L):
        for which, nm in gnames.items():
            g = f(inputs[nm])[l]
            gcols[:, (l * 9 + which) * 8:(l * 9 + which) * 8 + 8] = g.reshape(8, 128).T
    bg = f(inputs["b_gate"])
    bgate = np.zeros((128, L * 3 * 8), np.float32)
    for l in range(L):
        for i in range(3):
            bgate[:, (l * 3 + i) * 8:(l * 3 + i) * 8 + 8] = bg[l, i].reshape(8, 128).T
    gsub = np.ascontiguousarray(f(inputs["diff_subln_g"]).T)
    sink = f(inputs["sw_sink"]).reshape(-1)
    lamv = np.concatenate([f(inputs[k]).reshape(-1) for k in ("diff_lambda_q1", "diff_lambda_k1", "diff_lambda_q2", "diff_lambda_k2")])
    rpb = f(inputs["na_rpb"])
    idx = np.arange(15)
    rpbT = rpb[:, :, (14 - idx)[None, :, None], dc[:, None, :]]
    rpbT = np.ascontiguousarray(rpbT.reshape(L, 8, 64, 15 * 64))
    shared = {
        "ffn1_w_gu": f(inputs["ffn1_w_gu"]), "ffn2_w_gu": f(inputs["ffn2_w_gu"]),
        "ffn1_w_down": f(inputs["ffn1_w_down"]), "ffn2_w_down": f(inputs["ffn2_w_down"]),
        "w_in": f(inputs["w_in"]), "w_branch": f(inputs["w_branch"]), "w_out": f(inputs["w_out"]),
        "gcols": gcols, "bgate": bgate, "gsub": gsub, "sink": sink, "lamv": lamv, "rpbT": rpbT,
        "cmask": cmask, "strip": strip, "sel": sel, "selb": selb,
    }
    x = f(inputs["x"])
    in_maps = []
    xts = {}
    for c in range(8):
        b, half = c // 2, (c % 2 if NRANK == 2 else 0)
        m = dict(shared)
        if NRANK == 1 and ROT:
            half = c % 2
            perm = (np.arange(S) + half * (S // 2)) % S
            if (b, half) not in xts:
                xts[(b, half)] = (np.ascontiguousarray(x[b][perm].T), np.ascontiguousarray(cosT[:, perm]), np.ascontiguousarray(sinT[:, perm]))
            m["xT"], m["cosT"], m["sinT"] = xts[(b, half)]
            fl = np.zeros((128, 2), np.float32)
            fl[:, 0] = half
            fl[:, 1] = 1 - half
            m["flags"] = fl
            in_maps.append(m)
            continue
        if (b, half) not in xts:
            xts[(b, half)] = np.ascontiguousarray(x[b, half * NT:(half + 1) * NT, :].T)
        m["xT"] = xts[(b, half)]
        m["cosT"] = np.ascontiguousarray(cosT[:, half * NT:(half + 1) * NT])
        m["sinT"] = np.ascontiguousarray(sinT[:, half * NT:(half + 1) * NT])
        fl = np.zeros((128, 2), np.float32)
        if NRANK == 2:
            fl[:, 0] = half
            fl[:, 1] = 1 - half
        m["flags"] = fl
        in_maps.append(m)
    return in_maps


_NC_CACHE = {}
FUSED = True


def _get_nc(cfg_key, cfg):
    if cfg_key not in _NC_CACHE:
        _NC_CACHE[cfg_key] = build_program(cfg)
    return _NC_CACHE[cfg_key]


def _run(nc, in_maps, names):
    maps = [{k: m[k] for k in names if k in m} for m in in_maps]
    res = run_bass_kernel_spmd(nc, maps, core_ids=list(range(len(maps))))
    return res.results


WNAMES = ["ffn1_w_gu", "ffn2_w_gu", "ffn1_w_down", "ffn2_w_down", "w_in", "w_branch", "w_out", "gcols", "bgate", "gsub", "sink",
          "lamv", "rpbT", "cmask", "strip", "sel", "selb", "cosT", "sinT", "flags"]


def _pair_gather(results, name):
    out = []
    for c in range(len(results)):
        p = (c // 2) * 2
        out.append(np.concatenate([np.asarray(results[p][name]), np.asarray(results[p + 1][name])], axis=0))
    return out


def kernel_split(in_maps, cfg_extra=None):
    ce = cfg_extra or {}
    n = len(in_maps)
    r1 = _run(_get_nc("l1", dict(ce, launch=1)), in_maps, WNAMES + ["xT"])
    kvall = _pair_gather(r1, "kvloc_out")
    for c in range(n):
        in_maps[c]["h_in"] = np.asarray(r1[c]["h_out"])
        in_maps[c]["kvall_in"] = kvall[c]
        in_maps[c]["kvloc_in"] = np.asarray(r1[c]["kvloc_out"])
    del r1
    r2 = _run(_get_nc("l2", dict(ce, launch=2)), in_maps, WNAMES + ["h_in", "kvall_in", "kvloc_in"])
    kvall = _pair_gather(r2, "kvloc_out")
    for c in range(n):
        in_maps[c]["h_in"] = np.asarray(r2[c]["h_out"])
        in_maps[c]["kvall_in"] = kvall[c]
        in_maps[c]["kvloc_in"] = np.asarray(r2[c]["kvloc_out"])
    del r2
    r3 = _run(_get_nc("l3", dict(ce, launch=3)), in_maps, WNAMES + ["h_in", "kvall_in", "kvloc_in"])
    return [np.asarray(r3[c]["outT"]) for c in range(n)]


def kernel(**inputs):
    if FUSED:
        set_mode(1, rot=True)
        in_maps = _host_inputs(inputs)
        res = _run(_get_nc("fused", {}), in_maps, WNAMES + ["xT"])
        out = np.empty((4, S, D), np.float32)
        for c in range(8):
            b, half = c // 2, c % 2
            out[b, half * (S // 2):(half + 1) * (S // 2), :] = np.asarray(res[c]["outT"]).T
        return out
    set_mode(2)
    in_maps = _host_inputs(inputs)
    outs = kernel_split(in_maps)
    out = np.empty((4, S, D), np.float32)
    for c in range(8):
        b, half = c // 2, c % 2
        out[b, half * NT:(half + 1) * NT, :] = outs[c].T
    return out
```

```python
import contextlib
import numpy as np
import concourse.bass as bass
import concourse.mybir as mybir
from concourse.bass_utils import run_bass_kernel_spmd

F32 = mybir.dt.float32
BF16 = mybir.dt.bfloat16
AF = mybir.ActivationFunctionType
ALU = mybir.AluOpType

D = 1024
FF = 2816
S = 8192
NT = 4096
TB = 512
NBLK = NT // TB
NRANK = 2


ROT = False


def set_mode(nrank, rot=False):
    global NT, NBLK, NRANK, ROT
    ROT = rot
    NRANK = nrank
    NT = S // nrank
    NBLK = NT // TB
L = 2
EPS = 1e-6
KV_ROWS = 2304
KA0, KB0, KC0, VA0, VB0, VC0 = 0, 512, 640, 1152, 1664, 1792

N_DMA_SEMS = 16
LOOKAHEAD = 2
SAME_ENGINE_SYNC = True


class Op:
    __slots__ = ("eng", "emit", "deps", "is_dma", "marked", "semval", "dsem", "dval")

    def __init__(self, eng, emit, is_dma):
        self.eng = eng
        self.emit = emit
        self.deps = []
        self.is_dma = is_dma
        self.marked = False
        self.semval = 0
        self.dsem = None
        self.dval = 0


class Prog:
    ENGS = ("pe", "act", "dve", "pool", "sp")

    def __init__(self, nc):
        self.nc = nc
        self.ops = []
        self.last_writer = {}
        self.readers = {}

    def op(self, eng, emit, reads=(), writes=(), dma=False):
        o = Op(eng, emit, dma)
        deps = []
        reads = _expand(reads)
        writes = _expand(writes)
        for b in reads:
            w = self.last_writer.get(b)
            if w is not None:
                deps.append(w)
        for b in writes:
            w = self.last_writer.get(b)
            if w is not None:
                deps.append(w)
            rs = self.readers.get(b)
            if rs:
                deps.extend(rs)
        for b in reads:
            self.readers.setdefault(b, []).append(o)
        for b in writes:
            self.last_writer[b] = o
            self.readers[b] = []
        seen = set()
        for d in deps:
            if id(d) not in seen and d is not o:
                seen.add(id(d))
                o.deps.append(d)
        self.ops.append(o)
        return o

    def finalize(self, stack):
        nc = self.nc
        per_eng = {e: [] for e in self.ENGS}
        for o in self.ops:
            per_eng[o.eng].append(o)
        for o in self.ops:
            for d in o.deps:
                if d.is_dma:
                    continue
                if d.eng == o.eng and (o.eng == "pe" or not SAME_ENGINE_SYNC) and not o.is_dma:
                    continue
                d.marked = True
        self.csem = {e: stack.enter_context(nc.semaphore("c_" + e)) for e in ("pe", "act", "dve", "pool")}
        dsems = {}
        for e in self.ENGS:
            ndma = sum(1 for o in per_eng[e] if o.is_dma)
            if ndma:
                dsems[e] = [stack.enter_context(nc.semaphore("d_%s_%d" % (e, i))) for i in range(min(ndma, N_DMA_SEMS))]
        for e in self.ENGS:
            cnt = 0
            j = 0
            for o in per_eng[e]:
                if o.is_dma:
                    o.dsem = dsems[e][j % N_DMA_SEMS]
                    o.dval = 16 * (j // N_DMA_SEMS + 1)
                    j += 1
                elif o.marked:
                    cnt += 1
                    o.semval = cnt
        self.per_eng = per_eng

    def emit_engine(self, ename, eng):
        waited = {}

        def wait(sem, val):
            key = id(sem)
            if waited.get(key, 0) >= val:
                return
            waited[key] = val
            eng.wait_ge(sem, val)

        for o in self.per_eng[ename]:
            for d in o.deps:
                if d.is_dma:
                    wait(d.dsem, d.dval)
                else:
                    if not d.marked:
                        continue
                    if d.eng == ename and not o.is_dma and (ename == "pe" or not SAME_ENGINE_SYNC):
                        continue
                    wait(self.csem[d.eng], d.semval)
            if o.is_dma and o.dval > 16:
                wait(o.dsem, o.dval - 16)
            if o.emit is None:
                continue
            ins = o.emit(eng)
            if o.is_dma:
                ins.then_inc(o.dsem, 16)
            elif o.marked:
                ins.then_inc(self.csem[ename], 1)

    def run_block(self):
        nc = self.nc
        with nc.Block() as block:
            @block.tensor
            def _(e):
                self.emit_engine("pe", e)

            @block.scalar
            def _(e):
                self.emit_engine("act", e)

            @block.vector
            def _(e):
                self.emit_engine("dve", e)

            @block.gpsimd
            def _(e):
                self.emit_engine("pool", e)

            @block.sync
            def _(e):
                self.emit_engine("sp", e)


class KS(list):
    pass


def _expand(keys):
    out = []
    for k in keys:
        if isinstance(k, KS):
            out.extend(k)
        else:
            out.append(k)
    return out


class AliasRot:
    def __init__(self, views, keysets):
        self.bufs, self.keys, self.i = views, keysets, 0

    def next(self):
        b, k = self.bufs[self.i], self.keys[self.i]
        self.i = (self.i + 1) % len(self.bufs)
        return b, k


class Rot:
    def __init__(self, K, name, n, shape, dt):
        self.bufs = [K.T("%s%d" % (name, i), shape, dt) for i in range(n)]
        self.keys = ["%s%d" % (name, i) for i in range(n)]
        self.i = 0

    def next(self):
        b, k = self.bufs[self.i], self.keys[self.i]
        self.i = (self.i + 1) % len(self.bufs)
        return b, k


class WSpec:
    def __init__(self, name, src2d, K, N, w, gk, swap=False):
        self.name, self.src, self.K, self.N, self.w, self.gk, self.swap = name, src2d, K, N, w, gk, swap
        self.ncb = N // w
        self.nkg = K // (128 * gk)
        assert self.ncb * w == N and self.nkg * 128 * gk == K


class KB:
    def __init__(self, cfg):
        self.cfg = cfg
        self.nc = bass.Bass("TRN2", target_bir_lowering=False)
        self.st = contextlib.ExitStack()
        self.P = Prog(self.nc)
        self.uid = 0
        self.bank_i = {}
        self.outs = []

    def T(self, name, shape, dt=F32):
        return self.st.enter_context(self.nc.sbuf_tensor(name, shape, dt))

    def dram(self, name, shape, dt, kind="Internal"):
        return self.nc.dram_tensor(name, shape, dt, kind=kind).ap()

    def key(self, p):
        self.uid += 1
        return "%s#%d" % (p, self.uid)

    def dma(self, out, in_, reads, writes, eng="sp"):
        return self.P.op(eng, lambda e: e.dma_start(out=out, in_=in_), reads, writes, dma=True)

    def mm(self, out, lhsT, rhs, start, stop, reads, writes, **kw):
        return self.P.op("pe", lambda e: e.matmul(out, lhsT=lhsT, rhs=rhs, start=start, stop=stop, **kw), reads, writes)

    def act(self, out, in_, func, reads, writes, **kw):
        return self.P.op("act", lambda e: e.activation(out=out, in_=in_, func=func, **kw), reads, writes)

    def tt(self, eng, out, in0, in1, op, reads, writes):
        return self.P.op(eng, lambda e: e.tensor_tensor(out=out, in0=in0, in1=in1, op=op), reads, writes)

    def ts(self, eng, out, in0, s1, s2, op0, op1, reads, writes):
        if op1 is None:
            return self.P.op(eng, lambda e: e.tensor_scalar(out=out, in0=in0, scalar1=s1, scalar2=None, op0=op0), reads, writes)
        return self.P.op(eng, lambda e: e.tensor_scalar(out=out, in0=in0, scalar1=s1, scalar2=s2, op0=op0, op1=op1), reads, writes)

    def stt(self, eng, out, in0, scalar, in1, op0, op1, reads, writes):
        return self.P.op(eng, lambda e: e.scalar_tensor_tensor(out=out, in0=in0, scalar=scalar, in1=in1, op0=op0, op1=op1), reads, writes)

    def copy(self, eng, out, in_, reads, writes):
        if eng == "act":
            return self.act(out, in_, AF.Copy, reads, writes)
        return self.P.op(eng, lambda e: e.tensor_copy(out=out, in_=in_), reads, writes)

    def recip(self, out, in_, reads, writes):
        return self.P.op("dve", lambda e: e.reciprocal(out=out, in_=in_), reads, writes)

    def bank(self, role, choices):
        i = self.bank_i.get(role, 0)
        self.bank_i[role] = i + 1
        b = choices[i % len(choices)]
        return self.ps[b], "ps%d" % b

    def build(self):
        nc, cfg = self.nc, self.cfg
        IN = lambda n, s, dt=F32: self.dram(n, s, dt, "ExternalInput")
        if cfg.get("launch", 0) in (0, 1):
            self.xT = IN("xT", [D, NT])
        self.w_gu = [IN("ffn1_w_gu", [L, D, 2 * FF]), IN("ffn2_w_gu", [L, D, 2 * FF])]
        self.w_dn = [IN("ffn1_w_down", [L, FF, D]), IN("ffn2_w_down", [L, FF, D])]
        self.w_in = IN("w_in", [L, D, 6912])
        self.w_br = IN("w_branch", [L, 3, 512, D])
        self.w_o = IN("w_out", [L, D, D])
        self.gcols_d = IN("gcols", [128, 9 * L * 8])
        self.bgate_d = IN("bgate", [128, L * 3 * 8])
        self.gsub_d = IN("gsub", [128, L])
        self.sink_d = IN("sink", [L * 8])
        self.lamv_d = IN("lamv", [4 * L * 64])
        self.rpbT_d = IN("rpbT", [L, 8, 64, 15 * 64])
        self.cmask_d = IN("cmask", [64, 64])
        self.cosT_d = IN("cosT", [128, NT])
        self.sinT_d = IN("sinT", [128, NT])
        self.strip_d = IN("strip", [128, 1152])
        self.sel_d = IN("sel", [128, 4])
        self.selb_d = IN("selb", [2, 256])
        self.flags_d = IN("flags", [128, 2])
        launch = cfg.get("launch", 0)
        self.launch = launch
        OUT = lambda n, s, dt=F32: self.dram(n, s, dt, "ExternalOutput")
        self.kvloc = [None] * L
        self.kvall = [None] * L
        if launch == 0:
            self.outT = OUT("outT", [D, NT // 2 if ROT else NT])
            self.hbuf = self.dram("hbuf", [D, NT], F32)
            self.kvloc = [self.dram("kvloc%d" % l, [KV_ROWS, NT], BF16) for l in range(L)]
            if NRANK == 2:
                self.kvall = [self.dram("kvall%d" % l, [2 * KV_ROWS, NT], BF16) for l in range(L)]
        else:
            self.lB = {2: 0, 3: 1}.get(launch)
            self.lA = {1: 0, 2: 1}.get(launch)
            if cfg.get("skipA"):
                self.lA = None
            if launch > 1:
                self.h_in = IN("h_in", [D, NT])
            self.h_out = OUT("outT" if launch == 3 else "h_out", [D, NT])
            if self.lA is not None:
                self.kvloc[self.lA] = OUT("kvloc_out", [KV_ROWS, NT], BF16)
            if self.lB is not None:
                self.kvall[self.lB] = IN("kvall_in", [2 * KV_ROWS, NT], BF16)
                self.kvloc[self.lB] = IN("kvloc_in", [KV_ROWS, NT], BF16)
        self.ebt = [self.dram("ebt%d" % l, [8, 64, 15 * 64], BF16) for l in range(L)]
        psA = self.st.enter_context(nc.psum_tensor("psA", [128, 2048], F32))
        self.ps = [psA[:, i * 512:(i + 1) * 512] for i in range(4)]
        self.ps += [self.st.enter_context(nc.psum_tensor("ps%d" % i, [128, 512], F32))[:] for i in range(4, 8)]
        self.ps2 = [(psA[:, 0:1024], KS(["ps0", "ps1"])), (psA[:, 1024:2048], KS(["ps2", "ps3"]))]
        self.ps2_i = 0
        self.ones_f = self.T("ones_f", [128, 128], F32)
        self.ones_b = self.T("ones_b", [128, 128], BF16)
        self.gcols = self.T("gcols_s", [128, 9 * L * 8], F32)
        self.nbg = self.T("nbg", [128, L * 3 * 8], F32)
        self.gsub = self.T("gsub_s", [128, L], F32)
        self.esink = self.T("esink", [128, L * 8], F32)
        self.lamt = self.T("lamt", [128, 8 * L], F32)
        self.lamc = self.T("lamc", [128, L], F32)
        self.nlam = self.T("nlam", [128, L], F32)
        self.dacc = self.T("dacc", [128, 2 * TB], F32)
        self.e2 = Rot(self, "e2b", 3, [128, 2 * TB], BF16)
        self.strip = self.T("strip_s", [128, 1152], BF16)
        self.sel = self.T("sel_s", [128, 4], BF16)
        self.selb = self.T("selb_s", [2, 256], F32)
        self.cmask = self.T("cmask_s", [64, 64], F32)
        self.eps_t = self.T("eps_t", [128, 1], F32)
        self.one_t = self.T("one_t", [128, 1], F32)
        self.flags = self.T("flags_s", [128, 2], F32)
        self.hbs = [self.T("hb0", [128, 8, TB], F32), self.T("hb1", [128, 8, TB], F32)]
        self.hcur = 0
        self.hb = self.hbs[0]
        self.xnT = self.T("xnT", [128, 8, TB], BF16)
        self.hT = self.T("hT", [128, 22, TB], BF16)
        self.ysb = self.T("ysb", [128, 8, TB], F32)
        self.vA = self.hT[:, 0:16, :]
        self.vAk = KS([("hT", j) for j in range(16)])
        self.cosT = self.T("cosT_s", [128, TB], F32)
        self.sinT = self.T("sinT_s", [128, TB], F32)
        self.wslot = Rot(self, "wslot", 4, [128, 2048], BF16)
        self.f32s = Rot(self, "f32s", 3, [128, TB], F32)
        self.bfs = Rot(self, "bfs", 4, [128, TB], BF16)
        self.rstd = Rot(self, "rstd", 2, [128, TB], F32)
        self.dfb = Rot(self, "dfb", 4, [128, TB], F32)
        self.sqb = Rot(self, "sqb", 2, [128, TB], BF16)
        self.pm = Rot(self, "pm", 4, [128, TB], BF16)
        self.kst = Rot(self, "kst", 2, [128, TB], BF16)
        self.vst = Rot(self, "vst", 1, [128, 4, 512], BF16)
        self.cin = AliasRot([self.ysb[:, 0:4, :].rearrange("p a b -> p (a b)"), self.ysb[:, 4:8, :].rearrange("p a b -> p (a b)")],
                            [KS([("ysb", c) for c in range(0, 4)]), KS([("ysb", c) for c in range(4, 8)])])
        self.cout = AliasRot([self.hT[:, 0:4, :].rearrange("p a b -> p (a b)"), self.hT[:, 4:8, :].rearrange("p a b -> p (a b)")],
                             [KS([("hT", c) for c in range(0, 4)]), KS([("hT", c) for c in range(4, 8)])])
        self.qT = self.T("qT", [128, 4, TB], BF16)
        self.oT = [self.T("oT%d" % i, [128, 4, TB], BF16) for i in range(3)]
        self.mT = self.T("mT", [128, 8, TB], BF16)
        self.macc = self.T("macc", [128, 2, TB], F32)
        self.kTA = self.T("kTA", [128, 4, 1024], BF16)
        self.ebh = Rot(self, "ebh", 2, [128, 15 * 64], BF16)
        self.kTB = self.T("kTB", [128, 2, 768], BF16)
        self.vB = self.T("vB", [128, 6, 128], BF16)
        self.kTC = Rot(self, "kTC", 2, [128, 1024], BF16)
        self.vC = Rot(self, "vC", 2, [128, 8, 128], BF16)

        self.kvkeys = {i: [] for i in range(L)}
        self.setup_consts()
        nl = cfg.get("layers", L)
        nblk = cfg.get("nblk", NBLK)
        stop_after = cfg.get("stop_after", None)
        if launch == 0:
            self.setup_weights(list(range(nl)), list(range(nl)))
            for blk in range(nblk):
                self.load_h(blk, self.xT)
                self.stageA(0, blk)
                self.store_h(blk, self.hbuf)
            for l in range(nl):
                self.allgather(l)
                if stop_after == ("A", l):
                    break
                for blk in range(nblk // 2 if (ROT and l == nl - 1) else nblk):
                    self.load_h(blk, self.hbuf)
                    self.stageB(l, blk)
                    last = (l == nl - 1)
                    if not last:
                        self.stageA(l + 1, blk)
                    self.store_h(blk, self.outT if (last and not cfg.get("debug")) else self.hbuf)
        else:
            self.setup_weights([] if self.lB is None else [self.lB], [] if self.lA is None else [self.lA])
            for blk in range(nblk):
                self.load_h(blk, self.xT if launch == 1 else self.h_in)
                if self.lB is not None:
                    self.stageB(self.lB, blk)
                if self.lA is not None:
                    self.stageA(self.lA, blk)
                self.store_h(blk, self.h_out)
        if cfg.get("debug") and launch == 0:
            dh = self.dram("dbg_h", [D, NT], F32, "ExternalOutput")
            self.dma(dh, self.hbuf, [("hbuf", b) for b in range(nblk)], ["dbg_h"], eng="pool")
            self.outs.append("dbg_h")
            dk = self.dram("dbg_kv", [2 * KV_ROWS, NT], BF16, "ExternalOutput")
            self.dma(dk, self.kvall[0], [("kvall", 0)], ["dbg_kv"], eng="pool")
            self.outs.append("dbg_kv")
        self.finish()
        return nc

    def setup_consts(self):
        P = self.P
        P.op("pool", lambda e: e.memset(self.ones_f[:], 1.0), writes=["ones_f"])
        P.op("pool", lambda e: e.memset(self.ones_b[:], 1.0), writes=["ones_b"])
        P.op("pool", lambda e: e.memset(self.eps_t[:], EPS), writes=["eps_t"])
        P.op("pool", lambda e: e.memset(self.one_t[:], 1.0), writes=["one_t"])
        self.dma(self.flags[:], self.flags_d, [], ["flags"])
        self.dma(self.gcols[:], self.gcols_d, [], ["gcols"])
        for l in range(L):
            for which in (1, 8):
                c0 = (l * 9 + which) * 8
                self.ts("dve", self.gcols[:, c0:c0 + 8], self.gcols[:, c0:c0 + 8], 0.5, None, ALU.mult, None, ["gcols"], ["gcols"])
        self.dma(self.nbg[:], self.bgate_d, [], ["nbg"])
        self.ts("dve", self.nbg[:], self.nbg[:], -1.0, None, ALU.mult, None, ["nbg"], ["nbg"])
        self.dma(self.gsub[:], self.gsub_d, [], ["gsub"])
        for l in range(L):
            lam_init = 0.8 - 0.6 * float(np.exp(-0.3 * l))
            self.ts("dve", self.gsub[:, l:l + 1], self.gsub[:, l:l + 1], 1.0 - lam_init, None, ALU.mult, None, ["gsub"], ["gsub"])
        self.dma(self.esink[:], self.sink_d.partition_broadcast(128), [], ["esink"])
        self.act(self.esink[:], self.esink[:], AF.Exp, ["esink"], ["esink"])
        lrt, lrk = self.cin.next()
        self.dma(lrt[:, 0:4 * L * 64], self.lamv_d.partition_broadcast(128), [], [lrk])
        lr = lrt
        n = L * 64
        for l in range(L):
            for pair in range(2):
                q = lr[:, (2 * pair) * n + l * 64:(2 * pair) * n + (l + 1) * 64]
                k = lr[:, (2 * pair + 1) * n + l * 64:(2 * pair + 1) * n + (l + 1) * 64]
                col = l * 4 + pair
                tmp, tk = self.f32s.next()
                self.tt("dve", tmp[:, 0:64], q, k, ALU.mult, [lrk], [tk])
                self.P.op("act", (lambda e, tmp=tmp, col=col: e.activation(out=tmp[:, 64:128], in_=tmp[:, 0:64], func=AF.Copy, accum_out=self.lamt[:, col:col + 1])), [tk], [tk, "lamt"])
            self.act(self.lamt[:, l * 4:l * 4 + 2], self.lamt[:, l * 4:l * 4 + 2], AF.Exp, ["lamt"], ["lamt"])
            lam_init = 0.8 - 0.6 * float(np.exp(-0.3 * l))
            self.tt("dve", self.lamt[:, l * 4 + 2:l * 4 + 3], self.lamt[:, l * 4:l * 4 + 1], self.lamt[:, l * 4 + 1:l * 4 + 2], ALU.subtract, ["lamt"], ["lamt"])
            self.ts("dve", self.lamc[:, l:l + 1], self.lamt[:, l * 4 + 2:l * 4 + 3], -1.0, -lam_init, ALU.mult, ALU.add, ["lamt"], ["lamc"])
            self.ts("dve", self.nlam[:, l:l + 1], self.lamt[:, l * 4 + 2:l * 4 + 3], -1.0, -lam_init, ALU.mult, ALU.add, ["lamt"], ["nlam"])
            self.P.op("dve", (lambda e, l=l: e.memset(self.lamc[0:1, l:l + 1], 1.0)), ["lamc"], ["lamc"])
        t, tk = self.cin.next()
        self.dma(t[:, 0:1152], self.strip_d, [], [tk])
        self.copy("dve", self.strip[:], t[:, 0:1152], [tk], ["strip"])
        t2, tk2 = self.cin.next()
        self.dma(t2[:, 0:4], self.sel_d, [], [tk2])
        self.copy("dve", self.sel[:], t2[:, 0:4], [tk2], ["sel"])
        self.dma(self.selb[:], self.selb_d, [], ["selb"])
        self.dma(self.cmask[:], self.cmask_d, [], ["cmask"])
        for l in range(L):
            for h in range(8):
                t, tk = self.cin.next()
                self.dma(t[0:64, 0:960], self.rpbT_d[l, h], [], [tk])
                self.act(t[0:64, 0:960], t[0:64, 0:960], AF.Exp, [tk], [tk])
                o, ok = self.cout.next()
                self.tt("dve", o[0:64, 0:960].rearrange("p (i c) -> p i c", c=64), t[0:64, 0:960].rearrange("p (i c) -> p i c", c=64),
                        self.cmask[:].unsqueeze(1).to_broadcast([64, 15, 64]), ALU.mult, [tk, "cmask"], [ok])
                self.dma(self.ebt[l][h], o[0:64, 0:960], [ok], [("ebt", l, h)], eng="pool")

    def conv_weight(self, spec):
        scr = self.dram("ws_" + spec.name, [spec.ncb, spec.nkg, 128, spec.gk * spec.w], BF16)
        spec.scr = scr
        n = spec.gk * spec.w
        for cb in range(spec.ncb):
            for kg in range(spec.nkg):
                src = spec.src[kg * spec.gk * 128:(kg + 1) * spec.gk * 128, cb * spec.w:(cb + 1) * spec.w].rearrange("(k p) n -> p k n", p=128)
                t, tk = self.cin.next()
                self.dma(t[:, 0:n].rearrange("p (k n) -> p k n", k=spec.gk), src, [], [tk])
                o, ok = self.cout.next()
                self.cv_i = getattr(self, "cv_i", 0) + 1
                eng = ("dve", "act", "pool")[self.cv_i % 3] if not spec.swap else ("dve", "act")[self.cv_i % 2]
                if not spec.swap:
                    self.copy(eng, o[:, 0:n], t[:, 0:n], [tk], [ok])
                else:
                    sv = t[:, 0:n].rearrange("p (a t d) -> p a t d", t=2, d=32)
                    dv = o[:, 0:n].rearrange("p (a t d) -> p a t d", t=2, d=32)
                    self.copy("dve", dv[:, :, 0, :], sv[:, :, 1, :], [tk], [ok])
                    self.copy("act", dv[:, :, 1, :], sv[:, :, 0, :], [tk], [ok])
                self.dma(scr[cb, kg], o[:, 0:n], [ok], [("ws", spec.name, cb, kg)], eng="pool")

    def setup_weights(self, layersB, layersA):
        self.W = {}
        for l in range(L):
            for f in range(2):
                s = WSpec("gu%d_%d" % (f, l), self.w_gu[f][l], D, 2 * FF, 256, 8)
                self.W[("gu", f, l)] = s
                s2 = WSpec("dn%d_%d" % (f, l), self.w_dn[f][l], FF, D, 128, 11)
                self.W[("dn", f, l)] = s2
            self.W[("in", l)] = WSpec("in_%d" % l, self.w_in[l], D, 6912, 256, 8)
            self.W[("insw", l)] = WSpec("insw_%d" % l, self.w_in[l][:, 1536:3328], D, 1792, 256, 8, swap=True)
            for i in range(3):
                self.W[("br", l, i)] = WSpec("br%d_%d" % (i, l), self.w_br[l, i], 512, D, 256, 4)
            self.W[("out", l)] = WSpec("out_%d" % l, self.w_o[l], D, D, 256, 8)
        order = []
        for l in sorted(set(layersA) | set(layersB)):
            if l in layersA:
                order += [("gu", 0, l), ("dn", 0, l)]
            order += [("in", l), ("insw", l)]
            if l in layersB:
                order += [("br", l, 0), ("br", l, 1), ("br", l, 2), ("out", l), ("gu", 1, l), ("dn", 1, l)]
        for k in order:
            self.conv_weight(self.W[k])

    def load_slot(self, spec, cb, kg):
        t, tk = self.wslot.next()
        n = spec.gk * spec.w
        self.dma(t[:, 0:n], spec.scr[cb, kg], [("ws", spec.name, cb, kg)], [tk])
        return t[:, 0:n].rearrange("p (k n) -> p k n", k=spec.gk), tk

    def hk(self, c):
        return ("hb", self.hcur, c)

    def load_h(self, blk, src_t):
        src = src_t.rearrange("(c p) t -> p c t", p=128)[:, :, blk * TB:(blk + 1) * TB]
        self.hcur = 1 - self.hcur
        self.hb = self.hbs[self.hcur]
        self.dma(self.hb[:], src, [("hbuf", blk)], [KS([self.hk(c) for c in range(8)])])
        self.dma(self.cosT[:], self.cosT_d[:, blk * TB:(blk + 1) * TB], [], ["cosT"])
        self.dma(self.sinT[:], self.sinT_d[:, blk * TB:(blk + 1) * TB], [], ["sinT"])

    def store_h(self, blk, dst_t):
        dst = dst_t.rearrange("(c p) t -> p c t", p=128)[:, :, blk * TB:(blk + 1) * TB]
        final = dst_t is not getattr(self, "hbuf", None)
        self.dma(dst, self.hb[:], [KS([self.hk(c) for c in range(8)])], [("out", blk) if final else ("hbuf", blk)], eng="pool")
        if final:
            self.outs.append(("out", blk))

    def finish(self):
        if self.launch != 0 and self.lA is not None:
            self.outs.extend(self.kvkeys[self.lA])
        self.P.op("sp", None, reads=self.outs)
        self.P.finalize(self.st)
        self.P.run_block()

    def stats_rstd(self, srcs, src_keys, inv_n, from_psum=False):
        psS, pk = self.bank("stat", [6, 7])
        n = len(srcs)
        for c, (s, sk) in enumerate(zip(srcs, src_keys)):
            sq, qk = self.sqb.next()
            self.act(sq[:], s, AF.Square, [sk], [qk])
            self.mm(psS[:], self.ones_b[:], sq[:], c == 0, c == n - 1, [qk, "ones_b"], [pk])
        ln, lk = self.rstd.next()
        self.act(ln[:], psS[:], AF.Ln, [pk, "eps_t"], [lk], scale=inv_n, bias=self.eps_t[:])
        self.act(ln[:], ln[:], AF.Exp, [lk], [lk], scale=-0.5)
        return ln, lk

    def norm_to_xnT(self, l, which):
        rstd, rk = self.stats_rstd([self.hb[:, c, :] for c in range(8)], [self.hk(c) for c in range(8)], 1.0 / D)
        g0 = (l * 9 + which) * 8
        for c in range(8):
            self.stt("dve", self.xnT[:, c, :], self.hb[:, c, :], self.gcols[:, g0 + c:g0 + c + 1], rstd[:], ALU.mult, ALU.mult,
                     [self.hk(c), "gcols", rk], [("xnT", c)])

    def post_norm_add(self, l, which):
        rstd, rk = self.stats_rstd([self.ysb[:, c, :] for c in range(8)], [("ysb", c) for c in range(8)], 1.0 / D)
        g0 = (l * 9 + which) * 8
        for c in range(8):
            t, tk = self.f32s.next()
            self.stt("dve", t[:], self.ysb[:, c, :], self.gcols[:, g0 + c:g0 + c + 1], rstd[:], ALU.mult, ALU.mult, [("ysb", c), "gcols", rk], [tk])
            self.tt("pool", self.hb[:, c, :], self.hb[:, c, :], t[:], ALU.add, [self.hk(c), tk], [self.hk(c)])

    def ffn(self, l, f):
        gu, dn = self.W[("gu", f, l)], self.W[("dn", f, l)]
        self.norm_to_xnT(l, 0 if f == 0 else 7)
        xk = [("xnT", c) for c in range(8)]
        for jj in range(11):
            gs, gk_ = self.load_slot(gu, jj, 0)
            us, uk_ = self.load_slot(gu, 11 + jj, 0)
            for sub in range(2):
                j = 2 * jj + sub
                pg, pgk = self.bank("ffn_g", [0, 1])
                pu, puk = self.bank("ffn_u", [2, 3])
                for kc in range(8):
                    self.mm(pg[:], gs[:, kc, sub * 128:(sub + 1) * 128], self.xnT[:, kc, :], kc == 0, kc == 7, [gk_, xk[kc]], [pgk])
                for kc in range(8):
                    self.mm(pu[:], us[:, kc, sub * 128:(sub + 1) * 128], self.xnT[:, kc, :], kc == 0, kc == 7, [uk_, xk[kc]], [puk])
                s, sk = self.f32s.next()
                self.act(s[:], pg[:], AF.Silu, [pgk], [sk])
                self.tt("dve", self.hT[:, j, :], s[:], pu[:], ALU.mult, [sk, puk], [("hT", j)])
        for c in range(8):
            py, pyk = self.bank("ffn_y", [4, 5])
            for kg in range(2):
                ws, wk = self.load_slot(dn, c, kg)
                for ki in range(11):
                    kc = kg * 11 + ki
                    self.mm(py[:], ws[:, ki, :], self.hT[:, kc, :], kc == 0, kc == 21, [wk, ("hT", kc)], [pyk])
            self.copy("act" if c % 2 == 0 else "dve", self.ysb[:, c, :], py[:], [pyk], [("ysb", c)])
        self.post_norm_add(l, 1 if f == 0 else 8)

    def proj_fm(self, l, cb_list, dst_fn, rope, xk):
        win, wsw = self.W[("in", l)], self.W[("insw", l)]
        i = 0
        for cb in cb_list:
            ws, wk = self.load_slot(win, cb, 0)
            if rope:
                ss, sk_ = self.load_slot(wsw, cb - 6, 0)
            for half in range(2):
                pk_, pkk = self.bank("pj_a", [0, 1])
                for kc in range(8):
                    self.mm(pk_[:], ws[:, kc, half * 128:(half + 1) * 128], self.xnT[:, kc, :], kc == 0, kc == 7, [wk, xk[kc]], [pkk])
                dst, dk, post = dst_fn(i)
                if dst is None:
                    i += 1
                    continue
                if rope:
                    ps2, ps2k = self.bank("pj_b", [2, 3])
                    for kc in range(8):
                        self.mm(ps2[:], ss[:, kc, half * 128:(half + 1) * 128], self.xnT[:, kc, :], kc == 0, kc == 7, [sk_, xk[kc]], [ps2k])
                    t1, t1k = self.f32s.next()
                    t2, t2k = self.f32s.next()
                    self.tt("dve", t1[:], pk_[:], self.cosT[:], ALU.mult, [pkk, "cosT"], [t1k])
                    self.tt("dve", t2[:], ps2[:], self.sinT[:], ALU.mult, [ps2k, "sinT"], [t2k])
                    self.tt("pool", dst, t1[:], t2[:], ALU.add, [t1k, t2k], [dk])
                else:
                    self.copy("act" if i % 2 == 0 else "dve", dst, pk_[:], [pkk], [dk])
                if post is not None:
                    post()
                i += 1

    def stageA(self, l, blk):
        if "ffn" not in self.cfg.get("skip", ()):
            self.ffn(l, 0)
        if "kv" in self.cfg.get("skip", ()):
            return
        self.norm_to_xnT(l, 2)
        xk = [("xnT", c) for c in range(8)]
        kv = self.kvloc[l]
        t0 = blk * TB
        if not hasattr(self, "kvkeys"):
            self.kvkeys = {i: [] for i in range(L)}

        def kdst_factory(row0, only0=False):
            def fn(i):
                if only0 and i > 0:
                    return None, None, None
                t, tk = self.kst.next()
                r0 = row0 + i * 128

                def post():
                    self.dma(kv[r0:r0 + 128, t0:t0 + TB], t[:], [tk], [self.key("kvw%d" % l)], eng="pool")
                    self.kvkeys[l].append("kvw%d#%d" % (l, self.uid))
                return t[:], tk, post
            return fn

        self.proj_fm(l, [2, 3], kdst_factory(KA0), False, xk)
        self.proj_fm(l, [8], kdst_factory(KB0, True), True, xk)
        self.proj_fm(l, [11, 12], kdst_factory(KC0), True, xk)
        win = self.W[("in", l)]
        for (cbs, vrow0, ncols) in (([4, 5], VA0, 512), ([8], VB0, 128), ([13, 14], VC0, 512)):
            vt, vk = self.vst.next()
            for ci, cb in enumerate(cbs):
                ws, wk = self.load_slot(win, cb, 0)
                for tt_ in range(4):
                    pv, pvk = self.bank("pj_v", [4, 5])
                    if ncols == 128:
                        for kc in range(8):
                            self.mm(pv[:, 0:128], self.xnT[:, kc, tt_ * 128:(tt_ + 1) * 128], ws[:, kc, 128:256], kc == 0, kc == 7, [wk, xk[kc]], [pvk])
                        self.copy("act" if tt_ % 2 == 0 else "dve", vt[:, tt_, 0:128], pv[:, 0:128], [pvk], [vk])
                    else:
                        for kc in range(8):
                            self.mm(pv[:, 0:256], self.xnT[:, kc, tt_ * 128:(tt_ + 1) * 128], ws[:, kc, :], kc == 0, kc == 7, [wk, xk[kc]], [pvk])
                        self.copy("act" if tt_ % 2 == 0 else "dve", vt[:, tt_, ci * 256:(ci + 1) * 256], pv[:, 0:256], [pvk], [vk])
            nrows = ncols * NT // NT
            view = kv[vrow0:vrow0 + ncols, :].rearrange("r (q c) -> (r q) c", c=ncols)
            dstv = view[t0:t0 + TB, :].rearrange("(t p) c -> p t c", p=128)
            self.dma(dstv, vt[:, :, 0:ncols], [vk], [self.key("kvw%d" % l)], eng="pool")
            self.kvkeys[l].append("kvw%d#%d" % (l, self.uid))

    def allgather(self, l):
        if NRANK == 1:
            return
        if self.cfg.get("no_ag"):
            self.dma(self.kvall[l][0:KV_ROWS, :], self.kvloc[l], list(self.kvkeys[l]), [("kvall", l)], eng="pool")
            return
        self.P.op("pool", lambda e: e.collective_compute("AllGather", ALU.bypass, replica_groups=[[0, 1], [2, 3], [4, 5], [6, 7]],
                                                         ins=[self.kvloc[l]], outs=[self.kvall[l]]),
                  reads=list(self.kvkeys[l]), writes=[("kvall", l)], dma=True)

    def setup_layer_tables(self, l):
        pass

    def kv_rows_tok(self, l, row0, nrows, tok_lo, tok_hi):
        out = []
        for r in range(2):
            lo, hi = max(tok_lo, r * NT), min(tok_hi, (r + 1) * NT)
            if lo < hi:
                out.append((self.kvall[l][r * KV_ROWS + row0:r * KV_ROWS + row0 + nrows, lo - r * NT:hi - r * NT], lo - tok_lo, hi - lo))
        return out

    def v_view(self, l, r, vrow0, ncols):
        return self.kvall[l][r * KV_ROWS + vrow0:r * KV_ROWS + vrow0 + ncols, :].rearrange("r (q c) -> (r q) c", c=ncols)

    def na_valid(self, half, R0, krl, j):
        if NRANK == 1 and ROT:
            tq = (R0 + j + 64 * half) % 128
            tk = ((krl % 128) + 64 * half) % 128
            if tk - tq != krl - (R0 + j):
                return False
            ws = min(max(tq - 4, 0), 120)
            return ws <= tk < ws + 8
        if NRANK == 1:
            half = 0
        qr = R0 + j + (NT // 64) * half
        kr = krl + (NT // 64) * half
        ws = min(max(qr - 4, 0), 120)
        return (0 <= kr <= 127) and (ws <= kr < ws + 8)

    def attn_na(self, l, blk):
        R0 = 8 * blk
        NR_ = NT // 64
        own_lo, own_hi = max(R0 - 4, 0), min(R0 + 12, NR_)
        kvl = self.kvloc[l]
        pieces = []
        if R0 - 4 < 0 and NRANK == 2:
            pieces.append(("prev", R0 - 4, 0))
        pieces.append(("own", own_lo, own_hi))
        if R0 + 12 > NR_ and NRANK == 2:
            pieces.append(("next", NR_, R0 + 12))
        if NRANK == 1 and ROT:
            if R0 - 4 < 0:
                pieces.append(("wrap", R0 - 4, 0))
            if R0 + 12 > NR_:
                pieces.append(("wrap", NR_, R0 + 12))
        for kind, lo, hi in pieces:
            s0 = lo - (R0 - 4)
            n = (hi - lo) * 64
            if kind == "wrap":
                wl, wh = lo % NR_, (hi - 1) % NR_ + 1
                ksrc = kvl[KA0:KA0 + 512, wl * 64:wh * 64]
                vsrc = kvl[VA0:VA0 + 512, :].rearrange("r (q c) -> (r q) c", c=512)[wl * 64:wh * 64, :]
                rk = list(self.kvkeys[l])
            elif kind == "own":
                ksrc = kvl[KA0:KA0 + 512, lo * 64:hi * 64]
                vsrc = kvl[VA0:VA0 + 512, :].rearrange("r (q c) -> (r q) c", c=512)[lo * 64:hi * 64, :]
                rk = list(self.kvkeys[l])
            elif kind == "prev":
                ksrc = self.kvall[l][KA0:KA0 + 512, NT + lo * 64:NT + hi * 64]
                vsrc = self.v_view(l, 0, VA0, 512)[NT + lo * 64:NT + hi * 64, :]
                rk = [("kvall", l)]
            else:
                ksrc = self.kvall[l][KV_ROWS + KA0:KV_ROWS + KA0 + 512, (lo - NR_) * 64:(hi - NR_) * 64]
                vsrc = self.v_view(l, 1, VA0, 512)[(lo - NR_) * 64:(hi - NR_) * 64, :]
                rk = [("kvall", l)]
            self.dma(self.kTA[:, :, s0 * 64:s0 * 64 + n], ksrc.rearrange("(c p) t -> p c t", p=128), rk, ["kTA"])
            for ph in range(2):
                self.dma(self.vA[ph * 64:(ph + 1) * 64, s0:s0 + (hi - lo), :], vsrc.rearrange("(r p) c -> p r c", p=64), rk, [self.vAk])
        segs = []
        for s_ in range(16):
            krl = R0 - 4 + s_
            cats = [(self.na_valid(0, R0, krl, j), self.na_valid(1, R0, krl, j)) for j in range(8)]
            j = 0
            while j < 8:
                if cats[j] == (False, False):
                    j += 1
                    continue
                j1 = j
                while j1 + 1 < 8 and cats[j1 + 1] == cats[j]:
                    j1 += 1
                segs.append((s_, krl, j, j1, cats[j]))
                j = j1 + 1
        for ch in range(4):
            eb, ebk = self.ebh.next()
            for ph in range(2):
                self.dma(eb[ph * 64:(ph + 1) * 64, :], self.ebt[l][2 * ch + ph], [("ebt", l, 2 * ch + ph)], [ebk])
            ebv = eb[:].rearrange("p (i c) -> p i c", c=64)
            pO = [self.ps[2], self.ps[3]]
            pOk = ["ps2", "ps3"]
            pR = [self.ps[4], self.ps[5]]
            pRk = ["ps4", "ps5"]

            def na_S(si, ch=ch):
                s_, krl, j0, j1, cat = segs[si]
                nq = (j1 - j0 + 1) * 64
                pS, pSk = self.bank("att_s", [0, 1, 6, 7])
                for ph in range(2):
                    b = ph * 64
                    self.mm(pS[b:b + 64, 0:nq], self.kTA[b:b + 64, ch, s_ * 64:(s_ + 1) * 64], self.qT[b:b + 64, ch, j0 * 64:(j1 + 1) * 64],
                            True, True, ["kTA", ("qT", ch)], [pSk])
                return pS, pSk

            pend, nemit = [], 0
            for si, (s_, krl, j0, j1, cat) in enumerate(segs):
                while nemit < len(segs) and nemit <= si + LOOKAHEAD:
                    pend.append(na_S(nemit))
                    nemit += 1
                pS, pSk = pend.pop(0)
                nq = (j1 - j0 + 1) * 64
                c0, c1 = j0 * 64, (j1 + 1) * 64
                e, ek = self.bfs.next()
                self.act(e[:, 0:nq], pS[:, 0:nq], AF.Exp, [pSk], [ek], scale=0.125)
                pm, pmk = self.pm.next()
                idx0 = 7 - krl + R0 + j0
                ev = e[:, 0:nq].rearrange("p (i c) -> p i c", c=64)
                pv = pm[:, 0:nq].rearrange("p (i c) -> p i c", c=64)
                tb = ebv[:, idx0:idx0 + (j1 - j0 + 1), :]
                if cat == (True, True):
                    self.tt("dve", pv, ev, tb, ALU.mult, [ek, ebk], [pmk])
                else:
                    fl = self.flags[:, 1:2] if cat == (True, False) else self.flags[:, 0:1]
                    self.stt("dve", pv, ev, fl, tb, ALU.mult, ALU.mult, [ek, ebk, "flags"], [pmk])
                last = si == len(segs) - 1
                for ph in range(2):
                    b = ph * 64
                    h = 2 * ch + ph
                    self.mm(pO[ph][0:64, c0:c1], self.vA[b:b + 64, s_, h * 64:(h + 1) * 64], pm[b:b + 64, 0:nq], si == 0, last, [self.vAk, pmk], [pOk[ph]], skip_group_check=True)
                    self.mm(pR[ph][0:64, c0:c1], self.ones_b[b:b + 64, 0:64], pm[b:b + 64, 0:nq], si == 0, last, ["ones_b", pmk], [pRk[ph]], skip_group_check=True)
            for ph in range(2):
                b = ph * 64
                r, rk_ = self.f32s.next()
                self.recip(r[0:64, :], pR[ph][0:64, :], [pRk[ph]], [rk_])
                self.tt("dve", self.oT[0][b:b + 64, ch, :], pO[ph][0:64, :], r[0:64, :], ALU.mult, [pOk[ph], rk_], [("oT0", ch, ph)])

    def attn_sw(self, l, blk):
        kvl = self.kvloc[l]
        T0 = blk * 4 - 1
        tiles = []
        vview = kvl[VB0:VB0 + 128, :].rearrange("r (q c) -> (r q) c", c=128)
        rk = list(self.kvkeys[l])
        rot_mode = (NRANK == 1 and ROT)
        if rot_mode:
            ntl = NT // 128
            tls = []
            for s_ in range(6):
                Tl = (T0 + s_) % ntl
                v = []
                for half in (0, 1):
                    tq0 = (4 * blk + 32 * half) % 64
                    tk = (Tl + 32 * half) % 64
                    v.append(tk - tq0 == s_ - 1)
                tiles.append({(True, True): "own", (False, True): "prev", (True, False): "next", (False, False): "none"}[tuple(v)])
                tls.append(Tl)
            s_ = 0
            while s_ < 6:
                e_ = s_
                while e_ + 1 < 6 and tls[e_ + 1] == tls[e_] + 1:
                    e_ += 1
                lo, hi = tls[s_], tls[e_] + 1
                for kvh in range(2):
                    for ph in range(2):
                        self.dma(self.kTB[ph * 64:(ph + 1) * 64, kvh, s_ * 128:(e_ + 1) * 128], kvl[KB0 + kvh * 64:KB0 + (kvh + 1) * 64, lo * 128:hi * 128], rk, ["kTB"])
                self.dma(self.vB[:, s_:e_ + 1, :], vview[lo * 128:hi * 128, :].rearrange("(t p) c -> p t c", p=128), rk, ["vB"])
                s_ = e_ + 1
        else:
            for s_ in range(6):
                T = T0 + s_
                kind = "prev" if T < 0 else ("next" if T >= NT // 128 else "own")
                if NRANK == 1 and kind != "own":
                    kind = "none"
                tiles.append(kind)
            own_s = [s_ for s_ in range(6) if tiles[s_] == "own"]
            lo, hi = T0 + own_s[0], T0 + own_s[-1] + 1
            for kvh in range(2):
                for ph in range(2):
                    self.dma(self.kTB[ph * 64:(ph + 1) * 64, kvh, own_s[0] * 128:(own_s[-1] + 1) * 128], kvl[KB0 + kvh * 64:KB0 + (kvh + 1) * 64, lo * 128:hi * 128], rk, ["kTB"])
            self.dma(self.vB[:, own_s[0]:own_s[-1] + 1, :], vview[lo * 128:hi * 128, :].rearrange("(t p) c -> p t c", p=128), rk, ["vB"])
        for s_ in range(6):
            if rot_mode or tiles[s_] in ("own", "none"):
                continue
            r = 0 if tiles[s_] == "prev" else 1
            t_lo = NT - 128 if r == 0 else 0
            for kvh in range(2):
                for ph in range(2):
                    self.dma(self.kTB[ph * 64:(ph + 1) * 64, kvh, s_ * 128:(s_ + 1) * 128],
                             self.kvall[l][r * KV_ROWS + KB0 + kvh * 64:r * KV_ROWS + KB0 + (kvh + 1) * 64, t_lo:t_lo + 128], [("kvall", l)], ["kTB"])
            self.dma(self.vB[:, s_, :], self.v_view(l, r, VB0, 128)[t_lo:t_lo + 128, :], [("kvall", l)], ["vB"])
        for h in range(8):
            ch, b, kvh = h // 2, (h % 2) * 64, h // 4
            pO, pOk = self.bank("att_o", [2, 3])
            pR, pRk = self.bank("att_r", [4, 5])
            proc = [s_ for s_ in range(6) if tiles[s_] != "none"]

            def crange(s_):
                return max(0, s_ - 2) * 128, (min(3, s_) + 1) * 128

            def sw_S(pi, ch=ch, b=b, kvh=kvh):
                s_ = proc[pi]
                c_lo, c_hi = crange(s_)
                pS, pSk = self.bank("att_s", [0, 1, 6, 7])
                self.mm(pS[:, c_lo:c_hi], self.kTB[b:b + 64, kvh, s_ * 128:(s_ + 1) * 128], self.qT[b:b + 64, ch, c_lo:c_hi], True, True, ["kTB", ("qT", ch)], [pSk])
                return pS, pSk

            pend, nemit = [], 0
            for pi, s_ in enumerate(proc):
                while nemit < len(proc) and nemit <= pi + LOOKAHEAD:
                    pend.append(sw_S(nemit))
                    nemit += 1
                pS, pSk = pend.pop(0)
                c_lo, c_hi = crange(s_)
                e, ek = self.bfs.next()
                self.act(e[:, c_lo:c_hi], pS[:, c_lo:c_hi], AF.Exp, [pSk], [ek], scale=0.125)
                pm, pmk = self.pm.next()
                off = 512 - (s_ - 1) * 128
                if tiles[s_] == "own":
                    self.tt("dve", pm[:, c_lo:c_hi], e[:, c_lo:c_hi], self.strip[:, off + c_lo:off + c_hi], ALU.mult, [ek, "strip"], [pmk])
                else:
                    fl = self.flags[:, 0:1] if tiles[s_] == "prev" else self.flags[:, 1:2]
                    self.stt("dve", pm[:, c_lo:c_hi], e[:, c_lo:c_hi], fl, self.strip[:, off + c_lo:off + c_hi], ALU.mult, ALU.mult, [ek, "strip", "flags"], [pmk])
                self.mm(pO[0:64, c_lo:c_hi], self.vB[:, s_, kvh * 64:(kvh + 1) * 64], pm[:, c_lo:c_hi], s_ == proc[0], s_ == proc[-1], ["vB", pmk], [pOk], skip_group_check=True)
                self.mm(pR[0:64, c_lo:c_hi], self.ones_b[:, 0:64], pm[:, c_lo:c_hi], s_ == proc[0], s_ == proc[-1], ["ones_b", pmk], [pRk], skip_group_check=True)
            r, rk_ = self.f32s.next()
            self.ts("dve", r[0:64, :], pR[0:64, :], self.esink[0:64, l * 8 + h:l * 8 + h + 1], None, ALU.add, None, [pRk, "esink"], [rk_])
            self.recip(r[0:64, :], r[0:64, :], [rk_], [rk_])
            self.tt("dve", self.oT[1][b:b + 64, ch, :], pO[0:64, :], r[0:64, :], ALU.mult, [pOk, rk_], [("oT1", ch, h % 2)])

    def attn_diff(self, l, blk):
        npr = NT // 1024
        nchunks = S // 1024
        for h in range(4):
            pO = [self.ps[4], self.ps[5]]
            pOk = ["ps4", "ps5"]
            acc = self.dacc
            self.P.op("pool", lambda e: e.memset(self.dacc[:, 0:TB], 0.0), [], ["dacc"])
            pR1, pR1k = self.bank("stat", [6, 7])
            chunks = {}

            def load_chunk(kc8, h=h):
                r, tl = kc8 // npr, (kc8 % npr) * 1024
                kt_, ktk = self.kTC.next()
                vt_, vtk = self.vC.next()
                if NRANK == 1:
                    kvl_ = self.kvloc[l]
                    rk_ = list(self.kvkeys[l])
                    self.dma(kt_[:], kvl_[KC0 + h * 128:KC0 + (h + 1) * 128, tl:tl + 1024], rk_, [ktk])
                    vv_ = kvl_[VC0:VC0 + 512, :].rearrange("r (q c) -> (r q) c", c=512)
                    self.dma(vt_[:], vv_[tl:tl + 1024, h * 128:(h + 1) * 128].rearrange("(t p) c -> p t c", p=128), rk_, [vtk])
                else:
                    self.dma(kt_[:], self.kvall[l][r * KV_ROWS + KC0 + h * 128:r * KV_ROWS + KC0 + (h + 1) * 128, tl:tl + 1024], [("kvall", l)], [ktk])
                    self.dma(vt_[:], self.v_view(l, r, VC0, 512)[tl:tl + 1024, h * 128:(h + 1) * 128].rearrange("(t p) c -> p t c", p=128), [("kvall", l)], [vtk])
                chunks[kc8] = (kt_, ktk, vt_, vtk)

            items = [(kc8, kt) for kc8 in range(nchunks) for kt in range(8)]

            def emit_S(idx, h=h):
                kc8, kt = items[idx]
                if kt == 0:
                    load_chunk(kc8)
                kt_, ktk, vt_, vtk = chunks[kc8]
                pS2, pS2k = self.ps2[self.ps2_i % 2]
                self.ps2_i += 1
                for t in range(2):
                    self.mm(pS2[:, t * 512:(t + 1) * 512], kt_[t * 64:(t + 1) * 64, kt * 128:(kt + 1) * 128], self.qT[t * 64:(t + 1) * 64, h, :],
                            True, True, [ktk, ("qT", h)], [pS2k])
                return pS2, pS2k

            cur = emit_S(0)
            n_it = len(items)
            for idx in range(n_it):
                nxt = emit_S(idx + 1) if idx + 1 < n_it else None
                kc8, kt = items[idx]
                kt_, ktk, vt_, vtk = chunks[kc8]
                pS2, pS2k = cur
                e, ek = self.e2.next()
                self.act(e[:], pS2, AF.Exp, [pS2k], [ek], scale=0.125)
                self.tt("dve", acc[:, 0:TB], acc[:, 0:TB], e[:, 0:TB], ALU.add, ["dacc", ek], ["dacc"])
                for t in range(2):
                    self.mm(pO[t], vt_[:, kt, :], e[:, t * 512:(t + 1) * 512], idx == 0, idx == n_it - 1, [vtk, ek], [pOk[t]])
                self.mm(pR1, self.ones_b[:], e[:, TB:2 * TB], idx == 0, idx == n_it - 1, ["ones_b", ek], [pR1k])
                cur = nxt
            rs = []
            for t in range(2):
                if t == 0:
                    pR, pRk = self.bank("stat", [6, 7])
                    self.mm(pR, self.ones_f[:], acc[:, 0:TB], True, True, ["ones_f", "dacc"], [pRk])
                else:
                    pR, pRk = pR1, pR1k
                r_, rk_ = self.dfb.next()
                self.recip(r_[:], pR, [pRk], [rk_])
                rs.append((r_, rk_))
            o0, o0k = self.dfb.next()
            self.tt("dve", o0[:], pO[0], rs[0][0][:], ALU.mult, [pOk[0], rs[0][1]], [o0k])
            o1, o1k = self.dfb.next()
            self.tt("dve", o1[:], pO[1], rs[1][0][:], ALU.mult, [pOk[1], rs[1][1]], [o1k])
            self.stt("dve", o0[:], o1[:], self.nlam[:, l:l + 1], o0[:], ALU.mult, ALU.add, [o1k, o0k, "nlam"], [o0k])
            rstd, rk2 = self.stats_rstd([o0[:]], [o0k], 1.0 / 128)
            self.stt("dve", self.oT[2][:, h, :], o0[:], self.gsub[:, l:l + 1], rstd[:], ALU.mult, ALU.mult, [o0k, "gsub", rk2], [("oT2", h, 0), ("oT2", h, 1)])

    def merge_out(self, l):
        win, wo = self.W[("in", l)], self.W[("out", l)]
        xk = [("xnT", c) for c in range(8)]
        for cp in range(4):
            for i in range(3):
                gs, gk_ = self.load_slot(win, 15 + i * 4 + cp, 0)
                bs, bk_ = self.load_slot(self.W[("br", l, i)], cp, 0)
                for sub in range(2):
                    c = 2 * cp + sub
                    pY, pYk = self.bank("mg_y", [0, 1])
                    pG, pGk = self.bank("mg_g", [2, 3])
                    for kc in range(4):
                        self.mm(pY[:], bs[:, kc, sub * 128:(sub + 1) * 128], self.oT[i][:, kc, :], kc == 0, kc == 3, [bk_, ("oT%d" % i, kc, 0), ("oT%d" % i, kc, 1)], [pYk])
                    for kc in range(8):
                        self.mm(pG[:], gs[:, kc, sub * 128:(sub + 1) * 128], self.xnT[:, kc, :], kc == 0, kc == 7, [gk_, xk[kc]], [pGk])
                    g, gk2 = self.f32s.next()
                    col = (l * 3 + i) * 8 + c
                    self.act(g[:], pG[:], AF.Exp, [pGk, "nbg"], [gk2], scale=-1.0, bias=self.nbg[:, col:col + 1])
                    self.act(g[:], g[:], AF.Ln, [gk2, "one_t"], [gk2], scale=1.0, bias=self.one_t[:])
                    self.act(g[:], g[:], AF.Exp, [gk2], [gk2], scale=-1.0)
                    if i == 0:
                        self.tt("dve", self.macc[:, sub, :], pY[:], g[:], ALU.mult, [pYk, gk2], [("macc", sub)])
                    elif i == 1:
                        self.tt("dve", g[:], pY[:], g[:], ALU.mult, [pYk, gk2], [gk2])
                        self.tt("dve", self.macc[:, sub, :], self.macc[:, sub, :], g[:], ALU.add, [("macc", sub), gk2], [("macc", sub)])
                    else:
                        self.tt("dve", g[:], pY[:], g[:], ALU.mult, [pYk, gk2], [gk2])
                        self.tt("dve", self.mT[:, c, :], self.macc[:, sub, :], g[:], ALU.add, [("macc", sub), gk2], [("mT", c)])
        for cb in range(4):
            ws, wk = self.load_slot(wo, cb, 0)
            for sub in range(2):
                c = 2 * cb + sub
                pM, pMk = self.bank("ffn_y", [4, 5])
                for kc in range(8):
                    self.mm(pM[:], ws[:, kc, sub * 128:(sub + 1) * 128], self.mT[:, kc, :], kc == 0, kc == 7, [wk, ("mT", kc)], [pMk])
                self.copy("act" if c % 2 == 0 else "dve", self.ysb[:, c, :], pM[:], [pMk], [("ysb", c)])
        self.post_norm_add(l, 6)

    def stageB(self, l, blk):
        cfg = self.cfg
        self.norm_to_xnT(l, 2)
        xk = [("xnT", c) for c in range(8)]
        parts = cfg.get("parts", "ABC")
        qdst = lambda i: (self.qT[:, i, :], ("qT", i), None)
        skip = cfg.get("skip", ())
        self.proj_fm(l, [0, 1], qdst, False, xk)
        if "na" not in skip:
            self.attn_na(l, blk)
        self.proj_fm(l, [6, 7], qdst, True, xk)
        if "sw" not in skip:
            self.attn_sw(l, blk)
        self.proj_fm(l, [9, 10], qdst, True, xk)
        if "diff" not in skip:
            self.attn_diff(l, blk)
        if cfg.get("dbg_o"):
            if not hasattr(self, "dbg_o"):
                self.dbg_o = [self.dram("dbg_o%d" % i, [512, NT], BF16, "ExternalOutput") for i in range(3)]
            for i in range(3):
                k_ = self.key("dbgo")
                self.dma(self.dbg_o[i].rearrange("(c p) t -> p c t", p=128)[:, :, blk * TB:(blk + 1) * TB], self.oT[i][:],
                         [("oT%d" % i, c, hh) for c in range(4) for hh in range(2)], [k_], eng="pool")
                self.outs.append(k_)
        if "merge" not in skip:
            self.merge_out(l)
        if "ffn" not in skip:
            self.ffn(l, 1)


def build_program(cfg):
    kb = KB(cfg)
    with kb.st:
        nc = kb.build()
    return nc


def _consts():
    pos = np.arange(S, dtype=np.float32)
    inv = (np.float32(10000.0) ** (-(np.arange(0, 64, 2, dtype=np.float32) / np.float32(64)))).astype(np.float32)
    ang = (pos[:, None] * inv[None, :]).astype(np.float32)
    ang = np.concatenate([ang, ang], axis=-1)
    cos = np.cos(ang).astype(np.float32)
    sin = np.sin(ang).astype(np.float32)
    sgn = np.concatenate([-np.ones(32, np.float32), np.ones(32, np.float32)])
    cosT = np.ascontiguousarray(np.concatenate([cos.T, cos.T], axis=0))
    sinT = np.ascontiguousarray(np.concatenate([(sin * sgn).T, (sin * sgn).T], axis=0))
    c = np.arange(64)
    col_start = np.clip(c - 8, 0, 48)
    col_ok = (c[None, :] >= col_start[:, None]) & (c[None, :] < col_start[:, None] + 16)
    cmask = np.ascontiguousarray(col_ok.T.astype(np.float32))
    dc = np.clip(c[:, None] - c[None, :] + 15, 0, 30)
    kl = np.arange(128)[:, None]
    xx = np.arange(1152)[None, :]
    strip = (np.abs(xx - 512 - kl) <= 128).astype(np.float32)
    sel = np.zeros((128, 4), np.float32)
    sel[:, 0] = 1.0
    sel[:, 3] = 1.0
    selb = np.zeros((2, 256), np.float32)
    selb[0, 0:128] = 1.0
    selb[1, 128:256] = 1.0
    return cosT, sinT, cmask, dc, strip, sel, selb


def _host_inputs(inputs):
    cosT, sinT, cmask, dc, strip, sel, selb = _consts()
    f = lambda a: np.ascontiguousarray(np.asarray(a, dtype=np.float32))
    gnames = {0: "ffn1_pre_g", 1: "ffn1_post_g", 2: "mix_pre_g", 6: "mix_post_g", 7: "ffn2_pre_g", 8: "ffn2_post_g"}
    gcols = np.zeros((128, 9 * L * 8), np.float32)
    for l in range(L):
        for which, nm in gnames.items():
            g = f(inputs[nm])[l]
            gcols[:, (l * 9 + which) * 8:(l * 9 + which) * 8 + 8] = g.reshape(8, 128).T
    bg = f(inputs["b_gate"])
    bgate = np.zeros((128, L * 3 * 8), np.float32)
    for l in range(L):
        for i in range(3):
            bgate[:, (l * 3 + i) * 8:(l * 3 + i) * 8 + 8] = bg[l, i].reshape(8, 128).T
    gsub = np.ascontiguousarray(f(inputs["diff_subln_g"]).T)
    sink = f(inputs["sw_sink"]).reshape(-1)
    lamv = np.concatenate([f(inputs[k]).reshape(-1) for k in ("diff_lambda_q1", "diff_lambda_k1", "diff_lambda_q2", "diff_lambda_k2")])
    rpb = f(inputs["na_rpb"])
    idx = np.arange(15)
    rpbT = rpb[:, :, (14 - idx)[None, :, None], dc[:, None, :]]
    rpbT = np.ascontiguousarray(rpbT.reshape(L, 8, 64, 15 * 64))
    shared = {
        "ffn1_w_gu": f(inputs["ffn1_w_gu"]), "ffn2_w_gu": f(inputs["ffn2_w_gu"]),
        "ffn1_w_down": f(inputs["ffn1_w_down"]), "ffn2_w_down": f(inputs["ffn2_w_down"]),
        "w_in": f(inputs["w_in"]), "w_branch": f(inputs["w_branch"]), "w_out": f(inputs["w_out"]),
        "gcols": gcols, "bgate": bgate, "gsub": gsub, "sink": sink, "lamv": lamv, "rpbT": rpbT,
        "cmask": cmask, "strip": strip, "sel": sel, "selb": selb,
    }
    x = f(inputs["x"])
    in_maps = []
    xts = {}
    for c in range(8):
        b, half = c // 2, (c % 2 if NRANK == 2 else 0)
        m = dict(shared)
        if NRANK == 1 and ROT:
            half = c % 2
            perm = (np.arange(S) + half * (S // 2)) % S
            if (b, half) not in xts:
                xts[(b, half)] = (np.ascontiguousarray(x[b][perm].T), np.ascontiguousarray(cosT[:, perm]), np.ascontiguousarray(sinT[:, perm]))
            m["xT"], m["cosT"], m["sinT"] = xts[(b, half)]
            fl = np.zeros((128, 2), np.float32)
            fl[:, 0] = half
            fl[:, 1] = 1 - half
            m["flags"] = fl
            in_maps.append(m)
            continue
        if (b, half) not in xts:
            xts[(b, half)] = np.ascontiguousarray(x[b, half * NT:(half + 1) * NT, :].T)
        m["xT"] = xts[(b, half)]
        m["cosT"] = np.ascontiguousarray(cosT[:, half * NT:(half + 1) * NT])
        m["sinT"] = np.ascontiguousarray(sinT[:, half * NT:(half + 1) * NT])
        fl = np.zeros((128, 2), np.float32)
        if NRANK == 2:
            fl[:, 0] = half
            fl[:, 1] = 1 - half
        m["flags"] = fl
        in_maps.append(m)
    return in_maps


_NC_CACHE = {}
FUSED = True


def _get_nc(cfg_key, cfg):
    if cfg_key not in _NC_CACHE:
        _NC_CACHE[cfg_key] = build_program(cfg)
    return _NC_CACHE[cfg_key]


def _run(nc, in_maps, names):
    maps = [{k: m[k] for k in names if k in m} for m in in_maps]
    res = run_bass_kernel_spmd(nc, maps, core_ids=list(range(len(maps))))
    return res.results


WNAMES = ["ffn1_w_gu", "ffn2_w_gu", "ffn1_w_down", "ffn2_w_down", "w_in", "w_branch", "w_out", "gcols", "bgate", "gsub", "sink",
          "lamv", "rpbT", "cmask", "strip", "sel", "selb", "cosT", "sinT", "flags"]


def _pair_gather(results, name):
    out = []
    for c in range(len(results)):
        p = (c // 2) * 2
        out.append(np.concatenate([np.asarray(results[p][name]), np.asarray(results[p + 1][name])], axis=0))
    return out


def kernel_split(in_maps, cfg_extra=None):
    ce = cfg_extra or {}
    n = len(in_maps)
    r1 = _run(_get_nc("l1", dict(ce, launch=1)), in_maps, WNAMES + ["xT"])
    kvall = _pair_gather(r1, "kvloc_out")
    for c in range(n):
        in_maps[c]["h_in"] = np.asarray(r1[c]["h_out"])
        in_maps[c]["kvall_in"] = kvall[c]
        in_maps[c]["kvloc_in"] = np.asarray(r1[c]["kvloc_out"])
    del r1
    r2 = _run(_get_nc("l2", dict(ce, launch=2)), in_maps, WNAMES + ["h_in", "kvall_in", "kvloc_in"])
    kvall = _pair_gather(r2, "kvloc_out")
    for c in range(n):
        in_maps[c]["h_in"] = np.asarray(r2[c]["h_out"])
        in_maps[c]["kvall_in"] = kvall[c]
        in_maps[c]["kvloc_in"] = np.asarray(r2[c]["kvloc_out"])
    del r2
    r3 = _run(_get_nc("l3", dict(ce, launch=3)), in_maps, WNAMES + ["h_in", "kvall_in", "kvloc_in"])
    return [np.asarray(r3[c]["outT"]) for c in range(n)]


def kernel(**inputs):
    if FUSED:
        set_mode(1, rot=True)
        in_maps = _host_inputs(inputs)
        res = _run(_get_nc("fused", {}), in_maps, WNAMES + ["xT"])
        out = np.empty((4, S, D), np.float32)
        for c in range(8):
            b, half = c // 2, c % 2
            out[b, half * (S // 2):(half + 1) * (S // 2), :] = np.asarray(res[c]["outT"]).T
        return out
    set_mode(2)
    in_maps = _host_inputs(inputs)
    outs = kernel_split(in_maps)
    out = np.empty((4, S, D), np.float32)
    for c in range(8):
        b, half = c // 2, c % 2
        out[b, half * NT:(half + 1) * NT, :] = outs[c].T
    return out
```
